# Optimizing a Trainium2 kernel written in Bass

```python
import jax, jax.numpy as jnp
from jax import lax
import numpy as np

D_MODEL = 2048
BATCH = 2
SEQ = 4096
DEPTH = 2

N_MEM = 256
RWKV_HEADS = 16
RWKV_HEAD_DIM = 64
RWKV_WIDTH = RWKV_HEADS * RWKV_HEAD_DIM
DECAY_RANK = 64
ICLR_RANK = 64
GATE_RANK = 128
DECAY_SCALE = 0.606531
GN_EPS = 64e-5
CONV_WIDTH = 1024
CONV_K = 3
XATTN_HEADS = 4
XATTN_HEAD_DIM = 256
XATTN_WIDTH = XATTN_HEADS * XATTN_HEAD_DIM
N_BRANCH = 3
BRANCH_WIDTH = 1024
D_FF = 5504
RWKV_PROJ = 3 * RWKV_WIDTH + 2 * DECAY_RANK + 2 * ICLR_RANK + GATE_RANK
CONV_PROJ = 3 * CONV_WIDTH
GATE_PROJ = N_BRANCH * D_MODEL
D_IN = RWKV_PROJ + CONV_PROJ + XATTN_WIDTH + GATE_PROJ
ALPHA = (2 * DEPTH) ** 0.25
BETA = (8 * DEPTH) ** -0.25
LN_EPS = 1e-5

kernel_name = "hybrid_rwkv7_shortconv_memxattn_macaron_deepnorm"


def layer_norm(x, g, b):
    xf = x.astype(jnp.float32)
    mu = jnp.mean(xf, axis=-1, keepdims=True)
    var = jnp.mean(jnp.square(xf - mu), axis=-1, keepdims=True)
    y = (xf - mu) * lax.rsqrt(var + LN_EPS)
    return (y * g.astype(jnp.float32) + b.astype(jnp.float32)).astype(x.dtype)


def swiglu(x, w_gate, w_up, w_down):
    return (jax.nn.silu(x @ w_gate) * (x @ w_up)) @ w_down


def _wkv7_scan(r, w, k, v, kk, a, reverse):
    b, t, h, n = r.shape

    def step(S, inp):
        r_t, w_t, k_t, v_t, kk_t, a_t = inp
        s_kk = jnp.einsum('bhvk,bhk->bhv', S, kk_t)
        S = (S * w_t[:, :, None, :]
             - s_kk[..., None] * (kk_t * a_t)[:, :, None, :]
             + v_t[..., None] * k_t[:, :, None, :])
        y_t = jnp.einsum('bhvk,bhk->bhv', S, r_t)
        return S, y_t

    xs = tuple(jnp.moveaxis(u, 1, 0) for u in (r, w, k, v, kk, a))
    s0 = jnp.zeros((b, h, n, n), jnp.float32)
    _, ys = lax.scan(step, s0, xs, reverse=reverse)
    return jnp.moveaxis(ys, 0, 1)


def rwkv7_branch(p, mu, w0, w_up, a0, a_up, g_up, k_k, k_a, r_k, gn_g, gn_b):
    b, t, _ = p.shape
    h, n = RWKV_HEADS, RWKV_HEAD_DIM
    prev = jnp.pad(p[:, :-1], ((0, 0), (1, 0), (0, 0)))
    nxt = jnp.pad(p[:, 1:], ((0, 0), (0, 1), (0, 0)))
    p = p + mu * (0.5 * (prev + nxt) - p)
    cuts = [RWKV_WIDTH, 2 * RWKV_WIDTH, 3 * RWKV_WIDTH,
            3 * RWKV_WIDTH + 2 * DECAY_RANK, 3 * RWKV_WIDTH + 2 * DECAY_RANK + 2 * ICLR_RANK]
    r, k, v, hw, ha, hg = jnp.split(p, cuts, axis=-1)
    hw = hw.reshape(b, t, 2, DECAY_RANK)
    ha = ha.reshape(b, t, 2, ICLR_RANK)
    w_logit = w0 + jnp.einsum('btdr,drc->btdc', jnp.tanh(hw), w_up)
    decay = jnp.exp(-DECAY_SCALE * jax.nn.sigmoid(w_logit.astype(jnp.float32)))
    a = jax.nn.sigmoid((a0 + jnp.einsum('btdr,drc->btdc', ha, a_up)).astype(jnp.float32))
    g = jax.nn.sigmoid(hg) @ g_up
    rf = r.astype(jnp.float32)
    kf = k.astype(jnp.float32)
    vf = v.astype(jnp.float32)
    r_h = rf.reshape(b, t, h, n)
    v_h = vf.reshape(b, t, h, n)
    kk = (kf * k_k.astype(jnp.float32)).reshape(b, t, h, n)
    kk = kk * lax.rsqrt(jnp.sum(kk * kk, axis=-1, keepdims=True) + 1e-12)
    k_dir = kf[:, :, None, :] * (1.0 + (a - 1.0) * k_a.astype(jnp.float32))
    k_dir_h = k_dir.reshape(b, t, 2, h, n)
    a_h = a.reshape(b, t, 2, h, n)
    decay_h = decay.reshape(b, t, 2, h, n)
    y_fwd = _wkv7_scan(r_h, decay_h[:, :, 0], k_dir_h[:, :, 0], v_h, kk, a_h[:, :, 0], reverse=False)
    y_bwd = _wkv7_scan(r_h, decay_h[:, :, 1], k_dir_h[:, :, 1], v_h, kk, a_h[:, :, 1], reverse=True)
    y = y_fwd + y_bwd
    m = jnp.mean(y, axis=-1, keepdims=True)
    var = jnp.mean(jnp.square(y - m), axis=-1, keepdims=True)
    y = ((y - m) * lax.rsqrt(var + GN_EPS)).reshape(b, t, RWKV_WIDTH)
    y = y * gn_g.astype(jnp.float32) + gn_b.astype(jnp.float32)
    k_bonus = jnp.mean(k_dir_h, axis=2)
    bonus = jnp.sum(r_h * k_bonus * r_k.astype(jnp.float32), axis=-1, keepdims=True) * v_h
    y = y + bonus.reshape(b, t, RWKV_WIDTH)
    return (y * g.astype(jnp.float32)).astype(p.dtype)


def shortconv_branch(p, conv_w):
    gate_b, gate_c, h = jnp.split(p, 3, axis=-1)
    hc = lax.conv_general_dilated(
        gate_c * h, conv_w[:, None, :].astype(p.dtype),
        window_strides=(1,), padding='SAME',
        dimension_numbers=('NWC', 'WIO', 'NWC'),
        feature_group_count=CONV_WIDTH)
    return gate_b * hc


def memory_xattn_branch(q, mem, ln_g, ln_b, w_kv):
    b, t, _ = q.shape
    mem_n = layer_norm(mem, ln_g, ln_b)
    k, v = jnp.split(mem_n @ w_kv, 2, axis=-1)
    qh = q.reshape(b, t, XATTN_HEADS, XATTN_HEAD_DIM)
    kh = k.reshape(b, -1, XATTN_HEADS, XATTN_HEAD_DIM)
    vh = v.reshape(b, -1, XATTN_HEADS, XATTN_HEAD_DIM)
    s = jnp.einsum('bthd,bmhd->bhtm', qh, kh).astype(jnp.float32) * (XATTN_HEAD_DIM ** -0.5)
    attn = jax.nn.softmax(s, axis=-1).astype(vh.dtype)
    o = jnp.einsum('bhtm,bmhd->bthd', attn, vh)
    return o.reshape(b, t, XATTN_WIDTH)


def setup_inputs(seed: int = 0) -> dict:
    key = jax.random.key(seed)
    ks = iter(jax.random.split(key, 40))
    L, D, F = DEPTH, D_MODEL, D_FF

    def nrm(shape, scale):
        return jax.random.normal(next(ks), shape, jnp.float32) * scale

    def gain(shape):
        return 1.0 + nrm(shape, 0.02)

    return {
        "x": nrm((BATCH, SEQ, D), 1.0),
        "mem": nrm((BATCH, N_MEM, D), 1.0),
        "ffn1_w_gate": nrm((L, D, F), D ** -0.5),
        "ffn1_w_up": nrm((L, D, F), D ** -0.5),
        "ffn1_w_down": nrm((L, F, D), BETA * F ** -0.5),
        "ln1_g": gain((L, D)),
        "ln1_b": nrm((L, D), 0.02),
        "w_in": nrm((L, D, D_IN), D ** -0.5),
        "rwkv_mu": jax.random.uniform(next(ks), (L, RWKV_PROJ), jnp.float32),
        "rwkv_w0": nrm((L, 2, RWKV_WIDTH), 1.5),
        "rwkv_w_up": nrm((L, 2, DECAY_RANK, RWKV_WIDTH), 0.5 * DECAY_RANK ** -0.5),
        "rwkv_a0": nrm((L, 2, RWKV_WIDTH), 0.5),
        "rwkv_a_up": nrm((L, 2, ICLR_RANK, RWKV_WIDTH), 0.5 * ICLR_RANK ** -0.5),
        "rwkv_g_up": nrm((L, GATE_RANK, RWKV_WIDTH), GATE_RANK ** -0.5),
        "rwkv_k_k": 0.85 + nrm((L, RWKV_WIDTH), 0.02),
        "rwkv_k_a": gain((L, RWKV_WIDTH)),
        "rwkv_r_k": nrm((L, RWKV_HEADS, RWKV_HEAD_DIM), 0.1),
        "rwkv_gn_g": gain((L, RWKV_WIDTH)),
        "rwkv_gn_b": nrm((L, RWKV_WIDTH), 0.02),
        "conv_w": nrm((L, CONV_K, CONV_WIDTH), CONV_K ** -0.5),
        "mem_ln_g": gain((L, D)),
        "mem_ln_b": nrm((L, D), 0.02),
        "w_mem_kv": nrm((L, D, 2 * XATTN_WIDTH), D ** -0.5),
        "w_branch": nrm((L, N_BRANCH, BRANCH_WIDTH, D), BETA * BRANCH_WIDTH ** -0.5),
        "gate_b": nrm((L, N_BRANCH, D), 0.1),
        "w_out": nrm((L, D, D), BETA * D ** -0.5),
        "ln2_g": gain((L, D)),
        "ln2_b": nrm((L, D), 0.02),
        "ffn2_w_gate": nrm((L, D, F), D ** -0.5),
        "ffn2_w_up": nrm((L, D, F), D ** -0.5),
        "ffn2_w_down": nrm((L, F, D), BETA * F ** -0.5),
        "ln3_g": gain((L, D)),
        "ln3_b": nrm((L, D), 0.02),
    }


def reference(x, mem, ffn1_w_gate, ffn1_w_up, ffn1_w_down, ln1_g, ln1_b, w_in,
              rwkv_mu, rwkv_w0, rwkv_w_up, rwkv_a0, rwkv_a_up, rwkv_g_up, rwkv_k_k,
              rwkv_k_a, rwkv_r_k, rwkv_gn_g, rwkv_gn_b, conv_w, mem_ln_g, mem_ln_b,
              w_mem_kv, w_branch, gate_b, w_out, ln2_g, ln2_b, ffn2_w_gate, ffn2_w_up,
              ffn2_w_down, ln3_g, ln3_b):
    b, t, d = x.shape
    cuts = [RWKV_PROJ, RWKV_PROJ + CONV_PROJ, RWKV_PROJ + CONV_PROJ + XATTN_WIDTH]
    for l in range(DEPTH):
        x = layer_norm(ALPHA * x + 0.5 * swiglu(x, ffn1_w_gate[l], ffn1_w_up[l], ffn1_w_down[l]),
                       ln1_g[l], ln1_b[l])
        p = x @ w_in[l]
        p_rwkv, p_conv, q_mem, gate_logits = jnp.split(p, cuts, axis=-1)
        y_rwkv = rwkv7_branch(p_rwkv, rwkv_mu[l], rwkv_w0[l], rwkv_w_up[l], rwkv_a0[l],
                              rwkv_a_up[l], rwkv_g_up[l], rwkv_k_k[l], rwkv_k_a[l],
                              rwkv_r_k[l], rwkv_gn_g[l], rwkv_gn_b[l])
        y_conv = shortconv_branch(p_conv, conv_w[l])
        y_mem = memory_xattn_branch(q_mem, mem, mem_ln_g[l], mem_ln_b[l], w_mem_kv[l])
        ys = jnp.stack([y_rwkv, y_conv, y_mem], axis=2)
        proj = jnp.einsum('btnc,ncd->btnd', ys, w_branch[l])
        gates = jax.nn.sigmoid(gate_logits.reshape(b, t, N_BRANCH, d) + gate_b[l])
        mixed = jnp.sum(gates * proj, axis=2) @ w_out[l]
        x = layer_norm(ALPHA * x + mixed, ln2_g[l], ln2_b[l])
        x = layer_norm(ALPHA * x + 0.5 * swiglu(x, ffn2_w_gate[l], ffn2_w_up[l], ffn2_w_down[l]),
                       ln3_g[l], ln3_b[l])
    return x
```

```python
import numpy as np
from contextlib import ExitStack
import concourse.bass as bass
import concourse.mybir as mybir
from concourse.bass_utils import run_bass_kernel_spmd


F32 = mybir.dt.float32
BF16 = mybir.dt.bfloat16
AF = mybir.ActivationFunctionType
ALU = mybir.AluOpType
ND = 6


class Res:
    __slots__ = ("w", "r")

    def __init__(self):
        self.w = None
        self.r = []


def resgrid(*shape):
    a = np.empty(shape, dtype=object)
    for idx in np.ndindex(*shape):
        a[idx] = Res()
    return a


class _Rec:
    def __init__(self):
        self.calls = []

    def __getattr__(self, name):
        def f(*a, **k):
            self.calls.append((name, a, k))
            return None
        return f


def _replay(calls):
    def fn(e):
        ins = None
        for name, a, k in calls:
            ins = getattr(e, name)(*a, **k)
        return ins
    return fn


class Prog:
    ENGS = ("tensor", "vector", "scalar", "gpsimd", "sync")

    def __init__(self, nc, es):
        self.nc = nc
        self.es = es
        self.q = {e: [] for e in self.ENGS}
        self.sem = {}
        for e in ("tensor", "vector", "scalar", "gpsimd"):
            self.sem[("e", e)] = es.enter_context(nc.semaphore("s_" + e))
        self.ecnt = {e: 0 for e in self.ENGS}
        self.dcnt = {}
        self.dnext = {}
        for qn in ("sync", "gpsimd", "scalar"):
            self.dnext[qn] = 0
            for i in range(ND):
                self.sem[("d", qn, i)] = es.enter_context(nc.semaphore(f"d_{qn}{i}"))
                self.dcnt[(qn, i)] = 0
        self.seen = {e: {} for e in self.ENGS}
        self.out_stamps = []

    stop_stage = None

    def stage(self, n):
        if self.stop_stage is not None and n == self.stop_stage:
            raise StopIteration

    def sb(self, name, shape, dt=F32):
        return self.es.enter_context(self.nc.sbuf_tensor(name, list(shape), dt))

    def ps(self, name, shape, dt=F32):
        return self.es.enter_context(self.nc.psum_tensor(name, list(shape), dt))

    def _deps(self, eng, reads, writes, extra=()):
        deps = {}

        def add(st):
            if st is None:
                return
            k, v = st
            if deps.get(k, 0) < v:
                deps[k] = v

        for r in reads:
            add(r.w)
        for w in writes:
            add(w.w)
            for s in w.r:
                add(s)
        for s in extra:
            add(s)
        waits = []
        for k, v in deps.items():
            if eng == "tensor" and k == ("e", "tensor"):
                continue
            if self.seen[eng].get(k, 0) >= v:
                continue
            self.seen[eng][k] = v
            waits.append((k, v))
        return waits

    def _commit(self, st, reads, writes):
        for r in reads:
            r.r.append(st)
        for w in writes:
            w.w = st
            w.r = []

    def op(self, eng, fn, reads=(), writes=()):
        waits = self._deps(eng, reads, writes)
        self.ecnt[eng] += 1
        st = (("e", eng), self.ecnt[eng])
        rec = _Rec()
        fn(rec)
        assert rec.calls
        self.q[eng].append((waits, _replay(rec.calls), st))
        self._commit(st, reads, writes)
        return st

    def dma(self, qn, out, in_, reads=(), writes=(), is_out=False):
        i = self.dnext[qn]
        self.dnext[qn] = (i + 1) % ND
        key = ("d", qn, i)
        prev = self.dcnt[(qn, i)]
        extra = [(key, prev)] if prev > 0 else []
        waits = self._deps(qn, reads, writes, extra)
        self.dcnt[(qn, i)] = prev + 16
        st = (key, prev + 16)
        self.q[qn].append((waits, (lambda e, o=out, i_=in_: e.dma_start(out=o, in_=i_)), st))
        self._commit(st, reads, writes)
        if is_out:
            self.out_stamps.append(st)
        return st

    def finish(self):
        final = {}
        for qn in ("sync", "gpsimd", "scalar"):
            for i in range(ND):
                v = self.dcnt[(qn, i)]
                if v > 0:
                    final[("d", qn, i)] = v
        self.q["sync"].append((list(final.items()), None, None))

    def emit(self):
        nc = self.nc
        with nc.Block() as block:
            def mk(name):
                def body(e):
                    for waits, fn, st in self.q[name]:
                        for k, v in waits:
                            e.wait_ge(self.sem[k], v)
                        if fn is None:
                            continue
                        ins = fn(e)
                        if st is not None:
                            ins.then_inc(self.sem[st[0]], 16 if st[0][0] == "d" else 1)
                return body
            block.tensor(mk("tensor"))
            block.vector(mk("vector"))
            block.scalar(mk("scalar"))
            block.gpsimd(mk("gpsimd"))
            block.sync(mk("sync"))


C = 64
NB = 8
BT = NB * C
NIT = NB * 2
DEC = 0.606531
GN_EPS = 64e-5
(MU_R, MU_K, MU_V, MU_HW, MU_HA, MU_HG, W0F, W0B, A0F, A0B, KK_, KA_, RK_, GNG, GNB, CW0, CW1, CW2) = range(18)
NPV = 18


def scan_consts():
    idx = np.arange(C)
    cm = np.zeros((128, 7, 128), np.float32)
    cm[:, 0, :] = np.eye(128)
    bd = np.zeros((128, 128), np.float32)
    bd[:64, :64] = 1
    bd[64:, 64:] = 1
    cm[:, 1, :] = bd
    cm[:, 2, :] = bd / 64.0
    for d in range(2):
        if d == 0:
            strict = (idx[:, None] < idx[None, :]).astype(np.float32)
            incl = (idx[:, None] <= idx[None, :]).astype(np.float32)
        else:
            strict = (idx[:, None] > idx[None, :]).astype(np.float32)
            incl = (idx[:, None] >= idx[None, :]).astype(np.float32)
        m1 = np.zeros((128, 128), np.float32)
        m1[:64, :64] = -strict
        m1[:64, 64:] = incl
        m1[64:, 64:] = incl
        cm[:, 3 + 2 * d, :] = m1
        m2 = np.zeros((128, 128), np.float32)
        m2[:64, :64] = -strict.T
        m2[:64, 64:] = strict.T
        cm[:, 4 + 2 * d, :] = m2
    rmask = np.ones((128, BT), np.float32)
    rmask[:, ::C] = 0
    return cm, rmask


STOP = None


def build_scan(T, NBATCH):
    nc = bass.Bass("TRN2", target_bir_lowering=False)
    NTOK = T * NBATCH
    zin = nc.dram_tensor("zin", [9, 128, NTOK], F32, kind="ExternalInput").ap()
    pvec = nc.dram_tensor("pvec", [128, NPV], F32, kind="ExternalInput").ap()
    pmat = nc.dram_tensor("pmat", [128, 3, 128], F32, kind="ExternalInput").ap()
    cmat = nc.dram_tensor("cmat", [128, 7, 128], F32, kind="ExternalInput").ap()
    rmk = nc.dram_tensor("rmask", [128, BT], F32, kind="ExternalInput").ap()
    yout = nc.dram_tensor("yout", [2, 128, NTOK], F32, kind="ExternalOutput").ap()
    with ExitStack() as es:
        P = Prog(nc, es)
        try:
            emit_scan(P, zin, pvec, pmat, cmat, rmk, yout, T, NBATCH)
        except StopIteration:
            pass
        P.finish()
        P.emit()
    return nc


def emit_scan(P, zin, pvec, pmat, cmat, rmk, yout, T, NBATCH):
    nblk = T // BT
    sb, ps = P.sb, P.ps
    V, G, A, PE = "vector", "gpsimd", "scalar", "tensor"
    pv = sb("pv", [128, NPV + 8]); r_pv = Res()
    pm = sb("pm", [128, 3, 128]); r_pm = Res()
    cm = sb("cm", [128, 7, 128]); r_cm = Res()
    rm_t = sb("rmaskt", [128, BT]); r_rm = Res()
    P.dma("sync", pv[:, 0:NPV], pvec[:, :], writes=[r_pv])
    P.dma("sync", pm[:], pmat[:, :, :], writes=[r_pm])
    P.dma("sync", cm[:], cmat[:, :, :], writes=[r_cm])
    P.dma("sync", rm_t[:], rmk[:, :], writes=[r_rm])
    ident = cm[:, 0, :]
    bdones = cm[:, 1, :]
    bdavg = cm[:, 2, :]
    hm = sb("hm", [128, 8]); r_hm = Res()
    P.op(V, lambda e: e.tensor_scalar(out=pv[:, NPV:NPV + 6], in0=pv[:, 0:6], scalar1=-1.0, scalar2=1.0,
                                      op0=ALU.mult, op1=ALU.add), reads=[r_pv], writes=[r_hm])
    P.op(V, lambda e: e.tensor_scalar(out=hm[:, 0:6], in0=pv[:, 0:6], scalar1=0.5, scalar2=None, op0=ALU.mult),
         reads=[r_pv], writes=[r_hm])
    P.op(V, lambda e: e.tensor_scalar(out=hm[:, 7:8], in0=pv[:, KA_:KA_ + 1], scalar1=0.5, scalar2=None, op0=ALU.mult),
         reads=[r_pv], writes=[r_hm])
    P.op(V, lambda e: e.tensor_scalar(out=hm[:, 6:7], in0=pv[:, KA_:KA_ + 1], scalar1=-1.0, scalar2=1.0,
                                      op0=ALU.mult, op1=ALU.add), reads=[r_pv], writes=[r_hm])
    epsv = sb("epsv", [128, 2])
    P.op(V, lambda e: e.memset(epsv[:, 0:1], 1e-12), writes=[r_hm])
    P.op(V, lambda e: e.memset(epsv[:, 1:2], GN_EPS), writes=[r_hm])

    def col(j):
        return pv[:, j:j + 1]

    NRAW = 9
    raw = [sb(f"raw{i}", [128, BT + 2]) for i in range(NRAW)]
    r_raw = [Res() for _ in range(NRAW)]
    sh = [sb(f"sh{i}", [128, BT]) for i in range(6)]
    r_sh = [Res() for _ in range(6)]
    tmpA = sb("tmpA", [128, BT]); r_tmpA = Res()
    tmpB = sb("tmpB", [128, BT]); r_tmpB = Res()
    tmpA2 = sb("tmpA2", [128, BT]); r_tmpA2 = Res()
    tmpB2 = sb("tmpB2", [128, BT]); r_tmpB2 = Res()
    th = sb("th", [128, BT]); r_th = Res()
    sig = sb("sig", [128, BT]); r_sig = Res()
    aa = sb("aa", [128, BT]); r_aa = Res()
    af = sb("af", [128, BT]); r_af = Res()
    cumS = sb("cumS", [128, BT]); r_cum = Res()
    cp = sb("cp", [128, BT]); r_cp = Res()
    rmm = sb("rmm", [128, BT]); r_rmm = Res()
    cb = sb("cb", [128, BT]); r_cb = Res()
    eW = sb("eW", [128, BT]); r_eW = Res()
    eWp = sb("eWp", [128, BT]); r_eWp = Res()
    eWi = sb("eWi", [128, BT]); r_eWi = Res()
    eD = sb("eD", [128, BT]); r_eD = Res()
    WCt = sb("WCt", [128, NB]); r_WCt = Res()
    WCs = sb("WCs", [64, NB, 2]); r_WCs = Res()
    kkr = sb("kkr", [128, BT]); r_kkr = Res()
    sq = sb("sq", [128, BT]); r_sq = Res()
    rn = sb("rn", [128, BT]); r_rn = Res()
    kk = sb("kk", [128, BT]); r_kk = Res()
    t1 = sb("t1", [128, BT]); r_t1 = Res()
    kd = sb("kd", [128, BT]); r_kd = Res()
    bb = sb("bb", [128, BT]); r_bb = Res()
    LT = sb("LT", [128, NB, 2, C]); r_LT = Res()
    RT = sb("RT", [128, NB, 2, C]); r_RT = Res()
    bp = sb("bp", [128, BT]); r_bp = Res()
    ktp = sb("ktp", [128, BT]); r_ktp = Res()
    KLin = sb("KLin", [64, NIT, 128]); r_KLa = Res(); r_KLb = [Res() for _ in range(4)]
    RB = sb("RB", [64, NIT, 128]); r_RBa = Res(); r_RBb = [Res() for _ in range(4)]
    Bs = sb("Bs", [128, NIT, 128]); r_Bs_tl = Res(); r_Bs_tr = Res(); r_Bs_bl = Res(); r_Bs_br = [Res() for _ in range(4)]
    PPa = sb("PPa", [64, NIT, 192], BF16); r_PPa = [Res() for _ in range(8)]
    PPb = sb("PPb", [64, NIT, 192], BF16); r_PPb = [Res() for _ in range(8)]
    TTl = [sb("TTa", [64, NIT, 64]), sb("TTb", [64, NIT, 64])]; TTh = [sb("TTha", [64, NIT, 64], BF16), sb("TThb", [64, NIT, 64], BF16)]; r_TTl = [[Res() for _ in range(8)], [Res() for _ in range(8)]]
    KL = sb("KL", [64, NIT, 128]); r_KL = [Res() for _ in range(4)]
    ABQH = sb("ABQH", [128, NIT, 128]); r_AB = [Res() for _ in range(4)]
    Z = sb("Z", [128, NB, 2, 64]); r_Zv = Res(); r_Zs = [Res() for _ in range(NB)]
    STc = sb("STc", [64, 2, 64]); r_ST = Res()
    yf = sb("yf", [128, T]); r_yf = [Res() for _ in range(nblk)]
    cu = sb("cu", [128, BT + 2]); r_cu = Res()
    ysum = eW; r_ysum = r_eW
    ysq = eWp; r_ysq = r_eWp
    m2 = eWi; r_m2 = r_eWi
    yn = eD; r_yn = r_eD
    sg = kkr; r_sg = r_kkr
    rkb = rn; r_rkb = r_rn
    yo = bb; r_yo = r_bb
    hc = bp; r_hc = r_bp
    yc = ktp; r_yc = r_ktp
    pb = [ps(f"pb{i}", [128, 512]) for i in range(8)]
    r_pb = [Res() for _ in range(8)]

    def mm(e, out, lhsT, rhs, start=True, stop=True):
        rp = lhsT.base_partition(); cp_ = out.base_partition()
        if rp or cp_:
            return e.matmul(out, lhsT, rhs, start=start, stop=stop, tile_position=(rp, cp_))
        return e.matmul(out, lhsT, rhs, start=start, stop=stop)

    for b in range(NBATCH):
        for d in range(2):
            P.op(V, lambda e: e.memset(STc[:], 0.0), writes=[r_ST])
            blks = range(nblk) if d == 0 else range(nblk - 1, -1, -1)
            for blk in blks:
                t0 = blk * BT
                g0 = b * T + t0
                narr = 5 if d == 0 else 9
                arrs = [0, 1, 2, 3, 4] if d == 0 else list(range(9))
                for i in arrs:
                    lo = 1 if blk == 0 else 0
                    hi = BT + 1 if blk == nblk - 1 else BT + 2
                    if blk == 0:
                        P.op(G, lambda e, i=i: e.memset(raw[i][:, 0:1], 0.0), writes=[r_raw[i]])
                    if blk == nblk - 1:
                        P.op(G, lambda e, i=i: e.memset(raw[i][:, BT + 1:BT + 2], 0.0), writes=[r_raw[i]])
                    P.dma("sync", raw[i][:, lo:hi], zin[i, :, g0 - 1 + lo:g0 - 1 + hi], writes=[r_raw[i]])
                shl = [0, 1, 2, 3, 4] if d == 0 else [0, 1, 2, 3, 4, 5]
                for n, i in enumerate(shl):
                    e1 = G if n % 2 == 0 else V
                    tA, rA, tB, rB = (tmpA, r_tmpA, tmpB, r_tmpB) if n % 2 == 0 else (tmpA2, r_tmpA2, tmpB2, r_tmpB2)
                    P.op(e1, lambda e, i=i, tA=tA: e.tensor_tensor(out=tA[:], in0=raw[i][:, 0:BT], in1=raw[i][:, 2:BT + 2],
                                                                   op=ALU.add), reads=[r_raw[i]], writes=[rA])
                    P.op(A, lambda e, i=i, tA=tA, tB=tB: e.activation(out=tB[:], in_=tA[:], func=AF.Copy, scale=hm[:, i:i + 1]),
                         reads=[rA, r_hm], writes=[rB])
                    P.op(V, lambda e, i=i, tB=tB: e.scalar_tensor_tensor(out=sh[i][:], in0=raw[i][:, 1:BT + 1],
                                                                         scalar=pv[:, NPV + i:NPV + i + 1], in1=tB[:],
                                                                         op0=ALU.mult, op1=ALU.add),
                         reads=[r_raw[i], rB, r_hm], writes=[r_sh[i]])
                P.stage(1)
                rs, ks, vs, hws, has, hgs = sh
                ds = slice(64 * d, 64 * d + 64)
                P.op(A, lambda e: e.activation(out=th[:], in_=hws[:], func=AF.Tanh), reads=[r_sh[3]], writes=[r_th])
                P.op(PE, lambda e: mm(e, pb[0][:, :], pm[ds, 0, :], th[ds, :]), reads=[r_pm, r_th], writes=[r_pb[0]])
                P.op(PE, lambda e: mm(e, pb[1][:, :], pm[ds, 1, :], has[ds, :]), reads=[r_pm, r_sh[4]], writes=[r_pb[1]])
                P.op(A, lambda e: e.activation(out=sig[:], in_=pb[0][:, :], func=AF.Sigmoid, bias=col(W0F + d), scale=1.0),
                     reads=[r_pb[0], r_pv], writes=[r_sig])
                P.op(A, lambda e: e.activation(out=aa[:], in_=pb[1][:, :], func=AF.Sigmoid, bias=col(A0F + d), scale=1.0),
                     reads=[r_pb[1], r_pv], writes=[r_aa])
                P.stage(2)
                P.op(V, lambda e: e.tensor_tensor_scan(out=cumS[:], data0=rm_t[:], data1=sig[:], initial=0.0,
                                                       op0=ALU.mult, op1=ALU.add), reads=[r_rm, r_sig], writes=[r_cum])
                P.op(V, lambda e: e.tensor_tensor(out=cp[:], in0=cumS[:], in1=sig[:], op=ALU.subtract),
                     reads=[r_cum, r_sig], writes=[r_cp])
                cum3 = cumS[:].rearrange("p (c t) -> p c t", t=C)
                tot_b = cum3[:, :, C - 1:C].to_broadcast([128, NB, C])
                P.op(V, lambda e: e.tensor_tensor(out=rmm[:].rearrange("p (c t) -> p c t", t=C), in0=tot_b, in1=cum3,
                                                  op=ALU.subtract), reads=[r_cum], writes=[r_rmm])
                if d == 0:
                    c_cum, c_prev, c_rem = cumS, cp, rmm
                    rr_cum, rr_prev, rr_rem = r_cum, r_cp, r_rmm
                else:
                    P.op(V, lambda e: e.tensor_tensor(out=cb[:], in0=rmm[:], in1=sig[:], op=ALU.add),
                         reads=[r_rmm, r_sig], writes=[r_cb])
                    c_cum, c_prev, c_rem = cb, rmm, cp
                    rr_cum, rr_prev, rr_rem = r_cb, r_rmm, r_cp
                P.op(A, lambda e: e.activation(out=eW[:], in_=c_cum[:], func=AF.Exp, scale=-DEC), reads=[rr_cum], writes=[r_eW])
                P.op(A, lambda e: e.activation(out=eWp[:], in_=c_prev[:], func=AF.Exp, scale=-DEC), reads=[rr_prev], writes=[r_eWp])
                P.op(A, lambda e: e.activation(out=eWi[:], in_=c_cum[:], func=AF.Exp, scale=DEC), reads=[rr_cum], writes=[r_eWi])
                P.op(A, lambda e: e.activation(out=eD[:], in_=c_rem[:], func=AF.Exp, scale=-DEC), reads=[rr_rem], writes=[r_eD])
                P.op(A, lambda e: e.activation(out=WCt[:, :], in_=cum3[:, :, C - 1], func=AF.Exp, scale=-DEC),
                     reads=[r_cum], writes=[r_WCt])
                P.op(V, lambda e: e.tensor_copy(out=WCs[:, :, 0], in_=WCt[0:64, :]), reads=[r_WCt], writes=[r_WCs])
                P.op(V, lambda e: e.tensor_copy(out=WCs[:, :, 1], in_=WCt[64:128, :]), reads=[r_WCt], writes=[r_WCs])
                P.stage(3)
                P.op(V, lambda e: e.tensor_scalar(out=kkr[:], in0=ks[:], scalar1=col(KK_), scalar2=None, op0=ALU.mult),
                     reads=[r_sh[1], r_pv], writes=[r_kkr])
                P.op(G, lambda e: e.tensor_tensor(out=sq[:], in0=kkr[:], in1=kkr[:], op=ALU.mult), reads=[r_kkr], writes=[r_sq])
                P.op(PE, lambda e: mm(e, pb[2][:, :], bdones, sq[:]), reads=[r_cm, r_sq], writes=[r_pb[2]])
                P.op(A, lambda e: e.activation(out=rn[:], in_=pb[2][:, :], func=AF.Sqrt, bias=epsv[:, 0:1], scale=1.0),
                     reads=[r_pb[2], r_hm], writes=[r_rn])
                P.op(V, lambda e: e.reciprocal(out=rn[:], in_=rn[:]), reads=[r_rn], writes=[r_rn])
                P.op(G, lambda e: e.tensor_tensor(out=kk[:], in0=kkr[:], in1=rn[:], op=ALU.mult), reads=[r_kkr, r_rn], writes=[r_kk])
                P.op(V, lambda e: e.tensor_scalar(out=t1[:], in0=aa[:], scalar1=col(KA_), scalar2=hm[:, 6:7],
                                                  op0=ALU.mult, op1=ALU.add), reads=[r_aa, r_pv, r_hm], writes=[r_t1])
                P.op(G, lambda e: e.tensor_tensor(out=kd[:], in0=ks[:], in1=t1[:], op=ALU.mult), reads=[r_sh[1], r_t1], writes=[r_kd])
                P.op(G, lambda e: e.tensor_tensor(out=bb[:], in0=kk[:], in1=aa[:], op=ALU.mult), reads=[r_kk, r_aa], writes=[r_bb])
                P.stage(4)
                P.op(V, lambda e: e.tensor_tensor(out=LT[:, :, 0, :], in0=bb[:].rearrange("p (c t) -> p c t", t=C), in1=eWi[:].rearrange("p (c t) -> p c t", t=C), op=ALU.mult), reads=[r_bb, r_eWi], writes=[r_LT])
                P.op(G, lambda e: e.tensor_tensor(out=LT[:, :, 1, :], in0=kd[:].rearrange("p (c t) -> p c t", t=C), in1=eWi[:].rearrange("p (c t) -> p c t", t=C), op=ALU.mult), reads=[r_kd, r_eWi], writes=[r_LT])
                P.op(V, lambda e: e.tensor_tensor(out=RT[:, :, 0, :], in0=kk[:].rearrange("p (c t) -> p c t", t=C), in1=eWp[:].rearrange("p (c t) -> p c t", t=C), op=ALU.mult), reads=[r_kk, r_eWp], writes=[r_RT])
                P.op(G, lambda e: e.tensor_tensor(out=RT[:, :, 1, :], in0=rs[:].rearrange("p (c t) -> p c t", t=C), in1=eW[:].rearrange("p (c t) -> p c t", t=C), op=ALU.mult), reads=[r_sh[0], r_eW], writes=[r_RT])
                P.op(V, lambda e: e.tensor_tensor(out=bp[:], in0=bb[:], in1=eD[:], op=ALU.mult), reads=[r_bb, r_eD], writes=[r_bp])
                P.op(G, lambda e: e.tensor_tensor(out=ktp[:], in0=kd[:], in1=eD[:], op=ALU.mult), reads=[r_kd, r_eD], writes=[r_ktp])
                P.stage(5)
                side = []

                def transp(src_ap_fn, rsrc, bank0, evac):
                    def f(e):
                        ins = None
                        for c in range(NB):
                            o = pb[bank0 + c // 4][0:64, (c % 4) * 128:(c % 4 + 1) * 128]
                            ins = e.transpose(o, src_ap_fn(c), ident)
                        return ins
                    def thunk():
                        P.op(PE, f, reads=[rsrc, r_cm], writes=[r_pb[bank0], r_pb[bank0 + 1]])
                        evac()
                    side.append(thunk)

                def ps2(bank0):
                    return [pb[bank0 + j][0:64, :].rearrange("p (c n) -> p c n", n=128) for j in range(2)]

                def ev_kap():
                    for j in range(2):
                        src = ps2(4)[j].rearrange("p c (h k) -> p c h k", h=2)
                        dst = KLin[:, j * 8:(j + 1) * 8, 0:64].rearrange("p (c h) k -> p c h k", h=2)
                        P.op(V if j == 0 else A,
                             (lambda e, s=src, d_=dst: e.tensor_copy(out=d_, in_=s)) if j == 0 else
                             (lambda e, s=src, d_=dst: e.activation(out=d_, in_=s, func=AF.Copy)),
                             reads=[r_pb[4 + j]], writes=[r_KLa])
                transp(lambda c: RT[:, c, 0, :], r_RT, 4, ev_kap)

                def ev_bp():
                    for j in range(2):
                        src = ps2(6)[j].rearrange("p c (h k) -> p c h k", h=2)
                        dst = RB[:, j * 8:(j + 1) * 8, 0:64].rearrange("p (c h) k -> p c h k", h=2)
                        P.op(V if j == 0 else A,
                             (lambda e, s=src, d_=dst: e.tensor_copy(out=d_, in_=s)) if j == 0 else
                             (lambda e, s=src, d_=dst: e.activation(out=d_, in_=s, func=AF.Copy)),
                             reads=[r_pb[6 + j]], writes=[r_RBa])
                transp(lambda c: bp[:, c * C:(c + 1) * C], r_bp, 6, ev_bp)

                def ev_ktp():
                    for j in range(2):
                        src = ps2(4)[j].rearrange("p c (h k) -> p c h k", h=2)
                        dst = Bs[64:128, j * 8:(j + 1) * 8, 0:64].rearrange("p (c h) k -> p c h k", h=2)
                        P.op(V if j == 0 else A,
                             (lambda e, s=src, d_=dst: e.tensor_copy(out=d_, in_=s)) if j == 0 else
                             (lambda e, s=src, d_=dst: e.activation(out=d_, in_=s, func=AF.Copy)),
                             reads=[r_pb[4 + j]], writes=[r_Bs_bl])
                transp(lambda c: ktp[:, c * C:(c + 1) * C], r_ktp, 4, ev_ktp)

                def ev_v():
                    for j in range(2):
                        src = ps2(6)[j].rearrange("p c (h k) -> p c h k", h=2)
                        dst = Z[64:128, j * 4:(j + 1) * 4, :, :]
                        P.op(V if j == 0 else A,
                             (lambda e, s=src, d_=dst: e.tensor_copy(out=d_, in_=s)) if j == 0 else
                             (lambda e, s=src, d_=dst: e.activation(out=d_, in_=s, func=AF.Copy)),
                             reads=[r_pb[6 + j]], writes=[r_Zv])
                transp(lambda c: vs[:, c * C:(c + 1) * C], r_sh[2], 6, ev_v)
                P.stage(6)
                idb = ident[0:64, 0:64].unsqueeze(1).to_broadcast([64, NIT, 64])
                wcb = WCs[:].rearrange("p c h -> p (c h)").unsqueeze(2).to_broadcast([64, NIT, 64])
                def bs_thunk():
                    P.op(G, lambda e: e.tensor_tensor(out=Bs[0:64, :, 0:64], in0=idb, in1=wcb, op=ALU.mult),
                         reads=[r_cm, r_WCs], writes=[r_Bs_tl])
                    for h in range(2):
                        dst = Bs[0:64, :, 64:128].rearrange("p (c h) t -> p c h t", h=2)[:, :, h, :]
                        src = RT[64 * h:64 * h + 64, :, 1, :]
                        P.op(G, lambda e, s=src, d_=dst: e.tensor_copy(out=d_, in_=s), reads=[r_RT], writes=[r_Bs_tr])
                side.append(bs_thunk)
                P.stage(7)
                m1 = cm[:, 3 + 2 * d, :]
                m2_ = cm[0:64, 4 + 2 * d, :]
                for grp in range(NIT // 4):
                    bk1 = grp % 2
                    bk2 = 2 + grp % 2

                    def fg(e, grp=grp, bk1=bk1, bk2=bk2):
                        ins = None
                        for q in range(4):
                            it = grp * 4 + q
                            c, h = it // 2, it % 2
                            hs = slice(64 * h, 64 * h + 64)
                            cs = slice(c * C, (c + 1) * C)
                            mm(e, pb[bk1][:, q * 128:(q + 1) * 128], LT[hs, c].rearrange("p a t -> p (a t)"), RT[hs, c].rearrange("p a t -> p (a t)"))
                            ins = mm(e, pb[bk2][0:64, q * 128:(q + 1) * 128], RT[hs, c, 0, :], LT[hs, c].rearrange("p a t -> p (a t)"))
                        return ins
                    P.op(PE, fg, reads=[r_LT, r_RT], writes=[r_pb[bk1], r_pb[bk2]])
                    its = slice(grp * 4, grp * 4 + 4)
                    g1 = pb[bk1][:, :].rearrange("p (q n) -> p q n", n=128)
                    g2 = pb[bk2][0:64, :].rearrange("p (q n) -> p q n", n=128)
                    mTL = m1[0:64, 0:64].unsqueeze(1).to_broadcast([64, 4, 64])
                    mTR = m1[0:64, 64:128].unsqueeze(1).to_broadcast([64, 4, 64])
                    mBR = m1[64:128, 64:128].unsqueeze(1).to_broadcast([64, 4, 64])
                    m2L = m2_[:, 0:64].unsqueeze(1).to_broadcast([64, 4, 64])
                    m2R = m2_[:, 64:128].unsqueeze(1).to_broadcast([64, 4, 64])
                    P.op(V, lambda e, its=its, g1=g1, mTL=mTL: e.tensor_tensor(out=PPa[:, its, 0:64], in0=g1[0:64, :, 0:64], in1=mTL, op=ALU.mult),
                         reads=[r_pb[bk1], r_cm], writes=[r_PPa[2 * grp], r_PPa[2 * grp + 1]])
                    P.op(V, lambda e, its=its, g1=g1, mTR=mTR: e.tensor_tensor(out=RB[:, its, 64:128], in0=g1[0:64, :, 64:128], in1=mTR, op=ALU.mult),
                         reads=[r_pb[bk1], r_cm], writes=[r_RBb[grp]])
                    P.op(V, lambda e, its=its, g1=g1, mBR=mBR: e.tensor_tensor(out=Bs[64:128, its, 64:128], in0=g1[64:128, :, 64:128], in1=mBR, op=ALU.mult),
                         reads=[r_pb[bk1], r_cm], writes=[r_Bs_br[grp]])
                    P.op(V, lambda e, its=its, g2=g2, m2L=m2L: e.tensor_tensor(out=PPa[:, its, 64:128], in0=g2[:, :, 0:64], in1=m2L, op=ALU.mult),
                         reads=[r_pb[bk2], r_cm], writes=[r_PPa[2 * grp], r_PPa[2 * grp + 1]])
                    P.op(V, lambda e, its=its, g2=g2, m2R=m2R: e.tensor_tensor(out=KLin[:, its, 64:128], in0=g2[:, :, 64:128], in1=m2R, op=ALU.mult),
                         reads=[r_pb[bk2], r_cm], writes=[r_KLb[grp]])
                P.stage(8)
                idb16 = ident[0:64, 0:64].unsqueeze(1).to_broadcast([64, NIT, 64])
                P.op(V, lambda e: e.tensor_tensor(out=TTl[1][:], in0=PPa[:, :, 0:64], in1=idb16, op=ALU.add),
                     reads=list(r_PPa) + [r_cm], writes=list(r_TTl[1]))
                P.op(G, lambda e: e.tensor_copy(out=TTh[1][:], in_=TTl[1][:]), reads=list(r_TTl[1]), writes=list(r_TTl[1]))
                cur, r_cur, nxt, r_nxt = PPa, r_PPa, PPb, r_PPb
                for s in range(1, 7):
                    for grp in range(NIT // 2):
                        bk = grp % 4
                        its = slice(grp * 2, grp * 2 + 2)

                        def fi(e, s=s, grp=grp, bk=bk, cur=cur):
                            ins = None
                            for q in range(2):
                                it = grp * 2 + q
                                o = pb[bk][0:64, q * 192:(q + 1) * 192]
                                Pm = cur[:, it, 0:64]
                                PmT = cur[:, it, 64:128]
                                if s <= 5:
                                    ins = mm(e, o[:, 0:64], PmT, Pm)
                                    ins = mm(e, o[:, 64:128], Pm, PmT)
                                if s >= 2:
                                    ins = mm(e, o[:, 128:192], PmT, TTh[(s - 1) % 2][:, it, :])
                            return ins
                        rds = [r_cur[grp]] + ([r_TTl[(s - 1) % 2][grp]] if s >= 2 else [])
                        P.op(PE, fi, reads=rds, writes=[r_pb[bk]])
                        o3 = pb[bk][0:64, 0:384].rearrange("p (q n) -> p q n", n=192)
                        lo_c = 0 if s <= 5 else 128
                        hi_c = 192 if s >= 2 else 128
                        if grp % 2:
                            P.op(A, lambda e, its=its, o3=o3, nxt=nxt, lo_c=lo_c, hi_c=hi_c: e.activation(out=nxt[:, its, lo_c:hi_c], in_=o3[:, :, lo_c:hi_c], func=AF.Copy),
                                 reads=[r_pb[bk]], writes=[r_nxt[grp]])
                        else:
                            P.op(V, lambda e, its=its, o3=o3, nxt=nxt, lo_c=lo_c, hi_c=hi_c: e.tensor_copy(out=nxt[:, its, lo_c:hi_c], in_=o3[:, :, lo_c:hi_c]),
                                 reads=[r_pb[bk]], writes=[r_nxt[grp]])
                        if s >= 2:
                            P.op(G, lambda e, its=its, nxt=nxt, s=s: e.tensor_tensor(out=TTl[s % 2][:, its, :], in0=nxt[:, its, 128:192], in1=TTl[(s - 1) % 2][:, its, :], op=ALU.add),
                                 reads=[r_nxt[grp], r_TTl[(s - 1) % 2][grp]], writes=[r_TTl[s % 2][grp]])
                            if s <= 5:
                                P.op(G, lambda e, its=its, s=s: e.tensor_copy(out=TTh[s % 2][:, its, :], in_=TTl[s % 2][:, its, :]),
                                     reads=[r_TTl[s % 2][grp]], writes=[r_TTl[s % 2][grp]])
                        if side and grp % 2 == 1:
                            side.pop(0)()
                    cur, r_cur, nxt, r_nxt = nxt, r_nxt, cur, r_cur
                while side:
                    side.pop(0)()
                P.stage(9)
                for grp in range(NIT // 4):
                    bk = 6 + grp % 2

                    def f7(e, grp=grp, bk=bk):
                        ins = None
                        for q in range(4):
                            it = grp * 4 + q
                            ins = mm(e, pb[bk][0:64, q * 128:(q + 1) * 128], TTl[0][:, it, :], KLin[:, it, :])
                        return ins
                    P.op(PE, f7, reads=[r_TTl[0][2 * grp], r_TTl[0][2 * grp + 1], r_KLa, r_KLb[grp]], writes=[r_pb[bk]])
                    its = slice(grp * 4, grp * 4 + 4)
                    src = pb[bk][0:64, :].rearrange("p (q n) -> p q n", n=128)
                    P.op(A, lambda e, its=its, src=src: e.activation(out=KL[:, its, :], in_=src, func=AF.Copy),
                         reads=[r_pb[bk]], writes=[r_KL[grp]])
                for grp in range(NIT // 4):
                    bk = grp % 2

                    def f8(e, grp=grp, bk=bk):
                        ins = None
                        for q in range(4):
                            it = grp * 4 + q
                            ins = mm(e, pb[bk][:, q * 128:(q + 1) * 128], KL[:, it, :], RB[:, it, :])
                        return ins
                    P.op(PE, f8, reads=[r_KL[grp], r_RBa, r_RBb[grp]], writes=[r_pb[bk]])
                    its = slice(grp * 4, grp * 4 + 4)
                    src = pb[bk][:, :].rearrange("p (q n) -> p q n", n=128)
                    P.op(V, lambda e, its=its, src=src: e.scalar_tensor_tensor(out=ABQH[:, its, :], in0=src, scalar=-1.0, in1=Bs[:, its, :], op0=ALU.mult, op1=ALU.add),
                         reads=[r_pb[bk], r_Bs_tl, r_Bs_tr, r_Bs_bl, r_Bs_br[grp]], writes=[r_AB[grp]])
                P.stage(10)
                order = list(range(NB)) if d == 0 else list(range(NB - 1, -1, -1))
                P.op(V, lambda e, c0=order[0]: e.tensor_copy(out=Z[0:64, c0, :, :], in_=STc[:]), reads=[r_ST], writes=[r_Zs[order[0]]])
                ybank = 7
                for n, c in enumerate(order):
                    sbk = 2 + n % 2

                    def fs(e, c=c, sbk=sbk):
                        ins = None
                        for h in range(2):
                            it = c * 2 + h
                            ins = mm(e, pb[sbk][0:64, h * 64:(h + 1) * 64], ABQH[:, it, 0:64], Z[:, c, h, :])
                        return ins
                    P.op(PE, fs, reads=[r_AB[c // 2], r_Zv, r_Zs[c]], writes=[r_pb[sbk]])
                    src = pb[sbk][0:64, 0:128].rearrange("p (h v) -> p h v", h=2)
                    if n < NB - 1:
                        cn = order[n + 1]
                        P.op(A, lambda e, cn=cn, src=src: e.activation(out=Z[0:64, cn, :, :], in_=src, func=AF.Copy),
                             reads=[r_pb[sbk]], writes=[r_Zs[cn]])
                    else:
                        P.op(A, lambda e, src=src: e.activation(out=STc[:], in_=src, func=AF.Copy),
                             reads=[r_pb[sbk]], writes=[r_ST])

                    def fy(e, c=c):
                        ins = None
                        for h in range(2):
                            it = c * 2 + h
                            ins = mm(e, pb[ybank][64 * h:64 * h + 64, c * C:(c + 1) * C], Z[:, c, h, :], ABQH[:, it, 64:128])
                        return ins
                    P.op(PE, fy, reads=[r_AB[c // 2], r_Zv, r_Zs[c]], writes=[r_pb[ybank]])
                P.stage(11)
                if d == 0:
                    P.op(A, lambda e, t0=t0: e.activation(out=yf[:, t0:t0 + BT], in_=pb[ybank][:, :], func=AF.Copy),
                         reads=[r_pb[ybank]], writes=[r_yf[blk]])
                    continue
                P.op(V, lambda e, t0=t0: e.tensor_tensor(out=ysum[:], in0=pb[ybank][:, :], in1=yf[:, t0:t0 + BT], op=ALU.add),
                     reads=[r_pb[ybank], r_yf[blk]], writes=[r_ysum])
                P.op(A, lambda e: e.activation(out=ysq[:], in_=ysum[:], func=AF.Square), reads=[r_ysum], writes=[r_ysq])
                P.op(PE, lambda e: mm(e, pb[0][:, :], bdavg, ysum[:]), reads=[r_cm, r_ysum], writes=[r_pb[0]])
                P.op(PE, lambda e: mm(e, pb[1][:, :], bdavg, ysq[:]), reads=[r_cm, r_ysq], writes=[r_pb[1]])
                P.op(A, lambda e: e.activation(out=m2[:], in_=pb[0][:, :], func=AF.Square), reads=[r_pb[0]], writes=[r_m2])
                P.op(V, lambda e: e.tensor_tensor(out=m2[:], in0=pb[1][:, :], in1=m2[:], op=ALU.subtract), reads=[r_pb[1], r_m2], writes=[r_m2])
                P.op(A, lambda e: e.activation(out=m2[:], in_=m2[:], func=AF.Sqrt, bias=epsv[:, 1:2], scale=1.0),
                     reads=[r_m2, r_hm], writes=[r_m2])
                P.op(V, lambda e: e.reciprocal(out=m2[:], in_=m2[:]), reads=[r_m2], writes=[r_m2])
                P.op(V, lambda e: e.scalar_tensor_tensor(out=yn[:], in0=pb[0][:, :], scalar=-1.0, in1=ysum[:], op0=ALU.mult, op1=ALU.add), reads=[r_ysum, r_pb[0]], writes=[r_yn])
                P.op(V, lambda e: e.tensor_tensor(out=yn[:], in0=yn[:], in1=m2[:], op=ALU.mult), reads=[r_yn, r_m2], writes=[r_yn])
                P.op(V, lambda e: e.tensor_scalar(out=yn[:], in0=yn[:], scalar1=col(GNG), scalar2=col(GNB), op0=ALU.mult, op1=ALU.add),
                     reads=[r_yn, r_pv], writes=[r_yn])
                P.op(PE, lambda e: mm(e, pb[2][:, :], pm[0:64, 1, :], has[0:64, :]), reads=[r_pm, r_sh[4]], writes=[r_pb[2]])
                P.op(A, lambda e: e.activation(out=af[:], in_=pb[2][:, :], func=AF.Sigmoid, bias=col(A0F), scale=1.0),
                     reads=[r_pb[2], r_pv], writes=[r_af])
                P.op(G, lambda e: e.tensor_tensor(out=af[:], in0=af[:], in1=aa[:], op=ALU.add), reads=[r_af, r_aa], writes=[r_af])
                P.op(V, lambda e: e.tensor_scalar(out=t1[:], in0=af[:], scalar1=hm[:, 7:8], scalar2=hm[:, 6:7], op0=ALU.mult, op1=ALU.add),
                     reads=[r_af, r_pv, r_hm], writes=[r_t1])
                P.op(G, lambda e: e.tensor_tensor(out=rkb[:], in0=ks[:], in1=t1[:], op=ALU.mult), reads=[r_sh[1], r_t1], writes=[r_rkb])
                P.op(V, lambda e: e.scalar_tensor_tensor(out=rkb[:], in0=rkb[:], scalar=col(RK_), in1=rs[:], op0=ALU.mult, op1=ALU.mult),
                     reads=[r_rkb, r_sh[0], r_pv], writes=[r_rkb])
                P.op(PE, lambda e: mm(e, pb[3][:, :], bdones, rkb[:]), reads=[r_cm, r_rkb], writes=[r_pb[3]])
                P.op(V, lambda e: e.tensor_tensor(out=rkb[:], in0=pb[3][:, :], in1=vs[:], op=ALU.mult), reads=[r_pb[3], r_sh[2]], writes=[r_rkb])
                P.op(V, lambda e: e.tensor_tensor(out=yn[:], in0=yn[:], in1=rkb[:], op=ALU.add), reads=[r_yn, r_rkb], writes=[r_yn])
                P.op(A, lambda e: e.activation(out=sg[:], in_=hgs[:], func=AF.Sigmoid), reads=[r_sh[5]], writes=[r_sg])
                P.op(PE, lambda e: mm(e, pb[4][:, :], pm[:, 2, :], sg[:]), reads=[r_pm, r_sg], writes=[r_pb[4]])
                P.op(V, lambda e: e.tensor_tensor(out=yo[:], in0=pb[4][:, :], in1=yn[:], op=ALU.mult), reads=[r_yn, r_pb[4]], writes=[r_yo])
                P.dma("sync", yout[0, :, g0:g0 + BT], yo[:], reads=[r_yo], is_out=True)
                P.op(G, lambda e: e.tensor_tensor(out=cu[:], in0=raw[7][:], in1=raw[8][:], op=ALU.mult), reads=[r_raw[7], r_raw[8]], writes=[r_cu])
                P.op(V, lambda e: e.tensor_scalar(out=hc[:], in0=cu[:, 0:BT], scalar1=col(CW0), scalar2=None, op0=ALU.mult),
                     reads=[r_cu, r_pv], writes=[r_hc])
                P.op(V, lambda e: e.scalar_tensor_tensor(out=hc[:], in0=cu[:, 1:BT + 1], scalar=col(CW1), in1=hc[:], op0=ALU.mult, op1=ALU.add),
                     reads=[r_cu, r_hc, r_pv], writes=[r_hc])
                P.op(V, lambda e: e.scalar_tensor_tensor(out=hc[:], in0=cu[:, 2:BT + 2], scalar=col(CW2), in1=hc[:], op0=ALU.mult, op1=ALU.add),
                     reads=[r_cu, r_hc, r_pv], writes=[r_hc])
                P.op(G, lambda e: e.tensor_tensor(out=yc[:], in0=hc[:], in1=raw[6][:, 1:BT + 1], op=ALU.mult), reads=[r_hc, r_raw[6]], writes=[r_yc])
                P.dma("sync", yout[1, :, g0:g0 + BT], yc[:], reads=[r_yc], is_out=True)


D = 2048
KC = 16
F = 5504
FC = 43
NT = 1024
TT = 512
NTT = NT // TT
D_IN = 13696
RC = 6528
QC0 = 6528
GC0 = 7552
ALPHA = (2 * 2) ** 0.25
LN_EPS = 1e-5
V, G, A, PE = "vector", "gpsimd", "scalar", "tensor"
FGROUPS = [(0, 11), (11, 22), (22, 33), (33, 43)]


class DenseCtx:
    def __init__(self, P):
        self.P = P
        sb, ps = P.sb, P.ps
        self.pb = [ps(f"pb{i}", [128, 512]) for i in range(8)]
        self.r_pb = [Res() for _ in range(8)]
        self.bank = 0
        self.onesf = sb("onesf", [128, 128]); self.r_c = Res()
        self.onesb = sb("onesb", [128, 128], BF16)
        self.epsv = sb("epsv", [128, 1])
        P.op(V, lambda e: e.memset(self.onesf[:], 1.0 / D), writes=[self.r_c])
        P.op(V, lambda e: e.memset(self.onesb[:], 1.0), writes=[self.r_c])
        P.op(V, lambda e: e.memset(self.epsv[:], LN_EPS), writes=[self.r_c])
        self.wA = [sb(f"wA{i}", [128, 16, 256], BF16) for i in range(3)]
        self.r_wA = [Res() for _ in range(3)]
        self.wA_i = 0
        self.wD = [sb(f"wD{i}", [128, 11, 256], BF16) for i in range(2)]
        self.r_wD = [Res() for _ in range(2)]
        self.wD_i = 0
        self.tmp = [sb(f"tmp{i}", [128, 512]) for i in range(4)]
        self.r_tmp = [Res() for _ in range(4)]
        self.tmp_i = 0
        self.stat = [sb(f"stat{i}", [128, 512]) for i in range(3)]
        self.r_stat = [Res() for _ in range(3)]

    def nbank(self):
        b = self.bank
        self.bank = (b + 1) % 8
        return b

    def ntmp(self):
        i = self.tmp_i
        self.tmp_i = (i + 1) % 4
        return i

    def load_wA(self, w_ap, kc, mcols):
        i = self.wA_i
        self.wA_i = (i + 1) % 3
        t = self.wA[i]
        self.P.dma("gpsimd", t[:, 0:kc, 0:mcols], w_ap.rearrange("(k p) m -> p k m", p=128), writes=[self.r_wA[i]])
        return t, self.r_wA[i]

    def load_wD(self, w_ap, kc, mcols):
        i = self.wD_i
        self.wD_i = (i + 1) % 2
        t = self.wD[i]
        self.P.dma("gpsimd", t[:, 0:kc, 0:mcols], w_ap.rearrange("(k p) m -> p k m", p=128), writes=[self.r_wD[i]])
        return t, self.r_wD[i]


def mm_group(P, ctx, bank, pairs, reads, n=512, mrows=128):
    def f(e):
        ins = None
        L = len(pairs)
        for i, (l, r) in enumerate(pairs):
            ins = e.matmul(ctx.pb[bank][0:mrows, 0:n], l, r, start=(i == 0), stop=(i == L - 1))
        return ins
    P.op(PE, f, reads=reads, writes=[ctx.r_pb[bank]])


def layer_norm_fm(P, ctx, X, r_X, XB, r_XB, gb, r_gb, gcol, bcol, kc_n=KC, ntok=NT, write_x=True):
    tts = [(t0, min(TT, ntok - t0)) for t0 in range(0, ntok, TT)]
    for (t0, n) in tts:
        b_sum = ctx.nbank()
        mm_group(P, ctx, b_sum, [(ctx.onesf[:], X[:, kc, t0:t0 + n]) for kc in range(kc_n)], [ctx.r_c] + [r_X[kc] for kc in range(kc_n)], n=n)
        b_sq = ctx.nbank()
        sqs = []
        for kc in range(kc_n):
            ti = ctx.ntmp()
            P.op(A if kc % 2 else G, (lambda e, ti=ti, kc=kc: e.activation(out=ctx.tmp[ti][:, 0:n], in_=X[:, kc, t0:t0 + n], func=AF.Square)) if kc % 2 else
                 (lambda e, ti=ti, kc=kc: e.tensor_tensor(out=ctx.tmp[ti][:, 0:n], in0=X[:, kc, t0:t0 + n], in1=X[:, kc, t0:t0 + n], op=ALU.mult)),
                 reads=[r_X[kc]], writes=[ctx.r_tmp[ti]])
            P.op(PE, lambda e, ti=ti, kc=kc: e.matmul(ctx.pb[b_sq][:, 0:n], ctx.onesf[:], ctx.tmp[ti][:, 0:n], start=(kc == 0), stop=(kc == kc_n - 1)),
                 reads=[ctx.r_c, ctx.r_tmp[ti]], writes=[ctx.r_pb[b_sq]])
        mean_ps = ctx.pb[b_sum][:, 0:n]
        e2_ps = ctx.pb[b_sq][:, 0:n]
        m2, rstd, nmr = ctx.stat[0][:, 0:n], ctx.stat[1][:, 0:n], ctx.stat[2][:, 0:n]
        P.op(A, lambda e: e.activation(out=m2, in_=mean_ps, func=AF.Square), reads=[ctx.r_pb[b_sum]], writes=[ctx.r_stat[0]])
        P.op(V, lambda e: e.tensor_tensor(out=rstd, in0=e2_ps, in1=m2, op=ALU.subtract), reads=[ctx.r_pb[b_sq], ctx.r_stat[0]], writes=[ctx.r_stat[1]])
        P.op(A, lambda e: e.activation(out=rstd, in_=rstd, func=AF.Sqrt, bias=ctx.epsv[:, 0:1], scale=1.0), reads=[ctx.r_stat[1], ctx.r_c], writes=[ctx.r_stat[1]])
        P.op(V, lambda e: e.reciprocal(out=rstd, in_=rstd), reads=[ctx.r_stat[1]], writes=[ctx.r_stat[1]])
        P.op(V, lambda e: e.scalar_tensor_tensor(out=nmr, in0=mean_ps, scalar=-1.0, in1=rstd, op0=ALU.mult, op1=ALU.mult),
             reads=[ctx.r_pb[b_sum], ctx.r_stat[1]], writes=[ctx.r_stat[2]])
        for kc in range(kc_n):
            ti = ctx.ntmp()
            t = ctx.tmp[ti][:, 0:n]
            P.op(V, lambda e, t=t, kc=kc: e.tensor_tensor(out=t, in0=X[:, kc, t0:t0 + n], in1=rstd, op=ALU.mult),
                 reads=[r_X[kc], ctx.r_stat[1]], writes=[ctx.r_tmp[ti]])
            P.op(G, lambda e, t=t: e.tensor_tensor(out=t, in0=t, in1=nmr, op=ALU.add), reads=[ctx.r_tmp[ti], ctx.r_stat[2]], writes=[ctx.r_tmp[ti]])
            if write_x:
                P.op(V, lambda e, t=t, kc=kc: e.tensor_scalar(out=X[:, kc, t0:t0 + n], in0=t, scalar1=gb[:, gcol, kc:kc + 1], scalar2=gb[:, bcol, kc:kc + 1],
                                                              op0=ALU.mult, op1=ALU.add), reads=[ctx.r_tmp[ti], r_gb], writes=[r_X[kc]])
                P.op(A, lambda e, kc=kc: e.activation(out=XB[:, kc, t0:t0 + n], in_=X[:, kc, t0:t0 + n], func=AF.Copy), reads=[r_X[kc]], writes=[r_XB[kc]])
            else:
                P.op(V, lambda e, t=t, kc=kc: e.tensor_scalar(out=XB[:, kc, t0:t0 + n], in0=t, scalar1=gb[:, gcol, kc:kc + 1], scalar2=gb[:, bcol, kc:kc + 1],
                                                              op0=ALU.mult, op1=ALU.add), reads=[ctx.r_tmp[ti], r_gb], writes=[r_XB[kc]])


def ffn(P, ctx, X, r_X, XB, r_XB, H, r_H, wg, wu, wd):
    for kc in range(KC):
        P.op(G, lambda e, kc=kc: e.tensor_scalar(out=X[:, kc, :], in0=X[:, kc, :], scalar1=ALPHA, scalar2=None, op0=ALU.mult),
             reads=[r_X[kc]], writes=[r_X[kc]])
    for (f0, f1) in FGROUPS:
        nf = f1 - f0
        for fb in range(f0, f1, 2):
            nb = min(2, f1 - fb)
            wgt, r_wg = ctx.load_wA(wg[:, fb * 128:(fb + nb) * 128], KC, nb * 128)
            wut, r_wu = ctx.load_wA(wu[:, fb * 128:(fb + nb) * 128], KC, nb * 128)
            for j in range(nb):
                fi = fb + j - f0
                for tt in range(NTT):
                    ts_ = slice(tt * TT, (tt + 1) * TT)
                    bg = ctx.nbank()
                    mm_group(P, ctx, bg, [(wgt[:, kc, j * 128:(j + 1) * 128], XB[:, kc, ts_]) for kc in range(KC)], [r_wg] + list(r_XB))
                    bu = ctx.nbank()
                    mm_group(P, ctx, bu, [(wut[:, kc, j * 128:(j + 1) * 128], XB[:, kc, ts_]) for kc in range(KC)], [r_wu] + list(r_XB))
                    ti = ctx.ntmp()
                    P.op(A, lambda e, ti=ti, bg=bg: e.activation(out=ctx.tmp[ti][:], in_=ctx.pb[bg][:, :], func=AF.Silu),
                         reads=[ctx.r_pb[bg]], writes=[ctx.r_tmp[ti]])
                    P.op(V, lambda e, ti=ti, bu=bu, fi=fi, ts_=ts_: e.tensor_tensor(out=H[:, fi, ts_], in0=ctx.pb[bu][:, :], in1=ctx.tmp[ti][:], op=ALU.mult),
                         reads=[ctx.r_pb[bu], ctx.r_tmp[ti]], writes=[r_H[fi]])
        for db in range(0, KC, 2):
            wdt, r_wd = ctx.load_wD(wd[f0 * 128:f1 * 128, db * 128:(db + 2) * 128], nf, 256)
            for j in range(2):
                dc = db + j
                for tt in range(NTT):
                    ts_ = slice(tt * TT, (tt + 1) * TT)
                    b = ctx.nbank()
                    mm_group(P, ctx, b, [(wdt[:, fi, j * 128:(j + 1) * 128], H[:, fi, ts_]) for fi in range(nf)], [r_wd] + [r_H[fi] for fi in range(nf)])
                    P.op(V, lambda e, b=b, dc=dc, ts_=ts_: e.scalar_tensor_tensor(out=X[:, dc, ts_], in0=ctx.pb[b][:, :], scalar=0.5, in1=X[:, dc, ts_],
                                                                                 op0=ALU.mult, op1=ALU.add),
                         reads=[ctx.r_pb[b], r_X[dc]], writes=[r_X[dc]])


def load_x(P, X, r_X, XB, r_XB, xT):
    for kc in range(KC):
        P.dma("sync", X[:, kc, :], xT[kc * 128:(kc + 1) * 128, :], writes=[r_X[kc]])
        P.op(A if kc % 2 else V, (lambda e, kc=kc: e.activation(out=XB[:, kc, :], in_=X[:, kc, :], func=AF.Copy)) if kc % 2 else
             (lambda e, kc=kc: e.tensor_copy(out=XB[:, kc, :], in_=X[:, kc, :])), reads=[r_X[kc]], writes=[r_XB[kc]])


def build_A():
    nc = bass.Bass("TRN2", target_bir_lowering=False)
    xT = nc.dram_tensor("xT", [D, NT], F32, kind="ExternalInput").ap()
    wg = nc.dram_tensor("wg", [D, F], F32, kind="ExternalInput").ap()
    wu = nc.dram_tensor("wu", [D, F], F32, kind="ExternalInput").ap()
    wd = nc.dram_tensor("wd", [F, D], F32, kind="ExternalInput").ap()
    lngb = nc.dram_tensor("lngb", [128, 2, KC], F32, kind="ExternalInput").ap()
    w_in = nc.dram_tensor("w_in", [D, RC], F32, kind="ExternalInput").ap()
    x1T = nc.dram_tensor("x1T", [D, NT], F32, kind="ExternalOutput").ap()
    pT = nc.dram_tensor("pT", [RC, NT], F32, kind="ExternalOutput").ap()
    with ExitStack() as es:
        P = Prog(nc, es)
        ctx = DenseCtx(P)
        X = P.sb("X", [128, KC, NT]); r_X = [Res() for _ in range(KC)]
        XB = P.sb("XB", [128, KC, NT], BF16); r_XB = [Res() for _ in range(KC)]
        H = P.sb("H", [128, 11, NT], BF16); r_H = [Res() for _ in range(11)]
        gb = P.sb("gb", [128, 2, KC]); r_gb = Res()
        P.dma("sync", gb[:], lngb[:, :, :], writes=[r_gb])
        load_x(P, X, r_X, XB, r_XB, xT)
        ffn(P, ctx, X, r_X, XB, r_XB, H, r_H, wg, wu, wd)
        layer_norm_fm(P, ctx, X, r_X, XB, r_XB, gb, r_gb, 0, 1)
        for kc in range(KC):
            P.dma("sync", x1T[kc * 128:(kc + 1) * 128, :], X[:, kc, :], reads=[r_X[kc]], is_out=True)
        for mb in range(0, RC // 128, 2):
            nb = min(2, RC // 128 - mb)
            wt, r_w = ctx.load_wA(w_in[:, mb * 128:(mb + nb) * 128], KC, nb * 128)
            for j in range(nb):
                m = mb + j
                for tt in range(NTT):
                    ts_ = slice(tt * TT, (tt + 1) * TT)
                    b = ctx.nbank()
                    mm_group(P, ctx, b, [(wt[:, kc, j * 128:(j + 1) * 128], XB[:, kc, ts_]) for kc in range(KC)], [r_w] + list(r_XB))
                    ti = ctx.ntmp()
                    if (m + tt) % 2:
                        P.op(A, lambda e, ti=ti, b=b: e.activation(out=ctx.tmp[ti][:], in_=ctx.pb[b][:, :], func=AF.Copy), reads=[ctx.r_pb[b]], writes=[ctx.r_tmp[ti]])
                    else:
                        P.op(V, lambda e, ti=ti, b=b: e.tensor_copy(out=ctx.tmp[ti][:], in_=ctx.pb[b][:, :]), reads=[ctx.r_pb[b]], writes=[ctx.r_tmp[ti]])
                    P.dma("sync", pT[m * 128:(m + 1) * 128, ts_], ctx.tmp[ti][:], reads=[ctx.r_tmp[ti]], is_out=True)
        P.finish()
        P.emit()
    return nc


def build_C():
    nc = bass.Bass("TRN2", target_bir_lowering=False)
    x1T = nc.dram_tensor("x1T", [D, NT], F32, kind="ExternalInput").ap()
    yT = nc.dram_tensor("yT", [2, 1024, NT], F32, kind="ExternalInput").ap()
    memT = nc.dram_tensor("memT", [D, 256], F32, kind="ExternalInput").ap()
    w_q = nc.dram_tensor("w_q", [D, 1024], F32, kind="ExternalInput").ap()
    w_g = nc.dram_tensor("w_g", [D, 3 * D], F32, kind="ExternalInput").ap()
    w_kv = nc.dram_tensor("w_kv", [D, 2048], F32, kind="ExternalInput").ap()
    w_br = nc.dram_tensor("w_br", [3, 1024, D], F32, kind="ExternalInput").ap()
    w_o = nc.dram_tensor("w_o", [D, D], F32, kind="ExternalInput").ap()
    vecs = nc.dram_tensor("vecs", [128, 9, KC], F32, kind="ExternalInput").ap()
    wg = nc.dram_tensor("wg", [D, F], F32, kind="ExternalInput").ap()
    wu = nc.dram_tensor("wu", [D, F], F32, kind="ExternalInput").ap()
    wd = nc.dram_tensor("wd", [F, D], F32, kind="ExternalInput").ap()
    x2T = nc.dram_tensor("x2T", [D, NT], F32, kind="ExternalOutput").ap()
    with ExitStack() as es:
        P = Prog(nc, es)
        ctx = DenseCtx(P)
        ARENA = P.sb("ARENA", [128, KC * NT])
        ARENA2 = P.sb("ARENA2", [128, 8192])
        X = ARENA[:, :].rearrange("p (c t) -> p c t", t=NT); r_X = [Res() for _ in range(KC)]
        XB = P.sb("XB", [128, KC, NT], BF16); r_XB = [Res() for _ in range(KC)]
        Yall = ARENA[:, 0:12288].bitcast(BF16).rearrange("p (c t) -> p c t", t=NT); r_Y = [Res() for _ in range(24)]
        QB = ARENA[:, 12288:16384].bitcast(BF16).rearrange("p (c t) -> p c t", t=NT); r_QB = [Res() for _ in range(8)]
        A2B = ARENA2[:, :].bitcast(BF16)
        MIXB = A2B.rearrange("p (c t) -> p c t", t=NT); r_MIX = [Res() for _ in range(KC)]
        H = A2B[:, 0:11 * NT].rearrange("p (c t) -> p c t", t=NT); r_H = [Res() for _ in range(11)]
        MX = ARENA2[:, 0:4096].rearrange("p (c t) -> p c t", t=256); r_MX = [Res() for _ in range(KC)]
        MB = ARENA2[:, 4096:6144].bitcast(BF16).rearrange("p (c t) -> p c t", t=256); r_MB = [Res() for _ in range(KC)]
        KTB = ARENA2[:, 6144:7168].bitcast(BF16).rearrange("p (c t) -> p c t", t=256); r_KTB = Res()
        VB = ARENA2[:, 7168:8192].bitcast(BF16).rearrange("p (c t) -> p c t", t=1024); r_VB = Res()
        vc = P.sb("vc", [128, 9, KC]); r_vc = Res()
        P.dma("sync", vc[:], vecs[:, :, :], writes=[r_vc])
        for kc in range(KC):
            P.dma("gpsimd", XB[:, kc, :], x1T[kc * 128:(kc + 1) * 128, :], writes=[r_XB[kc]])
        for n in range(2):
            for c in range(8):
                P.dma("gpsimd", Yall[:, n * 8 + c, :], yT[n, c * 128:(c + 1) * 128, :], writes=[r_Y[n * 8 + c]])
        for kc in range(KC):
            P.dma("sync", MX[:, kc, :], memT[kc * 128:(kc + 1) * 128, :], writes=[r_MX[kc]])
        layer_norm_fm(P, ctx, MX, r_MX, MB, r_MB, vc, r_vc, 0, 1, ntok=256, write_x=False)
        for mb in range(0, 8, 2):
            wt, r_w = ctx.load_wA(w_kv[:, mb * 128:(mb + 2) * 128], KC, 256)
            for j in range(2):
                b = ctx.nbank()
                mm_group(P, ctx, b, [(wt[:, kc, j * 128:(j + 1) * 128], MB[:, kc, :]) for kc in range(KC)], [r_w] + list(r_MB), n=256)
                P.op(V, lambda e, b=b, m=mb + j: e.tensor_copy(out=KTB[:, m, :], in_=ctx.pb[b][:, 0:256]), reads=[ctx.r_pb[b]], writes=[r_KTB])
        for vb in range(4):
            wt, r_w = ctx.load_wA(w_kv[:, 1024 + vb * 256:1024 + (vb + 1) * 256], KC, 256)
            for mc in range(2):
                b = ctx.nbank()
                mm_group(P, ctx, b, [(MB[:, kc, mc * 128:(mc + 1) * 128], wt[:, kc, :]) for kc in range(KC)], [r_w] + list(r_MB), n=256)
                P.op(V, lambda e, b=b, mc=mc, vb=vb: e.tensor_copy(out=VB[:, mc, vb * 256:(vb + 1) * 256], in_=ctx.pb[b][:, 0:256]), reads=[ctx.r_pb[b]], writes=[r_VB])
        for mb in range(0, 8, 2):
            wt, r_w = ctx.load_wA(w_q[:, mb * 128:(mb + 2) * 128], KC, 256)
            for j in range(2):
                for tt in range(NTT):
                    ts_ = slice(tt * TT, (tt + 1) * TT)
                    b = ctx.nbank()
                    mm_group(P, ctx, b, [(wt[:, kc, j * 128:(j + 1) * 128], XB[:, kc, ts_]) for kc in range(KC)], [r_w] + list(r_XB))
                    P.op(A, lambda e, b=b, m=mb + j, ts_=ts_: e.activation(out=QB[:, m, ts_], in_=ctx.pb[b][:, :], func=AF.Copy), reads=[ctx.r_pb[b]], writes=[r_QB[mb + j]])
        EB = P.sb("EB", [128, 2, TT], BF16); r_EB = Res()
        rden = P.sb("rden", [128, TT]); r_rden = Res()
        for h in range(4):
            for tt in range(NTT):
                ts_ = slice(tt * TT, (tt + 1) * TT)
                for mc in range(2):
                    b = ctx.nbank()
                    mm_group(P, ctx, b, [(KTB[:, h * 2 + dc, mc * 128:(mc + 1) * 128], QB[:, h * 2 + dc, ts_]) for dc in range(2)], [r_KTB, r_QB[h * 2], r_QB[h * 2 + 1]])
                    P.op(A, lambda e, b=b, mc=mc: e.activation(out=EB[:, mc, :], in_=ctx.pb[b][:, :], func=AF.Exp, scale=1.0 / 16.0), reads=[ctx.r_pb[b]], writes=[r_EB])
                b = ctx.nbank()
                mm_group(P, ctx, b, [(ctx.onesb[:], EB[:, mc, :]) for mc in range(2)], [ctx.r_c, r_EB])
                P.op(V, lambda e, b=b: e.reciprocal(out=rden[:], in_=ctx.pb[b][:, :]), reads=[ctx.r_pb[b]], writes=[r_rden])
                for dc in range(2):
                    b = ctx.nbank()
                    mm_group(P, ctx, b, [(VB[:, mc, h * 256 + dc * 128:h * 256 + (dc + 1) * 128], EB[:, mc, :]) for mc in range(2)], [r_VB, r_EB])
                    P.op(V, lambda e, b=b, c=16 + h * 2 + dc, ts_=ts_: e.tensor_tensor(out=Yall[:, c, ts_], in0=ctx.pb[b][:, :], in1=rden[:], op=ALU.mult),
                         reads=[ctx.r_pb[b], r_rden], writes=[r_Y[16 + h * 2 + dc]])
        gate = P.sb("gate", [128, TT]); r_gate = Res()
        term = P.sb("term", [128, TT]); r_term = Res()
        ACC = P.sb("ACC", [128, 2, NT]); r_ACC = Res()
        for db in range(0, KC, 2):
            for n in range(3):
                wt, r_w = ctx.load_wA(w_g[:, n * D + db * 128:n * D + (db + 2) * 128], KC, 256)
                wbt, r_wb = ctx.load_wD(w_br[n, :, db * 128:(db + 2) * 128], 8, 256)
                for j in range(2):
                    dc = db + j
                    for tt in range(NTT):
                        ts_ = slice(tt * TT, (tt + 1) * TT)
                        bg = ctx.nbank()
                        mm_group(P, ctx, bg, [(wt[:, kc, j * 128:(j + 1) * 128], XB[:, kc, ts_]) for kc in range(KC)], [r_w] + list(r_XB))
                        P.op(A, lambda e, bg=bg, n=n, dc=dc: e.activation(out=gate[:], in_=ctx.pb[bg][:, :], func=AF.Sigmoid, bias=vc[:, 2 + n, dc:dc + 1], scale=1.0),
                             reads=[ctx.r_pb[bg], r_vc], writes=[r_gate])
                        bp_ = ctx.nbank()
                        mm_group(P, ctx, bp_, [(wbt[:, c, j * 128:(j + 1) * 128], Yall[:, n * 8 + c, ts_]) for c in range(8)], [r_wb] + [r_Y[n * 8 + c] for c in range(8)])
                        if n == 0:
                            P.op(V, lambda e, bp_=bp_, j=j, ts_=ts_: e.tensor_tensor(out=ACC[:, j, ts_], in0=ctx.pb[bp_][:, :], in1=gate[:], op=ALU.mult),
                                 reads=[ctx.r_pb[bp_], r_gate], writes=[r_ACC])
                        else:
                            P.op(V, lambda e, bp_=bp_: e.tensor_tensor(out=term[:], in0=ctx.pb[bp_][:, :], in1=gate[:], op=ALU.mult),
                                 reads=[ctx.r_pb[bp_], r_gate], writes=[r_term])
                            if n == 1:
                                P.op(G, lambda e, j=j, ts_=ts_: e.tensor_tensor(out=ACC[:, j, ts_], in0=ACC[:, j, ts_], in1=term[:], op=ALU.add), reads=[r_ACC, r_term], writes=[r_ACC])
                            else:
                                P.op(G, lambda e, dc=dc, j=j, ts_=ts_: e.tensor_tensor(out=MIXB[:, dc, ts_], in0=ACC[:, j, ts_], in1=term[:], op=ALU.add),
                                     reads=[r_ACC, r_term], writes=[r_MIX[dc]])
        xs = [P.sb(f"xs{i}", [128, TT]) for i in range(2)]; r_xs = [Res(), Res()]
        cnt = 0
        for db in range(0, KC, 2):
            wt, r_w = ctx.load_wA(w_o[:, db * 128:(db + 2) * 128], KC, 256)
            for j in range(2):
                dc = db + j
                for tt in range(NTT):
                    ts_ = slice(tt * TT, (tt + 1) * TT)
                    i = cnt % 2; cnt += 1
                    P.dma("sync", xs[i][:], x1T[dc * 128:(dc + 1) * 128, ts_], writes=[r_xs[i]])
                    P.op(G, lambda e, i=i: e.tensor_scalar(out=xs[i][:], in0=xs[i][:], scalar1=ALPHA, scalar2=None, op0=ALU.mult), reads=[r_xs[i]], writes=[r_xs[i]])
                    b = ctx.nbank()
                    mm_group(P, ctx, b, [(wt[:, kc, j * 128:(j + 1) * 128], MIXB[:, kc, ts_]) for kc in range(KC)], [r_w] + list(r_MIX))
                    P.op(V, lambda e, b=b, i=i, dc=dc, ts_=ts_: e.tensor_tensor(out=X[:, dc, ts_], in0=ctx.pb[b][:, :], in1=xs[i][:], op=ALU.add),
                         reads=[ctx.r_pb[b], r_xs[i]], writes=[r_X[dc]])
        layer_norm_fm(P, ctx, X, r_X, XB, r_XB, vc, r_vc, 5, 6)
        ffn(P, ctx, X, r_X, XB, r_XB, H, r_H, wg, wu, wd)
        layer_norm_fm(P, ctx, X, r_X, XB, r_XB, vc, r_vc, 7, 8)
        for kc in range(KC):
            P.dma("sync", x2T[kc * 128:(kc + 1) * 128, :], X[:, kc, :], reads=[r_X[kc]], is_out=True)
        P.finish()
        P.emit()
    return nc


_PROGS = {}


def _prog(name):
    if name not in _PROGS:
        _PROGS[name] = {"A": build_A, "C": build_C, "S": lambda: build_scan(4096, 2)}[name]()
    return _PROGS[name]


def _vec16(v):
    return np.ascontiguousarray(np.asarray(v, np.float32).reshape(16, 128).T)


def kernel(x, mem, ffn1_w_gate, ffn1_w_up, ffn1_w_down, ln1_g, ln1_b, w_in,
           rwkv_mu, rwkv_w0, rwkv_w_up, rwkv_a0, rwkv_a_up, rwkv_g_up, rwkv_k_k,
           rwkv_k_a, rwkv_r_k, rwkv_gn_g, rwkv_gn_b, conv_w, mem_ln_g, mem_ln_b,
           w_mem_kv, w_branch, gate_b, w_out, ln2_g, ln2_b, ffn2_w_gate, ffn2_w_up,
           ffn2_w_down, ln3_g, ln3_b):
    f32 = lambda a: np.ascontiguousarray(np.asarray(a, dtype=np.float32))
    x = f32(x); mem = f32(mem)
    NCORE = 8
    cores = list(range(NCORE))
    xT = [np.ascontiguousarray(x[c // 4, (c % 4) * 1024:(c % 4 + 1) * 1024].T) for c in cores]
    memT = [np.ascontiguousarray(mem[c // 4].T) for c in cores]
    cm, rmask = scan_consts()
    for l in range(2):
        w_in_l = f32(w_in[l])
        mA = {"wg": f32(ffn1_w_gate[l]), "wu": f32(ffn1_w_up[l]), "wd": f32(ffn1_w_down[l]),
              "lngb": np.ascontiguousarray(np.stack([_vec16(ln1_g[l]), _vec16(ln1_b[l])], 1)),
              "w_in": np.ascontiguousarray(w_in_l[:, :RC])}
        resA = run_bass_kernel_spmd(_prog("A"), [dict(mA, xT=xT[c]) for c in cores], core_ids=cores).results
        x1T = [np.asarray(resA[c]["x1T"]) for c in cores]
        pall = np.concatenate([np.asarray(resA[c]["pT"]) for c in cores], axis=1)
        del resA
        mu = f32(rwkv_mu[l]); w0 = f32(rwkv_w0[l]); a0 = f32(rwkv_a0[l])
        wup = f32(rwkv_w_up[l]); aup = f32(rwkv_a_up[l]); gup = f32(rwkv_g_up[l])
        kkv = f32(rwkv_k_k[l]); kav = f32(rwkv_k_a[l]); rkv = f32(rwkv_r_k[l]).reshape(-1)
        gng = f32(rwkv_gn_g[l]); gnb = f32(rwkv_gn_b[l]); cw = f32(conv_w[l])
        mS = []
        for c in cores:
            cs = slice(c * 128, (c + 1) * 128)
            zin = np.stack([pall[0:1024][cs], pall[1024:2048][cs], pall[2048:3072][cs], pall[3072:3200], pall[3200:3328], pall[3328:3456],
                            pall[3456:4480][cs], pall[4480:5504][cs], pall[5504:6528][cs]], 0)
            pv = np.stack([mu[0:1024][cs], mu[1024:2048][cs], mu[2048:3072][cs], mu[3072:3200], mu[3200:3328], mu[3328:3456],
                           w0[0][cs], w0[1][cs], a0[0][cs], a0[1][cs], kkv[cs], kav[cs], rkv[cs], gng[cs], gnb[cs],
                           cw[0][cs], cw[1][cs], cw[2][cs]], 1)
            pm = np.stack([wup[:, :, cs].reshape(128, 128), aup[:, :, cs].reshape(128, 128), gup[:, cs]], 1)
            mS.append({"zin": np.ascontiguousarray(zin), "pvec": np.ascontiguousarray(pv), "pmat": np.ascontiguousarray(pm), "cmat": cm, "rmask": rmask})
        del pall
        resS = run_bass_kernel_spmd(_prog("S"), mS, core_ids=cores).results
        yall = np.concatenate([np.asarray(resS[c]["yout"]) for c in cores], axis=1)
        del resS, mS
        vecs = np.ascontiguousarray(np.stack([_vec16(mem_ln_g[l]), _vec16(mem_ln_b[l]), _vec16(gate_b[l][0]), _vec16(gate_b[l][1]), _vec16(gate_b[l][2]),
                                              _vec16(ln2_g[l]), _vec16(ln2_b[l]), _vec16(ln3_g[l]), _vec16(ln3_b[l])], 1))
        mC = {"w_q": np.ascontiguousarray(w_in_l[:, 6528:7552]), "w_g": np.ascontiguousarray(w_in_l[:, 7552:]), "w_kv": f32(w_mem_kv[l]),
              "w_br": f32(w_branch[l]), "w_o": f32(w_out[l]), "vecs": vecs,
              "wg": f32(ffn2_w_gate[l]), "wu": f32(ffn2_w_up[l]), "wd": f32(ffn2_w_down[l])}
        resC = run_bass_kernel_spmd(_prog("C"), [dict(mC, x1T=x1T[c], yT=np.ascontiguousarray(yall[:, :, c * 1024:(c + 1) * 1024]), memT=memT[c]) for c in cores],
                                    core_ids=cores).results
        xT = [np.asarray(resC[c]["x2T"]) for c in cores]
        del resC, yall
    out = np.empty((2, 4096, 2048), np.float32)
    for c in cores:
        out[c // 4, (c % 4) * 1024:(c % 4 + 1) * 1024] = xT[c].T
    return out
```

```python
import numpy as np
from contextlib import ExitStack
import concourse.bass as bass
import concourse.mybir as mybir
from concourse.bass_utils import run_bass_kernel_spmd


F32 = mybir.dt.float32
BF16 = mybir.dt.bfloat16
AF = mybir.ActivationFunctionType
ALU = mybir.AluOpType
ND = 6


class Res:
    __slots__ = ("w", "r")

    def __init__(self):
        self.w = None
        self.r = []


def resgrid(*shape):
    a = np.empty(shape, dtype=object)
    for idx in np.ndindex(*shape):
        a[idx] = Res()
    return a


class _Rec:
    def __init__(self):
        self.calls = []

    def __getattr__(self, name):
        def f(*a, **k):
            self.calls.append((name, a, k))
            return None
        return f


def _replay(calls):
    def fn(e):
        ins = None
        for name, a, k in calls:
            ins = getattr(e, name)(*a, **k)
        return ins
    return fn


class Prog:
    ENGS = ("tensor", "vector", "scalar", "gpsimd", "sync")

    def __init__(self, nc, es):
        self.nc = nc
        self.es = es
        self.q = {e: [] for e in self.ENGS}
        self.sem = {}
        for e in ("tensor", "vector", "scalar", "gpsimd"):
            self.sem[("e", e)] = es.enter_context(nc.semaphore("s_" + e))
        self.ecnt = {e: 0 for e in self.ENGS}
        self.dcnt = {}
        self.dnext = {}
        for qn in ("sync", "gpsimd", "scalar"):
            self.dnext[qn] = 0
            for i in range(ND):
                self.sem[("d", qn, i)] = es.enter_context(nc.semaphore(f"d_{qn}{i}"))
                self.dcnt[(qn, i)] = 0
        self.seen = {e: {} for e in self.ENGS}
        self.out_stamps = []

    stop_stage = None

    def stage(self, n):
        if self.stop_stage is not None and n == self.stop_stage:
            raise StopIteration

    def sb(self, name, shape, dt=F32):
        return self.es.enter_context(self.nc.sbuf_tensor(name, list(shape), dt))

    def ps(self, name, shape, dt=F32):
        return self.es.enter_context(self.nc.psum_tensor(name, list(shape), dt))

    def _deps(self, eng, reads, writes, extra=()):
        deps = {}

        def add(st):
            if st is None:
                return
            k, v = st
            if deps.get(k, 0) < v:
                deps[k] = v

        for r in reads:
            add(r.w)
        for w in writes:
            add(w.w)
            for s in w.r:
                add(s)
        for s in extra:
            add(s)
        waits = []
        for k, v in deps.items():
            if eng == "tensor" and k == ("e", "tensor"):
                continue
            if self.seen[eng].get(k, 0) >= v:
                continue
            self.seen[eng][k] = v
            waits.append((k, v))
        return waits

    def _commit(self, st, reads, writes):
        for r in reads:
            r.r.append(st)
        for w in writes:
            w.w = st
            w.r = []

    def op(self, eng, fn, reads=(), writes=()):
        waits = self._deps(eng, reads, writes)
        self.ecnt[eng] += 1
        st = (("e", eng), self.ecnt[eng])
        rec = _Rec()
        fn(rec)
        assert rec.calls
        self.q[eng].append((waits, _replay(rec.calls), st))
        self._commit(st, reads, writes)
        return st

    def dma(self, qn, out, in_, reads=(), writes=(), is_out=False):
        i = self.dnext[qn]
        self.dnext[qn] = (i + 1) % ND
        key = ("d", qn, i)
        prev = self.dcnt[(qn, i)]
        extra = [(key, prev)] if prev > 0 else []
        waits = self._deps(qn, reads, writes, extra)
        self.dcnt[(qn, i)] = prev + 16
        st = (key, prev + 16)
        self.q[qn].append((waits, (lambda e, o=out, i_=in_: e.dma_start(out=o, in_=i_)), st))
        self._commit(st, reads, writes)
        if is_out:
            self.out_stamps.append(st)
        return st

    def finish(self):
        final = {}
        for qn in ("sync", "gpsimd", "scalar"):
            for i in range(ND):
                v = self.dcnt[(qn, i)]
                if v > 0:
                    final[("d", qn, i)] = v
        self.q["sync"].append((list(final.items()), None, None))

    def emit(self):
        nc = self.nc
        with nc.Block() as block:
            def mk(name):
                def body(e):
                    for waits, fn, st in self.q[name]:
                        for k, v in waits:
                            e.wait_ge(self.sem[k], v)
                        if fn is None:
                            continue
                        ins = fn(e)
                        if st is not None:
                            ins.then_inc(self.sem[st[0]], 16 if st[0][0] == "d" else 1)
                return body
            block.tensor(mk("tensor"))
            block.vector(mk("vector"))
            block.scalar(mk("scalar"))
            block.gpsimd(mk("gpsimd"))
            block.sync(mk("sync"))


C = 64
NB = 8
BT = NB * C
NIT = NB * 2
DEC = 0.606531
GN_EPS = 64e-5
(MU_R, MU_K, MU_V, MU_HW, MU_HA, MU_HG, W0F, W0B, A0F, A0B, KK_, KA_, RK_, GNG, GNB, CW0, CW1, CW2) = range(18)
NPV = 18


def scan_consts():
    idx = np.arange(C)
    cm = np.zeros((128, 7, 128), np.float32)
    cm[:, 0, :] = np.eye(128)
    bd = np.zeros((128, 128), np.float32)
    bd[:64, :64] = 1
    bd[64:, 64:] = 1
    cm[:, 1, :] = bd
    cm[:, 2, :] = bd / 64.0
    for d in range(2):
        if d == 0:
            strict = (idx[:, None] < idx[None, :]).astype(np.float32)
            incl = (idx[:, None] <= idx[None, :]).astype(np.float32)
        else:
            strict = (idx[:, None] > idx[None, :]).astype(np.float32)
            incl = (idx[:, None] >= idx[None, :]).astype(np.float32)
        m1 = np.zeros((128, 128), np.float32)
        m1[:64, :64] = -strict
        m1[:64, 64:] = incl
        m1[64:, 64:] = incl
        cm[:, 3 + 2 * d, :] = m1
        m2 = np.zeros((128, 128), np.float32)
        m2[:64, :64] = -strict.T
        m2[:64, 64:] = strict.T
        cm[:, 4 + 2 * d, :] = m2
    rmask = np.ones((128, BT), np.float32)
    rmask[:, ::C] = 0
    return cm, rmask


STOP = None


def build_scan(T, NBATCH):
    nc = bass.Bass("TRN2", target_bir_lowering=False)
    NTOK = T * NBATCH
    zin = nc.dram_tensor("zin", [9, 128, NTOK], F32, kind="ExternalInput").ap()
    pvec = nc.dram_tensor("pvec", [128, NPV], F32, kind="ExternalInput").ap()
    pmat = nc.dram_tensor("pmat", [128, 3, 128], F32, kind="ExternalInput").ap()
    cmat = nc.dram_tensor("cmat", [128, 7, 128], F32, kind="ExternalInput").ap()
    rmk = nc.dram_tensor("rmask", [128, BT], F32, kind="ExternalInput").ap()
    yout = nc.dram_tensor("yout", [2, 128, NTOK], F32, kind="ExternalOutput").ap()
    with ExitStack() as es:
        P = Prog(nc, es)
        try:
            emit_scan(P, zin, pvec, pmat, cmat, rmk, yout, T, NBATCH)
        except StopIteration:
            pass
        P.finish()
        P.emit()
    return nc


def emit_scan(P, zin, pvec, pmat, cmat, rmk, yout, T, NBATCH):
    nblk = T // BT
    sb, ps = P.sb, P.ps
    V, G, A, PE = "vector", "gpsimd", "scalar", "tensor"
    pv = sb("pv", [128, NPV + 8]); r_pv = Res()
    pm = sb("pm", [128, 3, 128]); r_pm = Res()
    cm = sb("cm", [128, 7, 128]); r_cm = Res()
    rm_t = sb("rmaskt", [128, BT]); r_rm = Res()
    P.dma("sync", pv[:, 0:NPV], pvec[:, :], writes=[r_pv])
    P.dma("sync", pm[:], pmat[:, :, :], writes=[r_pm])
    P.dma("sync", cm[:], cmat[:, :, :], writes=[r_cm])
    P.dma("sync", rm_t[:], rmk[:, :], writes=[r_rm])
    ident = cm[:, 0, :]
    bdones = cm[:, 1, :]
    bdavg = cm[:, 2, :]
    hm = sb("hm", [128, 8]); r_hm = Res()
    P.op(V, lambda e: e.tensor_scalar(out=pv[:, NPV:NPV + 6], in0=pv[:, 0:6], scalar1=-1.0, scalar2=1.0,
                                      op0=ALU.mult, op1=ALU.add), reads=[r_pv], writes=[r_hm])
    P.op(V, lambda e: e.tensor_scalar(out=hm[:, 0:6], in0=pv[:, 0:6], scalar1=0.5, scalar2=None, op0=ALU.mult),
         reads=[r_pv], writes=[r_hm])
    P.op(V, lambda e: e.tensor_scalar(out=hm[:, 7:8], in0=pv[:, KA_:KA_ + 1], scalar1=0.5, scalar2=None, op0=ALU.mult),
         reads=[r_pv], writes=[r_hm])
    P.op(V, lambda e: e.tensor_scalar(out=hm[:, 6:7], in0=pv[:, KA_:KA_ + 1], scalar1=-1.0, scalar2=1.0,
                                      op0=ALU.mult, op1=ALU.add), reads=[r_pv], writes=[r_hm])
    epsv = sb("epsv", [128, 2])
    P.op(V, lambda e: e.memset(epsv[:, 0:1], 1e-12), writes=[r_hm])
    P.op(V, lambda e: e.memset(epsv[:, 1:2], GN_EPS), writes=[r_hm])

    def col(j):
        return pv[:, j:j + 1]

    NRAW = 9
    raw = [sb(f"raw{i}", [128, BT + 2]) for i in range(NRAW)]
    r_raw = [Res() for _ in range(NRAW)]
    sh = [sb(f"sh{i}", [128, BT]) for i in range(6)]
    r_sh = [Res() for _ in range(6)]
    tmpA = sb("tmpA", [128, BT]); r_tmpA = Res()
    tmpB = sb("tmpB", [128, BT]); r_tmpB = Res()
    tmpA2 = sb("tmpA2", [128, BT]); r_tmpA2 = Res()
    tmpB2 = sb("tmpB2", [128, BT]); r_tmpB2 = Res()
    th = sb("th", [128, BT]); r_th = Res()
    sig = sb("sig", [128, BT]); r_sig = Res()
    aa = sb("aa", [128, BT]); r_aa = Res()
    af = sb("af", [128, BT]); r_af = Res()
    cumS = sb("cumS", [128, BT]); r_cum = Res()
    cp = sb("cp", [128, BT]); r_cp = Res()
    rmm = sb("rmm", [128, BT]); r_rmm = Res()
    cb = sb("cb", [128, BT]); r_cb = Res()
    eW = sb("eW", [128, BT]); r_eW = Res()
    eWp = sb("eWp", [128, BT]); r_eWp = Res()
    eWi = sb("eWi", [128, BT]); r_eWi = Res()
    eD = sb("eD", [128, BT]); r_eD = Res()
    WCt = sb("WCt", [128, NB]); r_WCt = Res()
    WCs = sb("WCs", [64, NB, 2]); r_WCs = Res()
    kkr = sb("kkr", [128, BT]); r_kkr = Res()
    sq = sb("sq", [128, BT]); r_sq = Res()
    rn = sb("rn", [128, BT]); r_rn = Res()
    kk = sb("kk", [128, BT]); r_kk = Res()
    t1 = sb("t1", [128, BT]); r_t1 = Res()
    kd = sb("kd", [128, BT]); r_kd = Res()
    bb = sb("bb", [128, BT]); r_bb = Res()
    LT = sb("LT", [128, NB, 2, C]); r_LT = Res()
    RT = sb("RT", [128, NB, 2, C]); r_RT = Res()
    bp = sb("bp", [128, BT]); r_bp = Res()
    ktp = sb("ktp", [128, BT]); r_ktp = Res()
    KLin = sb("KLin", [64, NIT, 128], BF16); r_KLa = Res(); r_KLb = [Res() for _ in range(4)]
    RB = sb("RB", [64, NIT, 128]); r_RBa = Res(); r_RBb = [Res() for _ in range(4)]
    Bs = sb("Bs", [128, NIT, 128]); r_Bs_tl = Res(); r_Bs_tr = Res(); r_Bs_bl = Res(); r_Bs_br = [Res() for _ in range(4)]
    PPa = sb("PPa", [64, NIT, 192], BF16); r_PPa = [Res() for _ in range(8)]
    PPb = sb("PPb", [64, NIT, 192], BF16); r_PPb = [Res() for _ in range(8)]
    TTl = [sb("TTa", [64, NIT, 64], BF16), sb("TTb", [64, NIT, 64], BF16)]; TTh = TTl; r_TTl = [[Res() for _ in range(8)], [Res() for _ in range(8)]]
    KL = sb("KL", [64, NIT, 128]); r_KL = [Res() for _ in range(4)]
    ABQH = sb("ABQH", [128, NIT, 128]); r_AB = [Res() for _ in range(4)]
    Z = sb("Z", [128, NB, 2, 64]); r_Zv = Res(); r_Zs = [Res() for _ in range(NB)]
    STc = sb("STc", [64, 2, 64]); r_ST = Res()
    yf = sb("yf", [128, T]); r_yf = [Res() for _ in range(nblk)]
    cu = sb("cu", [128, BT + 2]); r_cu = Res()
    ysum = eW; r_ysum = r_eW
    ysq = eWp; r_ysq = r_eWp
    m2 = eWi; r_m2 = r_eWi
    yn = eD; r_yn = r_eD
    sg = kkr; r_sg = r_kkr
    rkb = rn; r_rkb = r_rn
    yo = bb; r_yo = r_bb
    hc = bp; r_hc = r_bp
    yc = ktp; r_yc = r_ktp
    pb = [ps(f"pb{i}", [128, 512]) for i in range(8)]
    r_pb = [Res() for _ in range(8)]

    def mm(e, out, lhsT, rhs, start=True, stop=True):
        rp = lhsT.base_partition(); cp_ = out.base_partition()
        if rp or cp_:
            return e.matmul(out, lhsT, rhs, start=start, stop=stop, tile_position=(rp, cp_))
        return e.matmul(out, lhsT, rhs, start=start, stop=stop)

    for b in range(NBATCH):
        for d in range(2):
            P.op(V, lambda e: e.memset(STc[:], 0.0), writes=[r_ST])
            blks = range(nblk) if d == 0 else range(nblk - 1, -1, -1)
            for blk in blks:
                t0 = blk * BT
                g0 = b * T + t0
                narr = 5 if d == 0 else 9
                arrs = [0, 1, 2, 3, 4] if d == 0 else list(range(9))
                for i in arrs:
                    lo = 1 if blk == 0 else 0
                    hi = BT + 1 if blk == nblk - 1 else BT + 2
                    if blk == 0:
                        P.op(G, lambda e, i=i: e.memset(raw[i][:, 0:1], 0.0), writes=[r_raw[i]])
                    if blk == nblk - 1:
                        P.op(G, lambda e, i=i: e.memset(raw[i][:, BT + 1:BT + 2], 0.0), writes=[r_raw[i]])
                    P.dma("sync", raw[i][:, lo:hi], zin[i, :, g0 - 1 + lo:g0 - 1 + hi], writes=[r_raw[i]])
                shl = [0, 1, 2, 3, 4] if d == 0 else [0, 1, 2, 3, 4, 5]
                for n, i in enumerate(shl):
                    e1 = G if n % 2 == 0 else V
                    tA, rA, tB, rB = (tmpA, r_tmpA, tmpB, r_tmpB) if n % 2 == 0 else (tmpA2, r_tmpA2, tmpB2, r_tmpB2)
                    P.op(e1, lambda e, i=i, tA=tA: e.tensor_tensor(out=tA[:], in0=raw[i][:, 0:BT], in1=raw[i][:, 2:BT + 2],
                                                                   op=ALU.add), reads=[r_raw[i]], writes=[rA])
                    P.op(A, lambda e, i=i, tA=tA, tB=tB: e.activation(out=tB[:], in_=tA[:], func=AF.Copy, scale=hm[:, i:i + 1]),
                         reads=[rA, r_hm], writes=[rB])
                    P.op(V, lambda e, i=i, tB=tB: e.scalar_tensor_tensor(out=sh[i][:], in0=raw[i][:, 1:BT + 1],
                                                                         scalar=pv[:, NPV + i:NPV + i + 1], in1=tB[:],
                                                                         op0=ALU.mult, op1=ALU.add),
                         reads=[r_raw[i], rB, r_hm], writes=[r_sh[i]])
                P.stage(1)
                rs, ks, vs, hws, has, hgs = sh
                ds = slice(64 * d, 64 * d + 64)
                P.op(A, lambda e: e.activation(out=th[:], in_=hws[:], func=AF.Tanh), reads=[r_sh[3]], writes=[r_th])
                P.op(PE, lambda e: mm(e, pb[0][:, :], pm[ds, 0, :], th[ds, :]), reads=[r_pm, r_th], writes=[r_pb[0]])
                P.op(PE, lambda e: mm(e, pb[1][:, :], pm[ds, 1, :], has[ds, :]), reads=[r_pm, r_sh[4]], writes=[r_pb[1]])
                P.op(A, lambda e: e.activation(out=sig[:], in_=pb[0][:, :], func=AF.Sigmoid, bias=col(W0F + d), scale=1.0),
                     reads=[r_pb[0], r_pv], writes=[r_sig])
                P.op(A, lambda e: e.activation(out=aa[:], in_=pb[1][:, :], func=AF.Sigmoid, bias=col(A0F + d), scale=1.0),
                     reads=[r_pb[1], r_pv], writes=[r_aa])
                P.stage(2)
                P.op(V, lambda e: e.tensor_tensor_scan(out=cumS[:], data0=rm_t[:], data1=sig[:], initial=0.0,
                                                       op0=ALU.mult, op1=ALU.add), reads=[r_rm, r_sig], writes=[r_cum])
                P.op(V, lambda e: e.tensor_tensor(out=cp[:], in0=cumS[:], in1=sig[:], op=ALU.subtract),
                     reads=[r_cum, r_sig], writes=[r_cp])
                cum3 = cumS[:].rearrange("p (c t) -> p c t", t=C)
                tot_b = cum3[:, :, C - 1:C].to_broadcast([128, NB, C])
                P.op(V, lambda e: e.tensor_tensor(out=rmm[:].rearrange("p (c t) -> p c t", t=C), in0=tot_b, in1=cum3,
                                                  op=ALU.subtract), reads=[r_cum], writes=[r_rmm])
                if d == 0:
                    c_cum, c_prev, c_rem = cumS, cp, rmm
                    rr_cum, rr_prev, rr_rem = r_cum, r_cp, r_rmm
                else:
                    P.op(V, lambda e: e.tensor_tensor(out=cb[:], in0=rmm[:], in1=sig[:], op=ALU.add),
                         reads=[r_rmm, r_sig], writes=[r_cb])
                    c_cum, c_prev, c_rem = cb, rmm, cp
                    rr_cum, rr_prev, rr_rem = r_cb, r_rmm, r_cp
                P.op(A, lambda e: e.activation(out=eW[:], in_=c_cum[:], func=AF.Exp, scale=-DEC), reads=[rr_cum], writes=[r_eW])
                P.op(A, lambda e: e.activation(out=eWp[:], in_=c_prev[:], func=AF.Exp, scale=-DEC), reads=[rr_prev], writes=[r_eWp])
                P.op(A, lambda e: e.activation(out=eWi[:], in_=c_cum[:], func=AF.Exp, scale=DEC), reads=[rr_cum], writes=[r_eWi])
                P.op(A, lambda e: e.activation(out=eD[:], in_=c_rem[:], func=AF.Exp, scale=-DEC), reads=[rr_rem], writes=[r_eD])
                P.op(A, lambda e: e.activation(out=WCt[:, :], in_=cum3[:, :, C - 1], func=AF.Exp, scale=-DEC),
                     reads=[r_cum], writes=[r_WCt])
                P.op(V, lambda e: e.tensor_copy(out=WCs[:, :, 0], in_=WCt[0:64, :]), reads=[r_WCt], writes=[r_WCs])
                P.op(V, lambda e: e.tensor_copy(out=WCs[:, :, 1], in_=WCt[64:128, :]), reads=[r_WCt], writes=[r_WCs])
                P.stage(3)
                P.op(V, lambda e: e.tensor_scalar(out=kkr[:], in0=ks[:], scalar1=col(KK_), scalar2=None, op0=ALU.mult),
                     reads=[r_sh[1], r_pv], writes=[r_kkr])
                P.op(G, lambda e: e.tensor_tensor(out=sq[:], in0=kkr[:], in1=kkr[:], op=ALU.mult), reads=[r_kkr], writes=[r_sq])
                P.op(PE, lambda e: mm(e, pb[2][:, :], bdones, sq[:]), reads=[r_cm, r_sq], writes=[r_pb[2]])
                P.op(A, lambda e: e.activation(out=rn[:], in_=pb[2][:, :], func=AF.Sqrt, bias=epsv[:, 0:1], scale=1.0),
                     reads=[r_pb[2], r_hm], writes=[r_rn])
                P.op(V, lambda e: e.reciprocal(out=rn[:], in_=rn[:]), reads=[r_rn], writes=[r_rn])
                P.op(G, lambda e: e.tensor_tensor(out=kk[:], in0=kkr[:], in1=rn[:], op=ALU.mult), reads=[r_kkr, r_rn], writes=[r_kk])
                P.op(V, lambda e: e.tensor_scalar(out=t1[:], in0=aa[:], scalar1=col(KA_), scalar2=hm[:, 6:7],
                                                  op0=ALU.mult, op1=ALU.add), reads=[r_aa, r_pv, r_hm], writes=[r_t1])
                P.op(G, lambda e: e.tensor_tensor(out=kd[:], in0=ks[:], in1=t1[:], op=ALU.mult), reads=[r_sh[1], r_t1], writes=[r_kd])
                P.op(G, lambda e: e.tensor_tensor(out=bb[:], in0=kk[:], in1=aa[:], op=ALU.mult), reads=[r_kk, r_aa], writes=[r_bb])
                P.stage(4)
                P.op(V, lambda e: e.tensor_tensor(out=LT[:, :, 0, :], in0=bb[:].rearrange("p (c t) -> p c t", t=C), in1=eWi[:].rearrange("p (c t) -> p c t", t=C), op=ALU.mult), reads=[r_bb, r_eWi], writes=[r_LT])
                P.op(G, lambda e: e.tensor_tensor(out=LT[:, :, 1, :], in0=kd[:].rearrange("p (c t) -> p c t", t=C), in1=eWi[:].rearrange("p (c t) -> p c t", t=C), op=ALU.mult), reads=[r_kd, r_eWi], writes=[r_LT])
                P.op(V, lambda e: e.tensor_tensor(out=RT[:, :, 0, :], in0=kk[:].rearrange("p (c t) -> p c t", t=C), in1=eWp[:].rearrange("p (c t) -> p c t", t=C), op=ALU.mult), reads=[r_kk, r_eWp], writes=[r_RT])
                P.op(G, lambda e: e.tensor_tensor(out=RT[:, :, 1, :], in0=rs[:].rearrange("p (c t) -> p c t", t=C), in1=eW[:].rearrange("p (c t) -> p c t", t=C), op=ALU.mult), reads=[r_sh[0], r_eW], writes=[r_RT])
                P.op(V, lambda e: e.tensor_tensor(out=bp[:], in0=bb[:], in1=eD[:], op=ALU.mult), reads=[r_bb, r_eD], writes=[r_bp])
                P.op(G, lambda e: e.tensor_tensor(out=ktp[:], in0=kd[:], in1=eD[:], op=ALU.mult), reads=[r_kd, r_eD], writes=[r_ktp])
                P.stage(5)
                side = []

                def transp(src_ap_fn, rsrc, bank0, evac):
                    def f(e):
                        ins = None
                        for c in range(NB):
                            o = pb[bank0 + c // 4][0:64, (c % 4) * 128:(c % 4 + 1) * 128]
                            ins = e.transpose(o, src_ap_fn(c), ident)
                        return ins
                    def thunk():
                        P.op(PE, f, reads=[rsrc, r_cm], writes=[r_pb[bank0], r_pb[bank0 + 1]])
                        evac()
                    side.append(thunk)

                def ps2(bank0):
                    return [pb[bank0 + j][0:64, :].rearrange("p (c n) -> p c n", n=128) for j in range(2)]

                def ev_kap():
                    for j in range(2):
                        src = ps2(4)[j].rearrange("p c (h k) -> p c h k", h=2)
                        dst = KLin[:, j * 8:(j + 1) * 8, 0:64].rearrange("p (c h) k -> p c h k", h=2)
                        P.op(V if j == 0 else A,
                             (lambda e, s=src, d_=dst: e.tensor_copy(out=d_, in_=s)) if j == 0 else
                             (lambda e, s=src, d_=dst: e.activation(out=d_, in_=s, func=AF.Copy)),
                             reads=[r_pb[4 + j]], writes=[r_KLa])
                transp(lambda c: RT[:, c, 0, :], r_RT, 4, ev_kap)

                def ev_bp():
                    for j in range(2):
                        src = ps2(6)[j].rearrange("p c (h k) -> p c h k", h=2)
                        dst = RB[:, j * 8:(j + 1) * 8, 0:64].rearrange("p (c h) k -> p c h k", h=2)
                        P.op(V if j == 0 else A,
                             (lambda e, s=src, d_=dst: e.tensor_copy(out=d_, in_=s)) if j == 0 else
                             (lambda e, s=src, d_=dst: e.activation(out=d_, in_=s, func=AF.Copy)),
                             reads=[r_pb[6 + j]], writes=[r_RBa])
                transp(lambda c: bp[:, c * C:(c + 1) * C], r_bp, 6, ev_bp)

                def ev_ktp():
                    for j in range(2):
                        src = ps2(4)[j].rearrange("p c (h k) -> p c h k", h=2)
                        dst = Bs[64:128, j * 8:(j + 1) * 8, 0:64].rearrange("p (c h) k -> p c h k", h=2)
                        P.op(V if j == 0 else A,
                             (lambda e, s=src, d_=dst: e.tensor_copy(out=d_, in_=s)) if j == 0 else
                             (lambda e, s=src, d_=dst: e.activation(out=d_, in_=s, func=AF.Copy)),
                             reads=[r_pb[4 + j]], writes=[r_Bs_bl])
                transp(lambda c: ktp[:, c * C:(c + 1) * C], r_ktp, 4, ev_ktp)

                def ev_v():
                    for j in range(2):
                        src = ps2(6)[j].rearrange("p c (h k) -> p c h k", h=2)
                        dst = Z[64:128, j * 4:(j + 1) * 4, :, :]
                        P.op(V if j == 0 else A,
                             (lambda e, s=src, d_=dst: e.tensor_copy(out=d_, in_=s)) if j == 0 else
                             (lambda e, s=src, d_=dst: e.activation(out=d_, in_=s, func=AF.Copy)),
                             reads=[r_pb[6 + j]], writes=[r_Zv])
                transp(lambda c: vs[:, c * C:(c + 1) * C], r_sh[2], 6, ev_v)
                P.stage(6)
                idb = ident[0:64, 0:64].unsqueeze(1).to_broadcast([64, NIT, 64])
                wcb = WCs[:].rearrange("p c h -> p (c h)").unsqueeze(2).to_broadcast([64, NIT, 64])
                def bs_thunk():
                    P.op(G, lambda e: e.tensor_tensor(out=Bs[0:64, :, 0:64], in0=idb, in1=wcb, op=ALU.mult),
                         reads=[r_cm, r_WCs], writes=[r_Bs_tl])
                    for h in range(2):
                        dst = Bs[0:64, :, 64:128].rearrange("p (c h) t -> p c h t", h=2)[:, :, h, :]
                        src = RT[64 * h:64 * h + 64, :, 1, :]
                        P.op(G, lambda e, s=src, d_=dst: e.tensor_copy(out=d_, in_=s), reads=[r_RT], writes=[r_Bs_tr])
                side.append(bs_thunk)
                P.stage(7)
                m1 = cm[:, 3 + 2 * d, :]
                m2_ = cm[0:64, 4 + 2 * d, :]
                for grp in range(NIT // 4):
                    bk1 = grp % 2
                    bk2 = 2 + grp % 2

                    def fg(e, grp=grp, bk1=bk1, bk2=bk2):
                        ins = None
                        for q in range(4):
                            it = grp * 4 + q
                            c, h = it // 2, it % 2
                            hs = slice(64 * h, 64 * h + 64)
                            cs = slice(c * C, (c + 1) * C)
                            mm(e, pb[bk1][:, q * 128:(q + 1) * 128], LT[hs, c].rearrange("p a t -> p (a t)"), RT[hs, c].rearrange("p a t -> p (a t)"))
                            ins = mm(e, pb[bk2][0:64, q * 128:(q + 1) * 128], RT[hs, c, 0, :], LT[hs, c].rearrange("p a t -> p (a t)"))
                        return ins
                    P.op(PE, fg, reads=[r_LT, r_RT], writes=[r_pb[bk1], r_pb[bk2]])
                    its = slice(grp * 4, grp * 4 + 4)
                    g1 = pb[bk1][:, :].rearrange("p (q n) -> p q n", n=128)
                    g2 = pb[bk2][0:64, :].rearrange("p (q n) -> p q n", n=128)
                    mTL = m1[0:64, 0:64].unsqueeze(1).to_broadcast([64, 4, 64])
                    mTR = m1[0:64, 64:128].unsqueeze(1).to_broadcast([64, 4, 64])
                    mBR = m1[64:128, 64:128].unsqueeze(1).to_broadcast([64, 4, 64])
                    m2L = m2_[:, 0:64].unsqueeze(1).to_broadcast([64, 4, 64])
                    m2R = m2_[:, 64:128].unsqueeze(1).to_broadcast([64, 4, 64])
                    P.op(V, lambda e, its=its, g1=g1, mTL=mTL: e.tensor_tensor(out=PPa[:, its, 0:64], in0=g1[0:64, :, 0:64], in1=mTL, op=ALU.mult),
                         reads=[r_pb[bk1], r_cm], writes=[r_PPa[2 * grp], r_PPa[2 * grp + 1]])
                    P.op(V, lambda e, its=its, g1=g1, mTR=mTR: e.tensor_tensor(out=RB[:, its, 64:128], in0=g1[0:64, :, 64:128], in1=mTR, op=ALU.mult),
                         reads=[r_pb[bk1], r_cm], writes=[r_RBb[grp]])
                    P.op(V, lambda e, its=its, g1=g1, mBR=mBR: e.tensor_tensor(out=Bs[64:128, its, 64:128], in0=g1[64:128, :, 64:128], in1=mBR, op=ALU.mult),
                         reads=[r_pb[bk1], r_cm], writes=[r_Bs_br[grp]])
                    P.op(V, lambda e, its=its, g2=g2, m2L=m2L: e.tensor_tensor(out=PPa[:, its, 64:128], in0=g2[:, :, 0:64], in1=m2L, op=ALU.mult),
                         reads=[r_pb[bk2], r_cm], writes=[r_PPa[2 * grp], r_PPa[2 * grp + 1]])
                    P.op(V, lambda e, its=its, g2=g2, m2R=m2R: e.tensor_tensor(out=KLin[:, its, 64:128], in0=g2[:, :, 64:128], in1=m2R, op=ALU.mult),
                         reads=[r_pb[bk2], r_cm], writes=[r_KLb[grp]])
                P.stage(8)
                idb16 = ident[0:64, 0:64].unsqueeze(1).to_broadcast([64, NIT, 64])
                P.op(V, lambda e: e.tensor_tensor(out=TTl[1][:], in0=PPa[:, :, 0:64], in1=idb16, op=ALU.add),
                     reads=list(r_PPa) + [r_cm], writes=list(r_TTl[1]))
                cur, r_cur, nxt, r_nxt = PPa, r_PPa, PPb, r_PPb
                for s in range(1, 7):
                    for grp in range(NIT // 2):
                        bk = grp % 4
                        its = slice(grp * 2, grp * 2 + 2)

                        def fi(e, s=s, grp=grp, bk=bk, cur=cur):
                            ins = None
                            for q in range(2):
                                it = grp * 2 + q
                                o = pb[bk][0:64, q * 192:(q + 1) * 192]
                                Pm = cur[:, it, 0:64]
                                PmT = cur[:, it, 64:128]
                                if s <= 5:
                                    ins = mm(e, o[:, 0:64], PmT, Pm)
                                    ins = mm(e, o[:, 64:128], Pm, PmT)
                                if s >= 2:
                                    ins = mm(e, o[:, 128:192], PmT, TTh[(s - 1) % 2][:, it, :])
                            return ins
                        rds = [r_cur[grp]] + ([r_TTl[(s - 1) % 2][grp]] if s >= 2 else [])
                        P.op(PE, fi, reads=rds, writes=[r_pb[bk]])
                        o3 = pb[bk][0:64, 0:384].rearrange("p (q n) -> p q n", n=192)
                        lo_c = 0 if s <= 5 else 128
                        hi_c = 192 if s >= 2 else 128
                        if grp % 2:
                            P.op(A, lambda e, its=its, o3=o3, nxt=nxt, lo_c=lo_c, hi_c=hi_c: e.activation(out=nxt[:, its, lo_c:hi_c], in_=o3[:, :, lo_c:hi_c], func=AF.Copy),
                                 reads=[r_pb[bk]], writes=[r_nxt[grp]])
                        else:
                            P.op(V, lambda e, its=its, o3=o3, nxt=nxt, lo_c=lo_c, hi_c=hi_c: e.tensor_copy(out=nxt[:, its, lo_c:hi_c], in_=o3[:, :, lo_c:hi_c]),
                                 reads=[r_pb[bk]], writes=[r_nxt[grp]])
                        if s >= 2:
                            P.op(G if grp % 2 == 0 else V, lambda e, its=its, nxt=nxt, s=s: e.tensor_tensor(out=TTl[s % 2][:, its, :], in0=nxt[:, its, 128:192], in1=TTl[(s - 1) % 2][:, its, :], op=ALU.add),
                                 reads=[r_nxt[grp], r_TTl[(s - 1) % 2][grp]], writes=[r_TTl[s % 2][grp]])
                        if side and grp % 2 == 1:
                            side.pop(0)()
                    cur, r_cur, nxt, r_nxt = nxt, r_nxt, cur, r_cur
                while side:
                    side.pop(0)()
                P.stage(9)
                for grp in range(NIT // 4):
                    bk = 6 + grp % 2

                    def f7(e, grp=grp, bk=bk):
                        ins = None
                        for q in range(4):
                            it = grp * 4 + q
                            ins = mm(e, pb[bk][0:64, q * 128:(q + 1) * 128], TTl[0][:, it, :], KLin[:, it, :])
                        return ins
                    P.op(PE, f7, reads=[r_TTl[0][2 * grp], r_TTl[0][2 * grp + 1], r_KLa, r_KLb[grp]], writes=[r_pb[bk]])
                    its = slice(grp * 4, grp * 4 + 4)
                    src = pb[bk][0:64, :].rearrange("p (q n) -> p q n", n=128)
                    P.op(A, lambda e, its=its, src=src: e.activation(out=KL[:, its, :], in_=src, func=AF.Copy),
                         reads=[r_pb[bk]], writes=[r_KL[grp]])
                for grp in range(NIT // 4):
                    bk = grp % 2

                    def f8(e, grp=grp, bk=bk):
                        ins = None
                        for q in range(4):
                            it = grp * 4 + q
                            ins = mm(e, pb[bk][:, q * 128:(q + 1) * 128], KL[:, it, :], RB[:, it, :])
                        return ins
                    P.op(PE, f8, reads=[r_KL[grp], r_RBa, r_RBb[grp]], writes=[r_pb[bk]])
                    its = slice(grp * 4, grp * 4 + 4)
                    src = pb[bk][:, :].rearrange("p (q n) -> p q n", n=128)
                    P.op(V, lambda e, its=its, src=src: e.scalar_tensor_tensor(out=ABQH[:, its, :], in0=src, scalar=-1.0, in1=Bs[:, its, :], op0=ALU.mult, op1=ALU.add),
                         reads=[r_pb[bk], r_Bs_tl, r_Bs_tr, r_Bs_bl, r_Bs_br[grp]], writes=[r_AB[grp]])
                P.stage(10)
                order = list(range(NB)) if d == 0 else list(range(NB - 1, -1, -1))
                P.op(V, lambda e, c0=order[0]: e.tensor_copy(out=Z[0:64, c0, :, :], in_=STc[:]), reads=[r_ST], writes=[r_Zs[order[0]]])
                ybank = 7
                for n, c in enumerate(order):
                    sbk = 2 + n % 2

                    def fs(e, c=c, sbk=sbk):
                        ins = None
                        for h in range(2):
                            it = c * 2 + h
                            ins = mm(e, pb[sbk][0:64, h * 64:(h + 1) * 64], ABQH[:, it, 0:64], Z[:, c, h, :])
                        return ins
                    P.op(PE, fs, reads=[r_AB[c // 2], r_Zv, r_Zs[c]], writes=[r_pb[sbk]])
                    src = pb[sbk][0:64, 0:128].rearrange("p (h v) -> p h v", h=2)
                    if n < NB - 1:
                        cn = order[n + 1]
                        P.op(A, lambda e, cn=cn, src=src: e.activation(out=Z[0:64, cn, :, :], in_=src, func=AF.Copy),
                             reads=[r_pb[sbk]], writes=[r_Zs[cn]])
                    else:
                        P.op(A, lambda e, src=src: e.activation(out=STc[:], in_=src, func=AF.Copy),
                             reads=[r_pb[sbk]], writes=[r_ST])

                    def fy(e, c=c):
                        ins = None
                        for h in range(2):
                            it = c * 2 + h
                            ins = mm(e, pb[ybank][64 * h:64 * h + 64, c * C:(c + 1) * C], Z[:, c, h, :], ABQH[:, it, 64:128])
                        return ins
                    P.op(PE, fy, reads=[r_AB[c // 2], r_Zv, r_Zs[c]], writes=[r_pb[ybank]])
                P.stage(11)
                if d == 0:
                    P.op(A, lambda e, t0=t0: e.activation(out=yf[:, t0:t0 + BT], in_=pb[ybank][:, :], func=AF.Copy),
                         reads=[r_pb[ybank]], writes=[r_yf[blk]])
                    continue
                P.op(V, lambda e, t0=t0: e.tensor_tensor(out=ysum[:], in0=pb[ybank][:, :], in1=yf[:, t0:t0 + BT], op=ALU.add),
                     reads=[r_pb[ybank], r_yf[blk]], writes=[r_ysum])
                P.op(A, lambda e: e.activation(out=ysq[:], in_=ysum[:], func=AF.Square), reads=[r_ysum], writes=[r_ysq])
                P.op(PE, lambda e: mm(e, pb[0][:, :], bdavg, ysum[:]), reads=[r_cm, r_ysum], writes=[r_pb[0]])
                P.op(PE, lambda e: mm(e, pb[1][:, :], bdavg, ysq[:]), reads=[r_cm, r_ysq], writes=[r_pb[1]])
                P.op(A, lambda e: e.activation(out=m2[:], in_=pb[0][:, :], func=AF.Square), reads=[r_pb[0]], writes=[r_m2])
                P.op(V, lambda e: e.tensor_tensor(out=m2[:], in0=pb[1][:, :], in1=m2[:], op=ALU.subtract), reads=[r_pb[1], r_m2], writes=[r_m2])
                P.op(A, lambda e: e.activation(out=m2[:], in_=m2[:], func=AF.Sqrt, bias=epsv[:, 1:2], scale=1.0),
                     reads=[r_m2, r_hm], writes=[r_m2])
                P.op(V, lambda e: e.reciprocal(out=m2[:], in_=m2[:]), reads=[r_m2], writes=[r_m2])
                P.op(V, lambda e: e.scalar_tensor_tensor(out=yn[:], in0=pb[0][:, :], scalar=-1.0, in1=ysum[:], op0=ALU.mult, op1=ALU.add), reads=[r_ysum, r_pb[0]], writes=[r_yn])
                P.op(V, lambda e: e.tensor_tensor(out=yn[:], in0=yn[:], in1=m2[:], op=ALU.mult), reads=[r_yn, r_m2], writes=[r_yn])
                P.op(V, lambda e: e.tensor_scalar(out=yn[:], in0=yn[:], scalar1=col(GNG), scalar2=col(GNB), op0=ALU.mult, op1=ALU.add),
                     reads=[r_yn, r_pv], writes=[r_yn])
                P.op(PE, lambda e: mm(e, pb[2][:, :], pm[0:64, 1, :], has[0:64, :]), reads=[r_pm, r_sh[4]], writes=[r_pb[2]])
                P.op(A, lambda e: e.activation(out=af[:], in_=pb[2][:, :], func=AF.Sigmoid, bias=col(A0F), scale=1.0),
                     reads=[r_pb[2], r_pv], writes=[r_af])
                P.op(G, lambda e: e.tensor_tensor(out=af[:], in0=af[:], in1=aa[:], op=ALU.add), reads=[r_af, r_aa], writes=[r_af])
                P.op(V, lambda e: e.tensor_scalar(out=t1[:], in0=af[:], scalar1=hm[:, 7:8], scalar2=hm[:, 6:7], op0=ALU.mult, op1=ALU.add),
                     reads=[r_af, r_pv, r_hm], writes=[r_t1])
                P.op(G, lambda e: e.tensor_tensor(out=rkb[:], in0=ks[:], in1=t1[:], op=ALU.mult), reads=[r_sh[1], r_t1], writes=[r_rkb])
                P.op(V, lambda e: e.scalar_tensor_tensor(out=rkb[:], in0=rkb[:], scalar=col(RK_), in1=rs[:], op0=ALU.mult, op1=ALU.mult),
                     reads=[r_rkb, r_sh[0], r_pv], writes=[r_rkb])
                P.op(PE, lambda e: mm(e, pb[3][:, :], bdones, rkb[:]), reads=[r_cm, r_rkb], writes=[r_pb[3]])
                P.op(V, lambda e: e.tensor_tensor(out=rkb[:], in0=pb[3][:, :], in1=vs[:], op=ALU.mult), reads=[r_pb[3], r_sh[2]], writes=[r_rkb])
                P.op(V, lambda e: e.tensor_tensor(out=yn[:], in0=yn[:], in1=rkb[:], op=ALU.add), reads=[r_yn, r_rkb], writes=[r_yn])
                P.op(A, lambda e: e.activation(out=sg[:], in_=hgs[:], func=AF.Sigmoid), reads=[r_sh[5]], writes=[r_sg])
                P.op(PE, lambda e: mm(e, pb[4][:, :], pm[:, 2, :], sg[:]), reads=[r_pm, r_sg], writes=[r_pb[4]])
                P.op(V, lambda e: e.tensor_tensor(out=yo[:], in0=pb[4][:, :], in1=yn[:], op=ALU.mult), reads=[r_yn, r_pb[4]], writes=[r_yo])
                P.dma("sync", yout[0, :, g0:g0 + BT], yo[:], reads=[r_yo], is_out=True)
                P.op(G, lambda e: e.tensor_tensor(out=cu[:], in0=raw[7][:], in1=raw[8][:], op=ALU.mult), reads=[r_raw[7], r_raw[8]], writes=[r_cu])
                P.op(V, lambda e: e.tensor_scalar(out=hc[:], in0=cu[:, 0:BT], scalar1=col(CW0), scalar2=None, op0=ALU.mult),
                     reads=[r_cu, r_pv], writes=[r_hc])
                P.op(V, lambda e: e.scalar_tensor_tensor(out=hc[:], in0=cu[:, 1:BT + 1], scalar=col(CW1), in1=hc[:], op0=ALU.mult, op1=ALU.add),
                     reads=[r_cu, r_hc, r_pv], writes=[r_hc])
                P.op(V, lambda e: e.scalar_tensor_tensor(out=hc[:], in0=cu[:, 2:BT + 2], scalar=col(CW2), in1=hc[:], op0=ALU.mult, op1=ALU.add),
                     reads=[r_cu, r_hc, r_pv], writes=[r_hc])
                P.op(G, lambda e: e.tensor_tensor(out=yc[:], in0=hc[:], in1=raw[6][:, 1:BT + 1], op=ALU.mult), reads=[r_hc, r_raw[6]], writes=[r_yc])
                P.dma("sync", yout[1, :, g0:g0 + BT], yc[:], reads=[r_yc], is_out=True)


D = 2048
KC = 16
F = 5504
FC = 43
NT = 1024
TT = 512
NTT = NT // TT
D_IN = 13696
RC = 6528
QC0 = 6528
GC0 = 7552
ALPHA = (2 * 2) ** 0.25
LN_EPS = 1e-5
V, G, A, PE = "vector", "gpsimd", "scalar", "tensor"
FGROUPS = [(0, 11), (11, 22), (22, 33), (33, 43)]


class DenseCtx:
    def __init__(self, P):
        self.P = P
        sb, ps = P.sb, P.ps
        self.pb = [ps(f"pb{i}", [128, 512]) for i in range(8)]
        self.r_pb = [Res() for _ in range(8)]
        self.bank = 0
        self.onesf = sb("onesf", [128, 128]); self.r_c = Res()
        self.onesb = sb("onesb", [128, 128], BF16)
        self.epsv = sb("epsv", [128, 1])
        P.op(V, lambda e: e.memset(self.onesf[:], 1.0 / D), writes=[self.r_c])
        P.op(V, lambda e: e.memset(self.onesb[:], 1.0), writes=[self.r_c])
        P.op(V, lambda e: e.memset(self.epsv[:], LN_EPS), writes=[self.r_c])
        self.wA = [sb(f"wA{i}", [128, 16, 256], BF16) for i in range(3)]
        self.r_wA = [Res() for _ in range(3)]
        self.wA_i = 0
        self.wD = [sb(f"wD{i}", [128, 11, 256], BF16) for i in range(2)]
        self.r_wD = [Res() for _ in range(2)]
        self.wD_i = 0
        self.tmp = [sb(f"tmp{i}", [128, 512]) for i in range(4)]
        self.r_tmp = [Res() for _ in range(4)]
        self.tmp_i = 0
        self.stat = [sb(f"stat{i}", [128, 512]) for i in range(3)]
        self.r_stat = [Res() for _ in range(3)]

    def nbank(self):
        b = self.bank
        self.bank = (b + 1) % 8
        return b

    def ntmp(self):
        i = self.tmp_i
        self.tmp_i = (i + 1) % 4
        return i

    def load_wA(self, w_ap, kc, mcols):
        i = self.wA_i
        self.wA_i = (i + 1) % 3
        t = self.wA[i]
        self.P.dma("gpsimd", t[:, 0:kc, 0:mcols], w_ap.rearrange("(k p) m -> p k m", p=128), writes=[self.r_wA[i]])
        return t, self.r_wA[i]

    def load_wD(self, w_ap, kc, mcols):
        i = self.wD_i
        self.wD_i = (i + 1) % 2
        t = self.wD[i]
        self.P.dma("gpsimd", t[:, 0:kc, 0:mcols], w_ap.rearrange("(k p) m -> p k m", p=128), writes=[self.r_wD[i]])
        return t, self.r_wD[i]


def mm_group(P, ctx, bank, pairs, reads, n=512, mrows=128):
    def f(e):
        ins = None
        L = len(pairs)
        for i, (l, r) in enumerate(pairs):
            ins = e.matmul(ctx.pb[bank][0:mrows, 0:n], l, r, start=(i == 0), stop=(i == L - 1))
        return ins
    P.op(PE, f, reads=reads, writes=[ctx.r_pb[bank]])


def layer_norm_fm(P, ctx, X, r_X, XB, r_XB, gb, r_gb, gcol, bcol, kc_n=KC, ntok=NT, write_x=True):
    tts = [(t0, min(TT, ntok - t0)) for t0 in range(0, ntok, TT)]
    for (t0, n) in tts:
        b_sum = ctx.nbank()
        mm_group(P, ctx, b_sum, [(ctx.onesf[:], X[:, kc, t0:t0 + n]) for kc in range(kc_n)], [ctx.r_c] + [r_X[kc] for kc in range(kc_n)], n=n)
        b_sq = ctx.nbank()
        sqs = []
        for kc in range(kc_n):
            ti = ctx.ntmp()
            P.op(A if kc % 2 else G, (lambda e, ti=ti, kc=kc: e.activation(out=ctx.tmp[ti][:, 0:n], in_=X[:, kc, t0:t0 + n], func=AF.Square)) if kc % 2 else
                 (lambda e, ti=ti, kc=kc: e.tensor_tensor(out=ctx.tmp[ti][:, 0:n], in0=X[:, kc, t0:t0 + n], in1=X[:, kc, t0:t0 + n], op=ALU.mult)),
                 reads=[r_X[kc]], writes=[ctx.r_tmp[ti]])
            P.op(PE, lambda e, ti=ti, kc=kc: e.matmul(ctx.pb[b_sq][:, 0:n], ctx.onesf[:], ctx.tmp[ti][:, 0:n], start=(kc == 0), stop=(kc == kc_n - 1)),
                 reads=[ctx.r_c, ctx.r_tmp[ti]], writes=[ctx.r_pb[b_sq]])
        mean_ps = ctx.pb[b_sum][:, 0:n]
        e2_ps = ctx.pb[b_sq][:, 0:n]
        m2, rstd, nmr = ctx.stat[0][:, 0:n], ctx.stat[1][:, 0:n], ctx.stat[2][:, 0:n]
        P.op(A, lambda e: e.activation(out=m2, in_=mean_ps, func=AF.Square), reads=[ctx.r_pb[b_sum]], writes=[ctx.r_stat[0]])
        P.op(V, lambda e: e.tensor_tensor(out=rstd, in0=e2_ps, in1=m2, op=ALU.subtract), reads=[ctx.r_pb[b_sq], ctx.r_stat[0]], writes=[ctx.r_stat[1]])
        P.op(A, lambda e: e.activation(out=rstd, in_=rstd, func=AF.Sqrt, bias=ctx.epsv[:, 0:1], scale=1.0), reads=[ctx.r_stat[1], ctx.r_c], writes=[ctx.r_stat[1]])
        P.op(V, lambda e: e.reciprocal(out=rstd, in_=rstd), reads=[ctx.r_stat[1]], writes=[ctx.r_stat[1]])
        P.op(V, lambda e: e.scalar_tensor_tensor(out=nmr, in0=mean_ps, scalar=-1.0, in1=rstd, op0=ALU.mult, op1=ALU.mult),
             reads=[ctx.r_pb[b_sum], ctx.r_stat[1]], writes=[ctx.r_stat[2]])
        for kc in range(kc_n):
            ti = ctx.ntmp()
            t = ctx.tmp[ti][:, 0:n]
            P.op(V, lambda e, t=t, kc=kc: e.tensor_tensor(out=t, in0=X[:, kc, t0:t0 + n], in1=rstd, op=ALU.mult),
                 reads=[r_X[kc], ctx.r_stat[1]], writes=[ctx.r_tmp[ti]])
            P.op(G, lambda e, t=t: e.tensor_tensor(out=t, in0=t, in1=nmr, op=ALU.add), reads=[ctx.r_tmp[ti], ctx.r_stat[2]], writes=[ctx.r_tmp[ti]])
            if write_x:
                P.op(V, lambda e, t=t, kc=kc: e.tensor_scalar(out=X[:, kc, t0:t0 + n], in0=t, scalar1=gb[:, gcol, kc:kc + 1], scalar2=gb[:, bcol, kc:kc + 1],
                                                              op0=ALU.mult, op1=ALU.add), reads=[ctx.r_tmp[ti], r_gb], writes=[r_X[kc]])
                P.op(A, lambda e, kc=kc: e.activation(out=XB[:, kc, t0:t0 + n], in_=X[:, kc, t0:t0 + n], func=AF.Copy), reads=[r_X[kc]], writes=[r_XB[kc]])
            else:
                P.op(V, lambda e, t=t, kc=kc: e.tensor_scalar(out=XB[:, kc, t0:t0 + n], in0=t, scalar1=gb[:, gcol, kc:kc + 1], scalar2=gb[:, bcol, kc:kc + 1],
                                                              op0=ALU.mult, op1=ALU.add), reads=[ctx.r_tmp[ti], r_gb], writes=[r_XB[kc]])


def ffn(P, ctx, X, r_X, XB, r_XB, H, r_H, wg, wu, wd):
    for kc in range(KC):
        P.op(G, lambda e, kc=kc: e.tensor_scalar(out=X[:, kc, :], in0=X[:, kc, :], scalar1=ALPHA, scalar2=None, op0=ALU.mult),
             reads=[r_X[kc]], writes=[r_X[kc]])
    for (f0, f1) in FGROUPS:
        nf = f1 - f0
        for fb in range(f0, f1, 2):
            nb = min(2, f1 - fb)
            wgt, r_wg = ctx.load_wA(wg[:, fb * 128:(fb + nb) * 128], KC, nb * 128)
            wut, r_wu = ctx.load_wA(wu[:, fb * 128:(fb + nb) * 128], KC, nb * 128)
            for j in range(nb):
                fi = fb + j - f0
                for tt in range(NTT):
                    ts_ = slice(tt * TT, (tt + 1) * TT)
                    bg = ctx.nbank()
                    mm_group(P, ctx, bg, [(wgt[:, kc, j * 128:(j + 1) * 128], XB[:, kc, ts_]) for kc in range(KC)], [r_wg] + list(r_XB))
                    bu = ctx.nbank()
                    mm_group(P, ctx, bu, [(wut[:, kc, j * 128:(j + 1) * 128], XB[:, kc, ts_]) for kc in range(KC)], [r_wu] + list(r_XB))
                    ti = ctx.ntmp()
                    P.op(A, lambda e, ti=ti, bg=bg: e.activation(out=ctx.tmp[ti][:], in_=ctx.pb[bg][:, :], func=AF.Silu),
                         reads=[ctx.r_pb[bg]], writes=[ctx.r_tmp[ti]])
                    P.op(V, lambda e, ti=ti, bu=bu, fi=fi, ts_=ts_: e.tensor_tensor(out=H[:, fi, ts_], in0=ctx.pb[bu][:, :], in1=ctx.tmp[ti][:], op=ALU.mult),
                         reads=[ctx.r_pb[bu], ctx.r_tmp[ti]], writes=[r_H[fi]])
        for db in range(0, KC, 2):
            wdt, r_wd = ctx.load_wD(wd[f0 * 128:f1 * 128, db * 128:(db + 2) * 128], nf, 256)
            for j in range(2):
                dc = db + j
                for tt in range(NTT):
                    ts_ = slice(tt * TT, (tt + 1) * TT)
                    b = ctx.nbank()
                    mm_group(P, ctx, b, [(wdt[:, fi, j * 128:(j + 1) * 128], H[:, fi, ts_]) for fi in range(nf)], [r_wd] + [r_H[fi] for fi in range(nf)])
                    P.op(V, lambda e, b=b, dc=dc, ts_=ts_: e.scalar_tensor_tensor(out=X[:, dc, ts_], in0=ctx.pb[b][:, :], scalar=0.5, in1=X[:, dc, ts_],
                                                                                 op0=ALU.mult, op1=ALU.add),
                         reads=[ctx.r_pb[b], r_X[dc]], writes=[r_X[dc]])


def load_x(P, X, r_X, XB, r_XB, xT):
    for kc in range(KC):
        P.dma("sync", X[:, kc, :], xT[kc * 128:(kc + 1) * 128, :], writes=[r_X[kc]])
        P.op(A if kc % 2 else V, (lambda e, kc=kc: e.activation(out=XB[:, kc, :], in_=X[:, kc, :], func=AF.Copy)) if kc % 2 else
             (lambda e, kc=kc: e.tensor_copy(out=XB[:, kc, :], in_=X[:, kc, :])), reads=[r_X[kc]], writes=[r_XB[kc]])


def build_A():
    nc = bass.Bass("TRN2", target_bir_lowering=False)
    xT = nc.dram_tensor("xT", [D, NT], F32, kind="ExternalInput").ap()
    wg = nc.dram_tensor("wg", [D, F], F32, kind="ExternalInput").ap()
    wu = nc.dram_tensor("wu", [D, F], F32, kind="ExternalInput").ap()
    wd = nc.dram_tensor("wd", [F, D], F32, kind="ExternalInput").ap()
    lngb = nc.dram_tensor("lngb", [128, 2, KC], F32, kind="ExternalInput").ap()
    w_in = nc.dram_tensor("w_in", [D, RC], F32, kind="ExternalInput").ap()
    x1T = nc.dram_tensor("x1T", [D, NT], F32, kind="ExternalOutput").ap()
    pT = nc.dram_tensor("pT", [RC, NT], F32, kind="ExternalOutput").ap()
    with ExitStack() as es:
        P = Prog(nc, es)
        ctx = DenseCtx(P)
        X = P.sb("X", [128, KC, NT]); r_X = [Res() for _ in range(KC)]
        XB = P.sb("XB", [128, KC, NT], BF16); r_XB = [Res() for _ in range(KC)]
        H = P.sb("H", [128, 11, NT], BF16); r_H = [Res() for _ in range(11)]
        gb = P.sb("gb", [128, 2, KC]); r_gb = Res()
        P.dma("sync", gb[:], lngb[:, :, :], writes=[r_gb])
        load_x(P, X, r_X, XB, r_XB, xT)
        ffn(P, ctx, X, r_X, XB, r_XB, H, r_H, wg, wu, wd)
        layer_norm_fm(P, ctx, X, r_X, XB, r_XB, gb, r_gb, 0, 1)
        for kc in range(KC):
            P.dma("sync", x1T[kc * 128:(kc + 1) * 128, :], X[:, kc, :], reads=[r_X[kc]], is_out=True)
        for mb in range(0, RC // 128, 2):
            nb = min(2, RC // 128 - mb)
            wt, r_w = ctx.load_wA(w_in[:, mb * 128:(mb + nb) * 128], KC, nb * 128)
            for j in range(nb):
                m = mb + j
                for tt in range(NTT):
                    ts_ = slice(tt * TT, (tt + 1) * TT)
                    b = ctx.nbank()
                    mm_group(P, ctx, b, [(wt[:, kc, j * 128:(j + 1) * 128], XB[:, kc, ts_]) for kc in range(KC)], [r_w] + list(r_XB))
                    ti = ctx.ntmp()
                    if (m + tt) % 2:
                        P.op(A, lambda e, ti=ti, b=b: e.activation(out=ctx.tmp[ti][:], in_=ctx.pb[b][:, :], func=AF.Copy), reads=[ctx.r_pb[b]], writes=[ctx.r_tmp[ti]])
                    else:
                        P.op(V, lambda e, ti=ti, b=b: e.tensor_copy(out=ctx.tmp[ti][:], in_=ctx.pb[b][:, :]), reads=[ctx.r_pb[b]], writes=[ctx.r_tmp[ti]])
                    P.dma("sync", pT[m * 128:(m + 1) * 128, ts_], ctx.tmp[ti][:], reads=[ctx.r_tmp[ti]], is_out=True)
        P.finish()
        P.emit()
    return nc


def build_C():
    nc = bass.Bass("TRN2", target_bir_lowering=False)
    x1T = nc.dram_tensor("x1T", [D, NT], F32, kind="ExternalInput").ap()
    yT = nc.dram_tensor("yT", [2, 1024, NT], F32, kind="ExternalInput").ap()
    memT = nc.dram_tensor("memT", [D, 256], F32, kind="ExternalInput").ap()
    w_q = nc.dram_tensor("w_q", [D, 1024], F32, kind="ExternalInput").ap()
    w_g = nc.dram_tensor("w_g", [D, 3 * D], F32, kind="ExternalInput").ap()
    w_kv = nc.dram_tensor("w_kv", [D, 2048], F32, kind="ExternalInput").ap()
    w_br = nc.dram_tensor("w_br", [3, 1024, D], F32, kind="ExternalInput").ap()
    w_o = nc.dram_tensor("w_o", [D, D], F32, kind="ExternalInput").ap()
    vecs = nc.dram_tensor("vecs", [128, 9, KC], F32, kind="ExternalInput").ap()
    wg = nc.dram_tensor("wg", [D, F], F32, kind="ExternalInput").ap()
    wu = nc.dram_tensor("wu", [D, F], F32, kind="ExternalInput").ap()
    wd = nc.dram_tensor("wd", [F, D], F32, kind="ExternalInput").ap()
    x2T = nc.dram_tensor("x2T", [D, NT], F32, kind="ExternalOutput").ap()
    with ExitStack() as es:
        P = Prog(nc, es)
        ctx = DenseCtx(P)
        ARENA = P.sb("ARENA", [128, KC * NT])
        ARENA2 = P.sb("ARENA2", [128, 8192])
        X = ARENA[:, :].rearrange("p (c t) -> p c t", t=NT); r_X = [Res() for _ in range(KC)]
        XB = P.sb("XB", [128, KC, NT], BF16); r_XB = [Res() for _ in range(KC)]
        Yall = ARENA[:, 0:12288].bitcast(BF16).rearrange("p (c t) -> p c t", t=NT); r_Y = [Res() for _ in range(24)]
        QB = ARENA[:, 12288:16384].bitcast(BF16).rearrange("p (c t) -> p c t", t=NT); r_QB = [Res() for _ in range(8)]
        A2B = ARENA2[:, :].bitcast(BF16)
        MIXB = A2B.rearrange("p (c t) -> p c t", t=NT); r_MIX = [Res() for _ in range(KC)]
        H = A2B[:, 0:11 * NT].rearrange("p (c t) -> p c t", t=NT); r_H = [Res() for _ in range(11)]
        MX = ARENA2[:, 0:4096].rearrange("p (c t) -> p c t", t=256); r_MX = [Res() for _ in range(KC)]
        MB = ARENA2[:, 4096:6144].bitcast(BF16).rearrange("p (c t) -> p c t", t=256); r_MB = [Res() for _ in range(KC)]
        KTB = ARENA2[:, 6144:7168].bitcast(BF16).rearrange("p (c t) -> p c t", t=256); r_KTB = Res()
        VB = ARENA2[:, 7168:8192].bitcast(BF16).rearrange("p (c t) -> p c t", t=1024); r_VB = Res()
        vc = P.sb("vc", [128, 9, KC]); r_vc = Res()
        P.dma("sync", vc[:], vecs[:, :, :], writes=[r_vc])
        for kc in range(KC):
            P.dma("gpsimd", XB[:, kc, :], x1T[kc * 128:(kc + 1) * 128, :], writes=[r_XB[kc]])
        for n in range(2):
            for c in range(8):
                P.dma("gpsimd", Yall[:, n * 8 + c, :], yT[n, c * 128:(c + 1) * 128, :], writes=[r_Y[n * 8 + c]])
        for kc in range(KC):
            P.dma("sync", MX[:, kc, :], memT[kc * 128:(kc + 1) * 128, :], writes=[r_MX[kc]])
        layer_norm_fm(P, ctx, MX, r_MX, MB, r_MB, vc, r_vc, 0, 1, ntok=256, write_x=False)
        for mb in range(0, 8, 2):
            wt, r_w = ctx.load_wA(w_kv[:, mb * 128:(mb + 2) * 128], KC, 256)
            for j in range(2):
                b = ctx.nbank()
                mm_group(P, ctx, b, [(wt[:, kc, j * 128:(j + 1) * 128], MB[:, kc, :]) for kc in range(KC)], [r_w] + list(r_MB), n=256)
                P.op(V, lambda e, b=b, m=mb + j: e.tensor_copy(out=KTB[:, m, :], in_=ctx.pb[b][:, 0:256]), reads=[ctx.r_pb[b]], writes=[r_KTB])
        for vb in range(4):
            wt, r_w = ctx.load_wA(w_kv[:, 1024 + vb * 256:1024 + (vb + 1) * 256], KC, 256)
            for mc in range(2):
                b = ctx.nbank()
                mm_group(P, ctx, b, [(MB[:, kc, mc * 128:(mc + 1) * 128], wt[:, kc, :]) for kc in range(KC)], [r_w] + list(r_MB), n=256)
                P.op(V, lambda e, b=b, mc=mc, vb=vb: e.tensor_copy(out=VB[:, mc, vb * 256:(vb + 1) * 256], in_=ctx.pb[b][:, 0:256]), reads=[ctx.r_pb[b]], writes=[r_VB])
        for mb in range(0, 8, 2):
            wt, r_w = ctx.load_wA(w_q[:, mb * 128:(mb + 2) * 128], KC, 256)
            for j in range(2):
                for tt in range(NTT):
                    ts_ = slice(tt * TT, (tt + 1) * TT)
                    b = ctx.nbank()
                    mm_group(P, ctx, b, [(wt[:, kc, j * 128:(j + 1) * 128], XB[:, kc, ts_]) for kc in range(KC)], [r_w] + list(r_XB))
                    P.op(A, lambda e, b=b, m=mb + j, ts_=ts_: e.activation(out=QB[:, m, ts_], in_=ctx.pb[b][:, :], func=AF.Copy), reads=[ctx.r_pb[b]], writes=[r_QB[mb + j]])
        EB = P.sb("EB", [128, 2, TT], BF16); r_EB = Res()
        rden = P.sb("rden", [128, TT]); r_rden = Res()
        for h in range(4):
            for tt in range(NTT):
                ts_ = slice(tt * TT, (tt + 1) * TT)
                for mc in range(2):
                    b = ctx.nbank()
                    mm_group(P, ctx, b, [(KTB[:, h * 2 + dc, mc * 128:(mc + 1) * 128], QB[:, h * 2 + dc, ts_]) for dc in range(2)], [r_KTB, r_QB[h * 2], r_QB[h * 2 + 1]])
                    P.op(A, lambda e, b=b, mc=mc: e.activation(out=EB[:, mc, :], in_=ctx.pb[b][:, :], func=AF.Exp, scale=1.0 / 16.0), reads=[ctx.r_pb[b]], writes=[r_EB])
                b = ctx.nbank()
                mm_group(P, ctx, b, [(ctx.onesb[:], EB[:, mc, :]) for mc in range(2)], [ctx.r_c, r_EB])
                P.op(V, lambda e, b=b: e.reciprocal(out=rden[:], in_=ctx.pb[b][:, :]), reads=[ctx.r_pb[b]], writes=[r_rden])
                for dc in range(2):
                    b = ctx.nbank()
                    mm_group(P, ctx, b, [(VB[:, mc, h * 256 + dc * 128:h * 256 + (dc + 1) * 128], EB[:, mc, :]) for mc in range(2)], [r_VB, r_EB])
                    P.op(V, lambda e, b=b, c=16 + h * 2 + dc, ts_=ts_: e.tensor_tensor(out=Yall[:, c, ts_], in0=ctx.pb[b][:, :], in1=rden[:], op=ALU.mult),
                         reads=[ctx.r_pb[b], r_rden], writes=[r_Y[16 + h * 2 + dc]])
        gate = P.sb("gate", [128, TT]); r_gate = Res()
        term = P.sb("term", [128, TT]); r_term = Res()
        ACC = P.sb("ACC", [128, 2, NT]); r_ACC = Res()
        for db in range(0, KC, 2):
            for n in range(3):
                wt, r_w = ctx.load_wA(w_g[:, n * D + db * 128:n * D + (db + 2) * 128], KC, 256)
                wbt, r_wb = ctx.load_wD(w_br[n, :, db * 128:(db + 2) * 128], 8, 256)
                for j in range(2):
                    dc = db + j
                    for tt in range(NTT):
                        ts_ = slice(tt * TT, (tt + 1) * TT)
                        bg = ctx.nbank()
                        mm_group(P, ctx, bg, [(wt[:, kc, j * 128:(j + 1) * 128], XB[:, kc, ts_]) for kc in range(KC)], [r_w] + list(r_XB))
                        P.op(A, lambda e, bg=bg, n=n, dc=dc: e.activation(out=gate[:], in_=ctx.pb[bg][:, :], func=AF.Sigmoid, bias=vc[:, 2 + n, dc:dc + 1], scale=1.0),
                             reads=[ctx.r_pb[bg], r_vc], writes=[r_gate])
                        bp_ = ctx.nbank()
                        mm_group(P, ctx, bp_, [(wbt[:, c, j * 128:(j + 1) * 128], Yall[:, n * 8 + c, ts_]) for c in range(8)], [r_wb] + [r_Y[n * 8 + c] for c in range(8)])
                        if n == 0:
                            P.op(V, lambda e, bp_=bp_, j=j, ts_=ts_: e.tensor_tensor(out=ACC[:, j, ts_], in0=ctx.pb[bp_][:, :], in1=gate[:], op=ALU.mult),
                                 reads=[ctx.r_pb[bp_], r_gate], writes=[r_ACC])
                        else:
                            P.op(V, lambda e, bp_=bp_: e.tensor_tensor(out=term[:], in0=ctx.pb[bp_][:, :], in1=gate[:], op=ALU.mult),
                                 reads=[ctx.r_pb[bp_], r_gate], writes=[r_term])
                            if n == 1:
                                P.op(G, lambda e, j=j, ts_=ts_: e.tensor_tensor(out=ACC[:, j, ts_], in0=ACC[:, j, ts_], in1=term[:], op=ALU.add), reads=[r_ACC, r_term], writes=[r_ACC])
                            else:
                                P.op(G, lambda e, dc=dc, j=j, ts_=ts_: e.tensor_tensor(out=MIXB[:, dc, ts_], in0=ACC[:, j, ts_], in1=term[:], op=ALU.add),
                                     reads=[r_ACC, r_term], writes=[r_MIX[dc]])
        xs = [P.sb(f"xs{i}", [128, TT]) for i in range(2)]; r_xs = [Res(), Res()]
        cnt = 0
        for db in range(0, KC, 2):
            wt, r_w = ctx.load_wA(w_o[:, db * 128:(db + 2) * 128], KC, 256)
            for j in range(2):
                dc = db + j
                for tt in range(NTT):
                    ts_ = slice(tt * TT, (tt + 1) * TT)
                    i = cnt % 2; cnt += 1
                    P.dma("sync", xs[i][:], x1T[dc * 128:(dc + 1) * 128, ts_], writes=[r_xs[i]])
                    P.op(G, lambda e, i=i: e.tensor_scalar(out=xs[i][:], in0=xs[i][:], scalar1=ALPHA, scalar2=None, op0=ALU.mult), reads=[r_xs[i]], writes=[r_xs[i]])
                    b = ctx.nbank()
                    mm_group(P, ctx, b, [(wt[:, kc, j * 128:(j + 1) * 128], MIXB[:, kc, ts_]) for kc in range(KC)], [r_w] + list(r_MIX))
                    P.op(V, lambda e, b=b, i=i, dc=dc, ts_=ts_: e.tensor_tensor(out=X[:, dc, ts_], in0=ctx.pb[b][:, :], in1=xs[i][:], op=ALU.add),
                         reads=[ctx.r_pb[b], r_xs[i]], writes=[r_X[dc]])
        layer_norm_fm(P, ctx, X, r_X, XB, r_XB, vc, r_vc, 5, 6)
        ffn(P, ctx, X, r_X, XB, r_XB, H, r_H, wg, wu, wd)
        layer_norm_fm(P, ctx, X, r_X, XB, r_XB, vc, r_vc, 7, 8)
        for kc in range(KC):
            P.dma("sync", x2T[kc * 128:(kc + 1) * 128, :], X[:, kc, :], reads=[r_X[kc]], is_out=True)
        P.finish()
        P.emit()
    return nc


_PROGS = {}


def _prog(name):
    if name not in _PROGS:
        _PROGS[name] = {"A": build_A, "C": build_C, "S": lambda: build_scan(4096, 2)}[name]()
    return _PROGS[name]


def _vec16(v):
    return np.ascontiguousarray(np.asarray(v, np.float32).reshape(16, 128).T)


def kernel(x, mem, ffn1_w_gate, ffn1_w_up, ffn1_w_down, ln1_g, ln1_b, w_in,
           rwkv_mu, rwkv_w0, rwkv_w_up, rwkv_a0, rwkv_a_up, rwkv_g_up, rwkv_k_k,
           rwkv_k_a, rwkv_r_k, rwkv_gn_g, rwkv_gn_b, conv_w, mem_ln_g, mem_ln_b,
           w_mem_kv, w_branch, gate_b, w_out, ln2_g, ln2_b, ffn2_w_gate, ffn2_w_up,
           ffn2_w_down, ln3_g, ln3_b):
    f32 = lambda a: np.ascontiguousarray(np.asarray(a, dtype=np.float32))
    x = f32(x); mem = f32(mem)
    NCORE = 8
    cores = list(range(NCORE))
    xT = [np.ascontiguousarray(x[c // 4, (c % 4) * 1024:(c % 4 + 1) * 1024].T) for c in cores]
    memT = [np.ascontiguousarray(mem[c // 4].T) for c in cores]
    cm, rmask = scan_consts()
    for l in range(2):
        w_in_l = f32(w_in[l])
        mA = {"wg": f32(ffn1_w_gate[l]), "wu": f32(ffn1_w_up[l]), "wd": f32(ffn1_w_down[l]),
              "lngb": np.ascontiguousarray(np.stack([_vec16(ln1_g[l]), _vec16(ln1_b[l])], 1)),
              "w_in": np.ascontiguousarray(w_in_l[:, :RC])}
        resA = run_bass_kernel_spmd(_prog("A"), [dict(mA, xT=xT[c]) for c in cores], core_ids=cores).results
        x1T = [np.asarray(resA[c]["x1T"]) for c in cores]
        pall = np.concatenate([np.asarray(resA[c]["pT"]) for c in cores], axis=1)
        del resA
        mu = f32(rwkv_mu[l]); w0 = f32(rwkv_w0[l]); a0 = f32(rwkv_a0[l])
        wup = f32(rwkv_w_up[l]); aup = f32(rwkv_a_up[l]); gup = f32(rwkv_g_up[l])
        kkv = f32(rwkv_k_k[l]); kav = f32(rwkv_k_a[l]); rkv = f32(rwkv_r_k[l]).reshape(-1)
        gng = f32(rwkv_gn_g[l]); gnb = f32(rwkv_gn_b[l]); cw = f32(conv_w[l])
        mS = []
        for c in cores:
            cs = slice(c * 128, (c + 1) * 128)
            zin = np.stack([pall[0:1024][cs], pall[1024:2048][cs], pall[2048:3072][cs], pall[3072:3200], pall[3200:3328], pall[3328:3456],
                            pall[3456:4480][cs], pall[4480:5504][cs], pall[5504:6528][cs]], 0)
            pv = np.stack([mu[0:1024][cs], mu[1024:2048][cs], mu[2048:3072][cs], mu[3072:3200], mu[3200:3328], mu[3328:3456],
                           w0[0][cs], w0[1][cs], a0[0][cs], a0[1][cs], kkv[cs], kav[cs], rkv[cs], gng[cs], gnb[cs],
                           cw[0][cs], cw[1][cs], cw[2][cs]], 1)
            pm = np.stack([wup[:, :, cs].reshape(128, 128), aup[:, :, cs].reshape(128, 128), gup[:, cs]], 1)
            mS.append({"zin": np.ascontiguousarray(zin), "pvec": np.ascontiguousarray(pv), "pmat": np.ascontiguousarray(pm), "cmat": cm, "rmask": rmask})
        del pall
        resS = run_bass_kernel_spmd(_prog("S"), mS, core_ids=cores).results
        yall = np.concatenate([np.asarray(resS[c]["yout"]) for c in cores], axis=1)
        del resS, mS
        vecs = np.ascontiguousarray(np.stack([_vec16(mem_ln_g[l]), _vec16(mem_ln_b[l]), _vec16(gate_b[l][0]), _vec16(gate_b[l][1]), _vec16(gate_b[l][2]),
                                              _vec16(ln2_g[l]), _vec16(ln2_b[l]), _vec16(ln3_g[l]), _vec16(ln3_b[l])], 1))
        mC = {"w_q": np.ascontiguousarray(w_in_l[:, 6528:7552]), "w_g": np.ascontiguousarray(w_in_l[:, 7552:]), "w_kv": f32(w_mem_kv[l]),
              "w_br": f32(w_branch[l]), "w_o": f32(w_out[l]), "vecs": vecs,
              "wg": f32(ffn2_w_gate[l]), "wu": f32(ffn2_w_up[l]), "wd": f32(ffn2_w_down[l])}
        resC = run_bass_kernel_spmd(_prog("C"), [dict(mC, x1T=x1T[c], yT=np.ascontiguousarray(yall[:, :, c * 1024:(c + 1) * 1024]), memT=memT[c]) for c in cores],
                                    core_ids=cores).results
        xT = [np.asarray(resC[c]["x2T"]) for c in cores]
        del resC, yall
    out = np.empty((2, 4096, 2048), np.float32)
    for c in cores:
        out[c // 4, (c % 4) * 1024:(c % 4 + 1) * 1024] = xT[c].T
    return out
```

```python
import numpy as np
from contextlib import ExitStack
import concourse.bass as bass
import concourse.mybir as mybir
from concourse.bass_utils import run_bass_kernel_spmd


F32 = mybir.dt.float32
BF16 = mybir.dt.bfloat16
AF = mybir.ActivationFunctionType
ALU = mybir.AluOpType
ND = 6


class Res:
    __slots__ = ("w", "r")

    def __init__(self):
        self.w = None
        self.r = []


def resgrid(*shape):
    a = np.empty(shape, dtype=object)
    for idx in np.ndindex(*shape):
        a[idx] = Res()
    return a


class _Rec:
    def __init__(self):
        self.calls = []

    def __getattr__(self, name):
        def f(*a, **k):
            self.calls.append((name, a, k))
            return None
        return f


def _replay(calls):
    def fn(e):
        ins = None
        for name, a, k in calls:
            ins = getattr(e, name)(*a, **k)
        return ins
    return fn


class Prog:
    ENGS = ("tensor", "vector", "scalar", "gpsimd", "sync")

    def __init__(self, nc, es):
        self.nc = nc
        self.es = es
        self.q = {e: [] for e in self.ENGS}
        self.sem = {}
        for e in ("tensor", "vector", "scalar", "gpsimd"):
            self.sem[("e", e)] = es.enter_context(nc.semaphore("s_" + e))
        self.ecnt = {e: 0 for e in self.ENGS}
        self.dcnt = {}
        self.dnext = {}
        for qn in ("sync", "gpsimd", "scalar"):
            self.dnext[qn] = 0
            for i in range(ND):
                self.sem[("d", qn, i)] = es.enter_context(nc.semaphore(f"d_{qn}{i}"))
                self.dcnt[(qn, i)] = 0
        self.seen = {e: {} for e in self.ENGS}
        self.out_stamps = []

    stop_stage = None

    def stage(self, n):
        if self.stop_stage is not None and n == self.stop_stage:
            raise StopIteration

    def sb(self, name, shape, dt=F32):
        return self.es.enter_context(self.nc.sbuf_tensor(name, list(shape), dt))

    def ps(self, name, shape, dt=F32):
        return self.es.enter_context(self.nc.psum_tensor(name, list(shape), dt))

    def _deps(self, eng, reads, writes, extra=()):
        deps = {}

        def add(st):
            if st is None:
                return
            k, v = st
            if deps.get(k, 0) < v:
                deps[k] = v

        for r in reads:
            add(r.w)
        for w in writes:
            add(w.w)
            for s in w.r:
                add(s)
        for s in extra:
            add(s)
        waits = []
        for k, v in deps.items():
            if eng == "tensor" and k == ("e", "tensor"):
                continue
            if self.seen[eng].get(k, 0) >= v:
                continue
            self.seen[eng][k] = v
            waits.append((k, v))
        return waits

    def _commit(self, st, reads, writes):
        for r in reads:
            r.r.append(st)
        for w in writes:
            w.w = st
            w.r = []

    def op(self, eng, fn, reads=(), writes=()):
        waits = self._deps(eng, reads, writes)
        self.ecnt[eng] += 1
        st = (("e", eng), self.ecnt[eng])
        rec = _Rec()
        fn(rec)
        assert rec.calls
        self.q[eng].append((waits, _replay(rec.calls), st))
        self._commit(st, reads, writes)
        return st

    def dma(self, qn, out, in_, reads=(), writes=(), is_out=False):
        i = self.dnext[qn]
        self.dnext[qn] = (i + 1) % ND
        key = ("d", qn, i)
        prev = self.dcnt[(qn, i)]
        extra = [(key, prev)] if prev > 0 else []
        waits = self._deps(qn, reads, writes, extra)
        self.dcnt[(qn, i)] = prev + 16
        st = (key, prev + 16)
        self.q[qn].append((waits, (lambda e, o=out, i_=in_: e.dma_start(out=o, in_=i_)), st))
        self._commit(st, reads, writes)
        if is_out:
            self.out_stamps.append(st)
        return st

    def finish(self):
        final = {}
        for qn in ("sync", "gpsimd", "scalar"):
            for i in range(ND):
                v = self.dcnt[(qn, i)]
                if v > 0:
                    final[("d", qn, i)] = v
        self.q["sync"].append((list(final.items()), None, None))

    def emit(self):
        nc = self.nc
        with nc.Block() as block:
            def mk(name):
                def body(e):
                    for waits, fn, st in self.q[name]:
                        for k, v in waits:
                            e.wait_ge(self.sem[k], v)
                        if fn is None:
                            continue
                        ins = fn(e)
                        if st is not None:
                            ins.then_inc(self.sem[st[0]], 16 if st[0][0] == "d" else 1)
                return body
            block.tensor(mk("tensor"))
            block.vector(mk("vector"))
            block.scalar(mk("scalar"))
            block.gpsimd(mk("gpsimd"))
            block.sync(mk("sync"))


C = 64
NB = 8
BT = NB * C
NIT = NB * 2
DEC = 0.606531
GN_EPS = 64e-5
(MU_R, MU_K, MU_V, MU_HW, MU_HA, MU_HG, W0F, W0B, A0F, A0B, KK_, KA_, RK_, GNG, GNB, CW0, CW1, CW2) = range(18)
NPV = 18


def scan_consts():
    idx = np.arange(C)
    cm = np.zeros((128, 7, 128), np.float32)
    cm[:, 0, :] = np.eye(128)
    bd = np.zeros((128, 128), np.float32)
    bd[:64, :64] = 1
    bd[64:, 64:] = 1
    cm[:, 1, :] = bd
    cm[:, 2, :] = bd / 64.0
    for d in range(2):
        if d == 0:
            strict = (idx[:, None] < idx[None, :]).astype(np.float32)
            incl = (idx[:, None] <= idx[None, :]).astype(np.float32)
        else:
            strict = (idx[:, None] > idx[None, :]).astype(np.float32)
            incl = (idx[:, None] >= idx[None, :]).astype(np.float32)
        m1 = np.zeros((128, 128), np.float32)
        m1[:64, :64] = -strict
        m1[:64, 64:] = incl
        m1[64:, :64] = -strict
        m1[64:, 64:] = incl
        cm[:, 3 + 2 * d, :] = m1
        m2 = np.zeros((128, 128), np.float32)
        m2[:64, :64] = -strict.T
        m2[:64, 64:] = strict.T
        m2[64:, :64] = -strict.T
        m2[64:, 64:] = strict.T
        cm[:, 4 + 2 * d, :] = m2
    rmask = np.ones((128, BT), np.float32)
    rmask[:, ::C] = 0
    return cm, rmask


STOP = None


def build_scan(T, NBATCH):
    nc = bass.Bass("TRN2", target_bir_lowering=False)
    NTOK = T * NBATCH
    zin = nc.dram_tensor("zin", [9, 128, NTOK], F32, kind="ExternalInput").ap()
    pvec = nc.dram_tensor("pvec", [128, NPV], F32, kind="ExternalInput").ap()
    pmat = nc.dram_tensor("pmat", [128, 3, 128], F32, kind="ExternalInput").ap()
    cmat = nc.dram_tensor("cmat", [128, 7, 128], F32, kind="ExternalInput").ap()
    rmk = nc.dram_tensor("rmask", [128, BT], F32, kind="ExternalInput").ap()
    yout = nc.dram_tensor("yout", [2, 128, NTOK], F32, kind="ExternalOutput").ap()
    with ExitStack() as es:
        P = Prog(nc, es)
        try:
            emit_scan(P, zin, pvec, pmat, cmat, rmk, yout, T, NBATCH)
        except StopIteration:
            pass
        P.finish()
        P.emit()
    return nc


def emit_scan(P, zin, pvec, pmat, cmat, rmk, yout, T, NBATCH):
    nblk = T // BT
    sb, ps = P.sb, P.ps
    V, G, A, PE = "vector", "gpsimd", "scalar", "tensor"
    pv = sb("pv", [128, NPV + 8]); r_pv = Res()
    pm = sb("pm", [128, 3, 128]); r_pm = Res()
    cm = sb("cm", [128, 7, 128]); r_cm = Res()
    rm_t = sb("rmaskt", [128, BT]); r_rm = Res()
    P.dma("sync", pv[:, 0:NPV], pvec[:, :], writes=[r_pv])
    P.dma("sync", pm[:], pmat[:, :, :], writes=[r_pm])
    P.dma("sync", cm[:], cmat[:, :, :], writes=[r_cm])
    P.dma("sync", rm_t[:], rmk[:, :], writes=[r_rm])
    ident = cm[:, 0, :]
    bdones = cm[:, 1, :]
    bdavg = cm[:, 2, :]
    hm = sb("hm", [128, 8]); r_hm = Res()
    P.op(V, lambda e: e.tensor_scalar(out=pv[:, NPV:NPV + 6], in0=pv[:, 0:6], scalar1=-1.0, scalar2=1.0,
                                      op0=ALU.mult, op1=ALU.add), reads=[r_pv], writes=[r_hm])
    P.op(V, lambda e: e.tensor_scalar(out=hm[:, 0:6], in0=pv[:, 0:6], scalar1=0.5, scalar2=None, op0=ALU.mult),
         reads=[r_pv], writes=[r_hm])
    P.op(V, lambda e: e.tensor_scalar(out=hm[:, 7:8], in0=pv[:, KA_:KA_ + 1], scalar1=0.5, scalar2=None, op0=ALU.mult),
         reads=[r_pv], writes=[r_hm])
    P.op(V, lambda e: e.tensor_scalar(out=hm[:, 6:7], in0=pv[:, KA_:KA_ + 1], scalar1=-1.0, scalar2=1.0,
                                      op0=ALU.mult, op1=ALU.add), reads=[r_pv], writes=[r_hm])
    epsv = sb("epsv", [128, 2])
    P.op(V, lambda e: e.memset(epsv[:, 0:1], 1e-12), writes=[r_hm])
    P.op(V, lambda e: e.memset(epsv[:, 1:2], GN_EPS), writes=[r_hm])

    def col(j):
        return pv[:, j:j + 1]

    NRAW = 9
    raw = [sb(f"raw{i}", [128, BT + 2]) for i in range(NRAW)]
    r_raw = [Res() for _ in range(NRAW)]
    sh = [sb(f"sh{i}", [128, BT]) for i in range(6)]
    r_sh = [Res() for _ in range(6)]
    tmpA = sb("tmpA", [128, BT]); r_tmpA = Res()
    tmpB = sb("tmpB", [128, BT]); r_tmpB = Res()
    tmpA2 = sb("tmpA2", [128, BT]); r_tmpA2 = Res()
    tmpB2 = sb("tmpB2", [128, BT]); r_tmpB2 = Res()
    th = sb("th", [128, BT]); r_th = Res()
    sig = sb("sig", [128, BT]); r_sig = Res()
    aa = sb("aa", [128, BT]); r_aa = Res()
    af = sb("af", [128, BT]); r_af = Res()
    cumS = sb("cumS", [128, BT]); r_cum = Res()
    cp = sb("cp", [128, BT]); r_cp = Res()
    rmm = sb("rmm", [128, BT]); r_rmm = Res()
    cb = sb("cb", [128, BT]); r_cb = Res()
    eW = sb("eW", [128, BT]); r_eW = Res()
    eWp = sb("eWp", [128, BT]); r_eWp = Res()
    eWi = sb("eWi", [128, BT]); r_eWi = Res()
    eD = sb("eD", [128, BT]); r_eD = Res()
    WCt = sb("WCt", [128, NB]); r_WCt = Res()
    WCs = sb("WCs", [64, NB, 2]); r_WCs = Res()
    kkr = sb("kkr", [128, BT]); r_kkr = Res()
    sq = sb("sq", [128, BT]); r_sq = Res()
    rn = sb("rn", [128, BT]); r_rn = Res()
    kk = sb("kk", [128, BT]); r_kk = Res()
    t1 = sb("t1", [128, BT]); r_t1 = Res()
    kd = sb("kd", [128, BT]); r_kd = Res()
    bb = sb("bb", [128, BT]); r_bb = Res()
    LT = sb("LT", [128, NB, 2, C]); r_LT = Res()
    RT = sb("RT", [128, NB, 2, C]); r_RT = Res()
    bp = sb("bp", [128, BT]); r_bp = Res()
    ktp = sb("ktp", [128, BT]); r_ktp = Res()
    KLin = sb("KLin", [128, NB, 128], BF16); r_KLa = Res(); r_KLb = [Res() for _ in range(2)]
    RB = sb("RB", [128, NB, 128]); r_RBa = Res(); r_RBb = [Res() for _ in range(2)]
    Bs = sb("Bs", [128, NIT, 128]); r_Bs_tl = Res(); r_Bs_tr = Res(); r_Bs_bl = Res(); r_Bs_br = [Res() for _ in range(2)]
    PPa = sb("PPa", [128, NB, 192], BF16); r_PPa = [Res() for _ in range(4)]
    PPb = sb("PPb", [128, NB, 192], BF16); r_PPb = [Res() for _ in range(4)]
    TTl = [sb("TTa", [128, NB, 64], BF16), sb("TTb", [128, NB, 64], BF16)]; TTh = TTl; r_TTl = [[Res() for _ in range(4)], [Res() for _ in range(4)]]
    idst = sb("idst", [128, 64])
    KL = sb("KL", [128, NB, 128]); r_KL = [Res() for _ in range(2)]
    KLs = sb("KLs", [64, NB, 128]); r_KLs = [Res() for _ in range(2)]
    RBs = sb("RBs", [64, NB, 128]); r_RBs = [Res() for _ in range(2)]
    ABQH = sb("ABQH", [128, NIT, 128]); r_AB = [Res() for _ in range(4)]
    Z = sb("Z", [128, NB, 2, 64]); r_Zv = Res(); r_Zs = [Res() for _ in range(NB)]
    STc = sb("STc", [64, 2, 64]); r_ST = Res()
    yf = sb("yf", [128, T]); r_yf = [Res() for _ in range(nblk)]
    cu = sb("cu", [128, BT + 2]); r_cu = Res()
    ysum = eW; r_ysum = r_eW
    ysq = eWp; r_ysq = r_eWp
    m2 = eWi; r_m2 = r_eWi
    yn = eD; r_yn = r_eD
    sg = kkr; r_sg = r_kkr
    rkb = rn; r_rkb = r_rn
    yo = bb; r_yo = r_bb
    hc = bp; r_hc = r_bp
    yc = ktp; r_yc = r_ktp
    pb = [ps(f"pb{i}", [128, 512]) for i in range(8)]
    r_pb = [Res() for _ in range(8)]

    def mm(e, out, lhsT, rhs, start=True, stop=True):
        rp = lhsT.base_partition(); cp_ = out.base_partition()
        if rp or cp_:
            return e.matmul(out, lhsT, rhs, start=start, stop=stop, tile_position=(rp, cp_))
        return e.matmul(out, lhsT, rhs, start=start, stop=stop)

    P.op(V, lambda e: e.tensor_tensor(out=idst[:], in0=cm[:, 0, 0:64], in1=cm[:, 0, 64:128], op=ALU.add), reads=[r_cm], writes=[r_hm])
    for b in range(NBATCH):
        for d in range(2):
            P.op(V, lambda e: e.memset(STc[:], 0.0), writes=[r_ST])
            blks = range(nblk) if d == 0 else range(nblk - 1, -1, -1)
            for blk in blks:
                t0 = blk * BT
                g0 = b * T + t0
                narr = 5 if d == 0 else 9
                arrs = [0, 1, 2, 3, 4] if d == 0 else list(range(9))
                for i in arrs:
                    lo = 1 if blk == 0 else 0
                    hi = BT + 1 if blk == nblk - 1 else BT + 2
                    if blk == 0:
                        P.op(G, lambda e, i=i: e.memset(raw[i][:, 0:1], 0.0), writes=[r_raw[i]])
                    if blk == nblk - 1:
                        P.op(G, lambda e, i=i: e.memset(raw[i][:, BT + 1:BT + 2], 0.0), writes=[r_raw[i]])
                    P.dma("sync", raw[i][:, lo:hi], zin[i, :, g0 - 1 + lo:g0 - 1 + hi], writes=[r_raw[i]])
                shl = [0, 1, 2, 3, 4] if d == 0 else [0, 1, 2, 3, 4, 5]
                for n, i in enumerate(shl):
                    e1 = G if n % 2 == 0 else V
                    tA, rA, tB, rB = (tmpA, r_tmpA, tmpB, r_tmpB) if n % 2 == 0 else (tmpA2, r_tmpA2, tmpB2, r_tmpB2)
                    P.op(e1, lambda e, i=i, tA=tA: e.tensor_tensor(out=tA[:], in0=raw[i][:, 0:BT], in1=raw[i][:, 2:BT + 2],
                                                                   op=ALU.add), reads=[r_raw[i]], writes=[rA])
                    P.op(A, lambda e, i=i, tA=tA, tB=tB: e.activation(out=tB[:], in_=tA[:], func=AF.Copy, scale=hm[:, i:i + 1]),
                         reads=[rA, r_hm], writes=[rB])
                    P.op(V, lambda e, i=i, tB=tB: e.scalar_tensor_tensor(out=sh[i][:], in0=raw[i][:, 1:BT + 1],
                                                                         scalar=pv[:, NPV + i:NPV + i + 1], in1=tB[:],
                                                                         op0=ALU.mult, op1=ALU.add),
                         reads=[r_raw[i], rB, r_hm], writes=[r_sh[i]])
                P.stage(1)
                rs, ks, vs, hws, has, hgs = sh
                ds = slice(64 * d, 64 * d + 64)
                P.op(A, lambda e: e.activation(out=th[:], in_=hws[:], func=AF.Tanh), reads=[r_sh[3]], writes=[r_th])
                P.op(PE, lambda e: mm(e, pb[0][:, :], pm[ds, 0, :], th[ds, :]), reads=[r_pm, r_th], writes=[r_pb[0]])
                P.op(PE, lambda e: mm(e, pb[1][:, :], pm[ds, 1, :], has[ds, :]), reads=[r_pm, r_sh[4]], writes=[r_pb[1]])
                P.op(A, lambda e: e.activation(out=sig[:], in_=pb[0][:, :], func=AF.Sigmoid, bias=col(W0F + d), scale=1.0),
                     reads=[r_pb[0], r_pv], writes=[r_sig])
                P.op(A, lambda e: e.activation(out=aa[:], in_=pb[1][:, :], func=AF.Sigmoid, bias=col(A0F + d), scale=1.0),
                     reads=[r_pb[1], r_pv], writes=[r_aa])
                P.stage(2)
                P.op(V, lambda e: e.tensor_tensor_scan(out=cumS[:], data0=rm_t[:], data1=sig[:], initial=0.0,
                                                       op0=ALU.mult, op1=ALU.add), reads=[r_rm, r_sig], writes=[r_cum])
                P.op(V, lambda e: e.tensor_tensor(out=cp[:], in0=cumS[:], in1=sig[:], op=ALU.subtract),
                     reads=[r_cum, r_sig], writes=[r_cp])
                cum3 = cumS[:].rearrange("p (c t) -> p c t", t=C)
                tot_b = cum3[:, :, C - 1:C].to_broadcast([128, NB, C])
                P.op(V, lambda e: e.tensor_tensor(out=rmm[:].rearrange("p (c t) -> p c t", t=C), in0=tot_b, in1=cum3,
                                                  op=ALU.subtract), reads=[r_cum], writes=[r_rmm])
                if d == 0:
                    c_cum, c_prev, c_rem = cumS, cp, rmm
                    rr_cum, rr_prev, rr_rem = r_cum, r_cp, r_rmm
                else:
                    P.op(V, lambda e: e.tensor_tensor(out=cb[:], in0=rmm[:], in1=sig[:], op=ALU.add),
                         reads=[r_rmm, r_sig], writes=[r_cb])
                    c_cum, c_prev, c_rem = cb, rmm, cp
                    rr_cum, rr_prev, rr_rem = r_cb, r_rmm, r_cp
                P.op(A, lambda e: e.activation(out=eW[:], in_=c_cum[:], func=AF.Exp, scale=-DEC), reads=[rr_cum], writes=[r_eW])
                P.op(A, lambda e: e.activation(out=eWp[:], in_=c_prev[:], func=AF.Exp, scale=-DEC), reads=[rr_prev], writes=[r_eWp])
                P.op(A, lambda e: e.activation(out=eWi[:], in_=c_cum[:], func=AF.Exp, scale=DEC), reads=[rr_cum], writes=[r_eWi])
                P.op(A, lambda e: e.activation(out=eD[:], in_=c_rem[:], func=AF.Exp, scale=-DEC), reads=[rr_rem], writes=[r_eD])
                P.op(A, lambda e: e.activation(out=WCt[:, :], in_=cum3[:, :, C - 1], func=AF.Exp, scale=-DEC),
                     reads=[r_cum], writes=[r_WCt])
                P.op(V, lambda e: e.tensor_copy(out=WCs[:, :, 0], in_=WCt[0:64, :]), reads=[r_WCt], writes=[r_WCs])
                P.op(V, lambda e: e.tensor_copy(out=WCs[:, :, 1], in_=WCt[64:128, :]), reads=[r_WCt], writes=[r_WCs])
                P.stage(3)
                P.op(V, lambda e: e.tensor_scalar(out=kkr[:], in0=ks[:], scalar1=col(KK_), scalar2=None, op0=ALU.mult),
                     reads=[r_sh[1], r_pv], writes=[r_kkr])
                P.op(G, lambda e: e.tensor_tensor(out=sq[:], in0=kkr[:], in1=kkr[:], op=ALU.mult), reads=[r_kkr], writes=[r_sq])
                P.op(PE, lambda e: mm(e, pb[2][:, :], bdones, sq[:]), reads=[r_cm, r_sq], writes=[r_pb[2]])
                P.op(A, lambda e: e.activation(out=rn[:], in_=pb[2][:, :], func=AF.Sqrt, bias=epsv[:, 0:1], scale=1.0),
                     reads=[r_pb[2], r_hm], writes=[r_rn])
                P.op(V, lambda e: e.reciprocal(out=rn[:], in_=rn[:]), reads=[r_rn], writes=[r_rn])
                P.op(G, lambda e: e.tensor_tensor(out=kk[:], in0=kkr[:], in1=rn[:], op=ALU.mult), reads=[r_kkr, r_rn], writes=[r_kk])
                P.op(V, lambda e: e.tensor_scalar(out=t1[:], in0=aa[:], scalar1=col(KA_), scalar2=hm[:, 6:7],
                                                  op0=ALU.mult, op1=ALU.add), reads=[r_aa, r_pv, r_hm], writes=[r_t1])
                P.op(G, lambda e: e.tensor_tensor(out=kd[:], in0=ks[:], in1=t1[:], op=ALU.mult), reads=[r_sh[1], r_t1], writes=[r_kd])
                P.op(G, lambda e: e.tensor_tensor(out=bb[:], in0=kk[:], in1=aa[:], op=ALU.mult), reads=[r_kk, r_aa], writes=[r_bb])
                P.stage(4)
                P.op(V, lambda e: e.tensor_tensor(out=LT[:, :, 0, :], in0=bb[:].rearrange("p (c t) -> p c t", t=C), in1=eWi[:].rearrange("p (c t) -> p c t", t=C), op=ALU.mult), reads=[r_bb, r_eWi], writes=[r_LT])
                P.op(G, lambda e: e.tensor_tensor(out=LT[:, :, 1, :], in0=kd[:].rearrange("p (c t) -> p c t", t=C), in1=eWi[:].rearrange("p (c t) -> p c t", t=C), op=ALU.mult), reads=[r_kd, r_eWi], writes=[r_LT])
                P.op(V, lambda e: e.tensor_tensor(out=RT[:, :, 0, :], in0=kk[:].rearrange("p (c t) -> p c t", t=C), in1=eWp[:].rearrange("p (c t) -> p c t", t=C), op=ALU.mult), reads=[r_kk, r_eWp], writes=[r_RT])
                P.op(G, lambda e: e.tensor_tensor(out=RT[:, :, 1, :], in0=rs[:].rearrange("p (c t) -> p c t", t=C), in1=eW[:].rearrange("p (c t) -> p c t", t=C), op=ALU.mult), reads=[r_sh[0], r_eW], writes=[r_RT])
                P.op(V, lambda e: e.tensor_tensor(out=bp[:], in0=bb[:], in1=eD[:], op=ALU.mult), reads=[r_bb, r_eD], writes=[r_bp])
                P.op(G, lambda e: e.tensor_tensor(out=ktp[:], in0=kd[:], in1=eD[:], op=ALU.mult), reads=[r_kd, r_eD], writes=[r_ktp])
                P.stage(5)
                side = []

                def transp(src_ap_fn, rsrc, bank0, evac):
                    def f(e):
                        ins = None
                        for c in range(NB):
                            o = pb[bank0 + c // 4][0:64, (c % 4) * 128:(c % 4 + 1) * 128]
                            ins = e.transpose(o, src_ap_fn(c), ident)
                        return ins
                    def thunk():
                        P.op(PE, f, reads=[rsrc, r_cm], writes=[r_pb[bank0], r_pb[bank0 + 1]])
                        evac()
                    side.append(thunk)

                def ps2(bank0):
                    return [pb[bank0 + j][0:64, :].rearrange("p (c n) -> p c n", n=128) for j in range(2)]

                def transp_h(src_ap_fn, rsrc, bank, dst_tile, r_dst):
                    def f(e):
                        ins = None
                        for c in range(NB):
                            for h in range(2):
                                hs = slice(64 * h, 64 * h + 64)
                                ins = mm(e, pb[bank][hs, c * 64:(c + 1) * 64], src_ap_fn(c, hs), ident[hs, hs])
                        return ins

                    def thunk():
                        P.op(PE, f, reads=[rsrc, r_cm], writes=[r_pb[bank]])
                        src = pb[bank][:, :].rearrange("p (c k) -> p c k", k=64)
                        P.op(V, lambda e, src=src: e.tensor_copy(out=dst_tile[:, :, 0:64], in_=src), reads=[r_pb[bank]], writes=[r_dst])
                    side.append(thunk)
                transp_h(lambda c, hs: RT[hs, c, 0, :], r_RT, 4, KLin, r_KLa)
                transp_h(lambda c, hs: bp[hs, c * C:(c + 1) * C], r_bp, 5, RB, r_RBa)

                def ev_ktp():
                    for j in range(2):
                        src = ps2(6)[j].rearrange("p c (h k) -> p c h k", h=2)
                        dst = Bs[64:128, j * 8:(j + 1) * 8, 0:64].rearrange("p (c h) k -> p c h k", h=2)
                        P.op(V if j == 0 else A,
                             (lambda e, s=src, d_=dst: e.tensor_copy(out=d_, in_=s)) if j == 0 else
                             (lambda e, s=src, d_=dst: e.activation(out=d_, in_=s, func=AF.Copy)),
                             reads=[r_pb[6 + j]], writes=[r_Bs_bl])
                transp(lambda c: ktp[:, c * C:(c + 1) * C], r_ktp, 6, ev_ktp)

                def ev_v():
                    for j in range(2):
                        src = ps2(6)[j].rearrange("p c (h k) -> p c h k", h=2)
                        dst = Z[64:128, j * 4:(j + 1) * 4, :, :]
                        P.op(V if j == 0 else A,
                             (lambda e, s=src, d_=dst: e.tensor_copy(out=d_, in_=s)) if j == 0 else
                             (lambda e, s=src, d_=dst: e.activation(out=d_, in_=s, func=AF.Copy)),
                             reads=[r_pb[6 + j]], writes=[r_Zv])
                transp(lambda c: vs[:, c * C:(c + 1) * C], r_sh[2], 6, ev_v)
                P.stage(6)
                idb = ident[0:64, 0:64].unsqueeze(1).to_broadcast([64, NIT, 64])
                wcb = WCs[:].rearrange("p c h -> p (c h)").unsqueeze(2).to_broadcast([64, NIT, 64])
                def bs_thunk():
                    P.op(G, lambda e: e.tensor_tensor(out=Bs[0:64, :, 0:64], in0=idb, in1=wcb, op=ALU.mult),
                         reads=[r_cm, r_WCs], writes=[r_Bs_tl])
                    for h in range(2):
                        dst = Bs[0:64, :, 64:128].rearrange("p (c h) t -> p c h t", h=2)[:, :, h, :]
                        src = RT[64 * h:64 * h + 64, :, 1, :]
                        P.op(G, lambda e, s=src, d_=dst: e.tensor_copy(out=d_, in_=s), reads=[r_RT], writes=[r_Bs_tr])
                side.append(bs_thunk)
                P.stage(7)
                m1 = cm[:, 3 + 2 * d, :]
                m2_ = cm[:, 4 + 2 * d, :]
                for grp in range(2):
                    bk1, bk2, bk3 = 0, 1, 2 + grp

                    def fg(e, grp=grp, bk1=bk1, bk2=bk2, bk3=bk3):
                        ins = None
                        for q in range(4):
                            c = grp * 4 + q
                            for h in range(2):
                                hs = slice(64 * h, 64 * h + 64)
                                Lc = LT[hs, c].rearrange("p a t -> p (a t)")
                                Rc = RT[hs, c].rearrange("p a t -> p (a t)")
                                mm(e, pb[bk1][hs, q * 128:(q + 1) * 128], LT[hs, c, 0, :], Rc)
                                mm(e, pb[bk2][hs, q * 128:(q + 1) * 128], RT[hs, c, 0, :], Lc)
                                ins = mm(e, pb[bk3][hs, q * 64:(q + 1) * 64], LT[hs, c, 1, :], RT[hs, c, 1, :])
                        return ins
                    P.op(PE, fg, reads=[r_LT, r_RT], writes=[r_pb[bk1], r_pb[bk2], r_pb[bk3]])
                    cs4 = slice(grp * 4, grp * 4 + 4)
                    its8 = slice(grp * 8, grp * 8 + 8)
                    g1 = pb[bk1][:, :].rearrange("p (q n) -> p q n", n=128)
                    g2 = pb[bk2][:, :].rearrange("p (q n) -> p q n", n=128)
                    g3 = pb[bk3][64:128, :].rearrange("p (q n) -> p q n", n=64)
                    mTL = m1[:, 0:64].unsqueeze(1).to_broadcast([128, 4, 64])
                    mTR = m1[:, 64:128].unsqueeze(1).to_broadcast([128, 4, 64])
                    mBR = m1[64:128, 64:128].unsqueeze(1).to_broadcast([64, 8, 64])
                    m2L = m2_[:, 0:64].unsqueeze(1).to_broadcast([128, 4, 64])
                    m2R = m2_[:, 64:128].unsqueeze(1).to_broadcast([128, 4, 64])
                    rpp = [r_PPa[2 * grp], r_PPa[2 * grp + 1]]
                    P.op(V, lambda e, cs4=cs4, g1=g1, mTL=mTL: e.tensor_tensor(out=PPa[:, cs4, 0:64], in0=g1[:, :, 0:64], in1=mTL, op=ALU.mult),
                         reads=[r_pb[bk1], r_cm], writes=rpp)
                    P.op(V, lambda e, cs4=cs4, g1=g1, mTR=mTR: e.tensor_tensor(out=RB[:, cs4, 64:128], in0=g1[:, :, 64:128], in1=mTR, op=ALU.mult),
                         reads=[r_pb[bk1], r_cm], writes=[r_RBb[grp]])
                    for h in range(2):
                        hs = slice(64 * h, 64 * h + 64)
                        g3h = pb[bk3][hs, 0:256].rearrange("p (q n) -> p q n", n=64)
                        mh = m1[hs, 64:128].unsqueeze(1).to_broadcast([64, 4, 64])
                        dst = Bs[64:128, its8, 64:128].rearrange("p (c h) t -> p c h t", h=2)[:, :, h, :]
                        P.op(V, lambda e, g3h=g3h, mh=mh, dst=dst: e.tensor_tensor(out=dst, in0=g3h, in1=mh, op=ALU.mult),
                             reads=[r_pb[bk3], r_cm], writes=[r_Bs_br[grp]])
                    P.op(V, lambda e, cs4=cs4, g2=g2, m2L=m2L: e.tensor_tensor(out=PPa[:, cs4, 64:128], in0=g2[:, :, 0:64], in1=m2L, op=ALU.mult),
                         reads=[r_pb[bk2], r_cm], writes=rpp)
                    P.op(V, lambda e, cs4=cs4, g2=g2, m2R=m2R: e.tensor_tensor(out=KLin[:, cs4, 64:128], in0=g2[:, :, 64:128], in1=m2R, op=ALU.mult),
                         reads=[r_pb[bk2], r_cm], writes=[r_KLb[grp]])
                P.stage(8)
                idb16 = idst[:].unsqueeze(1).to_broadcast([128, NB, 64])
                P.op(V, lambda e: e.tensor_tensor(out=TTl[1][:], in0=PPa[:, :, 0:64], in1=idb16, op=ALU.add),
                     reads=list(r_PPa) + [r_hm], writes=list(r_TTl[1]))
                cur, r_cur, nxt, r_nxt = PPa, r_PPa, PPb, r_PPb
                for s in range(1, 7):
                    for grp in range(4):
                        bk = grp % 4
                        cs2 = slice(grp * 2, grp * 2 + 2)

                        def fi(e, s=s, grp=grp, bk=bk, cur=cur):
                            ins = None
                            for q in range(2):
                                c = grp * 2 + q
                                for h in range(2):
                                    hs = slice(64 * h, 64 * h + 64)
                                    o = pb[bk][hs, q * 192:(q + 1) * 192]
                                    Pm = cur[hs, c, 0:64]
                                    PmT = cur[hs, c, 64:128]
                                    if s <= 5:
                                        ins = mm(e, o[:, 0:64], PmT, Pm)
                                        ins = mm(e, o[:, 64:128], Pm, PmT)
                                    if s >= 2:
                                        ins = mm(e, o[:, 128:192], PmT, TTl[(s - 1) % 2][hs, c, :])
                            return ins
                        rds = [r_cur[grp]] + ([r_TTl[(s - 1) % 2][grp]] if s >= 2 else [])
                        P.op(PE, fi, reads=rds, writes=[r_pb[bk]])
                        o3 = pb[bk][:, 0:384].rearrange("p (q n) -> p q n", n=192)
                        lo_c = 0 if s <= 5 else 128
                        hi_c = 192 if s >= 2 else 128
                        if grp % 2:
                            P.op(A, lambda e, cs2=cs2, o3=o3, nxt=nxt, lo_c=lo_c, hi_c=hi_c: e.activation(out=nxt[:, cs2, lo_c:hi_c], in_=o3[:, :, lo_c:hi_c], func=AF.Copy),
                                 reads=[r_pb[bk]], writes=[r_nxt[grp]])
                        else:
                            P.op(V, lambda e, cs2=cs2, o3=o3, nxt=nxt, lo_c=lo_c, hi_c=hi_c: e.tensor_copy(out=nxt[:, cs2, lo_c:hi_c], in_=o3[:, :, lo_c:hi_c]),
                                 reads=[r_pb[bk]], writes=[r_nxt[grp]])
                        if s >= 2:
                            P.op(G if grp % 2 == 0 else V, lambda e, cs2=cs2, nxt=nxt, s=s: e.tensor_tensor(out=TTl[s % 2][:, cs2, :], in0=nxt[:, cs2, 128:192], in1=TTl[(s - 1) % 2][:, cs2, :], op=ALU.add),
                                 reads=[r_nxt[grp], r_TTl[(s - 1) % 2][grp]], writes=[r_TTl[s % 2][grp]])
                        if side:
                            side.pop(0)()
                    cur, r_cur, nxt, r_nxt = nxt, r_nxt, cur, r_cur
                while side:
                    side.pop(0)()
                P.stage(9)
                for grp in range(2):
                    bk = 6 + grp % 2

                    def f7(e, grp=grp, bk=bk):
                        ins = None
                        for q in range(4):
                            c = grp * 4 + q
                            for h in range(2):
                                hs = slice(64 * h, 64 * h + 64)
                                ins = mm(e, pb[bk][hs, q * 128:(q + 1) * 128], TTl[0][hs, c, :], KLin[hs, c, :])
                        return ins
                    P.op(PE, f7, reads=[r_TTl[0][2 * grp], r_TTl[0][2 * grp + 1], r_KLa, r_KLb[grp]], writes=[r_pb[bk]])
                    cs4 = slice(grp * 4, grp * 4 + 4)
                    src = pb[bk][:, :].rearrange("p (q n) -> p q n", n=128)
                    P.op(A, lambda e, cs4=cs4, src=src: e.activation(out=KL[:, cs4, :], in_=src, func=AF.Copy),
                         reads=[r_pb[bk]], writes=[r_KL[grp]])
                for g2_ in range(2):
                    c4 = slice(g2_ * 4, g2_ * 4 + 4)
                    P.op(V, lambda e, c4=c4: e.tensor_copy(out=KLs[:, c4, :], in_=KL[64:128, c4, :]), reads=[r_KL[g2_]], writes=[r_KLs[g2_]])
                    P.op(G, lambda e, c4=c4: e.tensor_copy(out=RBs[:, c4, :], in_=RB[64:128, c4, :]), reads=[r_RBa, r_RBb[g2_]], writes=[r_RBs[g2_]])
                for grp in range(NIT // 4):
                    bk = grp % 2

                    def f8(e, grp=grp, bk=bk):
                        ins = None
                        for q in range(4):
                            it = grp * 4 + q
                            ins = mm(e, pb[bk][:, q * 128:(q + 1) * 128], (KLs if it % 2 else KL)[0:64, it // 2, :], (RBs if it % 2 else RB)[0:64, it // 2, :])
                        return ins
                    P.op(PE, f8, reads=[r_KL[grp // 2], r_KLs[grp // 2], r_RBs[grp // 2], r_RBa, r_RBb[grp // 2]], writes=[r_pb[bk]])
                    its = slice(grp * 4, grp * 4 + 4)
                    src = pb[bk][:, :].rearrange("p (q n) -> p q n", n=128)
                    P.op(V, lambda e, its=its, src=src: e.scalar_tensor_tensor(out=ABQH[:, its, :], in0=src, scalar=-1.0, in1=Bs[:, its, :], op0=ALU.mult, op1=ALU.add),
                         reads=[r_pb[bk], r_Bs_tl, r_Bs_tr, r_Bs_bl, r_Bs_br[grp // 2]], writes=[r_AB[grp]])
                P.stage(10)
                order = list(range(NB)) if d == 0 else list(range(NB - 1, -1, -1))
                P.op(V, lambda e, c0=order[0]: e.tensor_copy(out=Z[0:64, c0, :, :], in_=STc[:]), reads=[r_ST], writes=[r_Zs[order[0]]])
                ybank = 7
                for n, c in enumerate(order):
                    sbk = 2 + n % 2

                    def fs(e, c=c, sbk=sbk):
                        ins = None
                        for h in range(2):
                            it = c * 2 + h
                            ins = mm(e, pb[sbk][0:64, h * 64:(h + 1) * 64], ABQH[:, it, 0:64], Z[:, c, h, :])
                        return ins
                    P.op(PE, fs, reads=[r_AB[c // 2], r_Zv, r_Zs[c]], writes=[r_pb[sbk]])
                    src = pb[sbk][0:64, 0:128].rearrange("p (h v) -> p h v", h=2)
                    if n < NB - 1:
                        cn = order[n + 1]
                        P.op(A, lambda e, cn=cn, src=src: e.activation(out=Z[0:64, cn, :, :], in_=src, func=AF.Copy),
                             reads=[r_pb[sbk]], writes=[r_Zs[cn]])
                    else:
                        P.op(A, lambda e, src=src: e.activation(out=STc[:], in_=src, func=AF.Copy),
                             reads=[r_pb[sbk]], writes=[r_ST])

                    def fy(e, c=c):
                        ins = None
                        for h in range(2):
                            it = c * 2 + h
                            ins = mm(e, pb[ybank][64 * h:64 * h + 64, c * C:(c + 1) * C], Z[:, c, h, :], ABQH[:, it, 64:128])
                        return ins
                    P.op(PE, fy, reads=[r_AB[c // 2], r_Zv, r_Zs[c]], writes=[r_pb[ybank]])
                P.stage(11)
                if d == 0:
                    P.op(A, lambda e, t0=t0: e.activation(out=yf[:, t0:t0 + BT], in_=pb[ybank][:, :], func=AF.Copy),
                         reads=[r_pb[ybank]], writes=[r_yf[blk]])
                    continue
                P.op(V, lambda e, t0=t0: e.tensor_tensor(out=ysum[:], in0=pb[ybank][:, :], in1=yf[:, t0:t0 + BT], op=ALU.add),
                     reads=[r_pb[ybank], r_yf[blk]], writes=[r_ysum])
                P.op(A, lambda e: e.activation(out=ysq[:], in_=ysum[:], func=AF.Square), reads=[r_ysum], writes=[r_ysq])
                P.op(PE, lambda e: mm(e, pb[0][:, :], bdavg, ysum[:]), reads=[r_cm, r_ysum], writes=[r_pb[0]])
                P.op(PE, lambda e: mm(e, pb[1][:, :], bdavg, ysq[:]), reads=[r_cm, r_ysq], writes=[r_pb[1]])
                P.op(A, lambda e: e.activation(out=m2[:], in_=pb[0][:, :], func=AF.Square), reads=[r_pb[0]], writes=[r_m2])
                P.op(V, lambda e: e.tensor_tensor(out=m2[:], in0=pb[1][:, :], in1=m2[:], op=ALU.subtract), reads=[r_pb[1], r_m2], writes=[r_m2])
                P.op(A, lambda e: e.activation(out=m2[:], in_=m2[:], func=AF.Sqrt, bias=epsv[:, 1:2], scale=1.0),
                     reads=[r_m2, r_hm], writes=[r_m2])
                P.op(V, lambda e: e.reciprocal(out=m2[:], in_=m2[:]), reads=[r_m2], writes=[r_m2])
                P.op(V, lambda e: e.scalar_tensor_tensor(out=yn[:], in0=pb[0][:, :], scalar=-1.0, in1=ysum[:], op0=ALU.mult, op1=ALU.add), reads=[r_ysum, r_pb[0]], writes=[r_yn])
                P.op(V, lambda e: e.tensor_tensor(out=yn[:], in0=yn[:], in1=m2[:], op=ALU.mult), reads=[r_yn, r_m2], writes=[r_yn])
                P.op(V, lambda e: e.tensor_scalar(out=yn[:], in0=yn[:], scalar1=col(GNG), scalar2=col(GNB), op0=ALU.mult, op1=ALU.add),
                     reads=[r_yn, r_pv], writes=[r_yn])
                P.op(PE, lambda e: mm(e, pb[2][:, :], pm[0:64, 1, :], has[0:64, :]), reads=[r_pm, r_sh[4]], writes=[r_pb[2]])
                P.op(A, lambda e: e.activation(out=af[:], in_=pb[2][:, :], func=AF.Sigmoid, bias=col(A0F), scale=1.0),
                     reads=[r_pb[2], r_pv], writes=[r_af])
                P.op(G, lambda e: e.tensor_tensor(out=af[:], in0=af[:], in1=aa[:], op=ALU.add), reads=[r_af, r_aa], writes=[r_af])
                P.op(V, lambda e: e.tensor_scalar(out=t1[:], in0=af[:], scalar1=hm[:, 7:8], scalar2=hm[:, 6:7], op0=ALU.mult, op1=ALU.add),
                     reads=[r_af, r_pv, r_hm], writes=[r_t1])
                P.op(G, lambda e: e.tensor_tensor(out=rkb[:], in0=ks[:], in1=t1[:], op=ALU.mult), reads=[r_sh[1], r_t1], writes=[r_rkb])
                P.op(V, lambda e: e.scalar_tensor_tensor(out=rkb[:], in0=rkb[:], scalar=col(RK_), in1=rs[:], op0=ALU.mult, op1=ALU.mult),
                     reads=[r_rkb, r_sh[0], r_pv], writes=[r_rkb])
                P.op(PE, lambda e: mm(e, pb[3][:, :], bdones, rkb[:]), reads=[r_cm, r_rkb], writes=[r_pb[3]])
                P.op(V, lambda e: e.tensor_tensor(out=rkb[:], in0=pb[3][:, :], in1=vs[:], op=ALU.mult), reads=[r_pb[3], r_sh[2]], writes=[r_rkb])
                P.op(V, lambda e: e.tensor_tensor(out=yn[:], in0=yn[:], in1=rkb[:], op=ALU.add), reads=[r_yn, r_rkb], writes=[r_yn])
                P.op(A, lambda e: e.activation(out=sg[:], in_=hgs[:], func=AF.Sigmoid), reads=[r_sh[5]], writes=[r_sg])
                P.op(PE, lambda e: mm(e, pb[4][:, :], pm[:, 2, :], sg[:]), reads=[r_pm, r_sg], writes=[r_pb[4]])
                P.op(V, lambda e: e.tensor_tensor(out=yo[:], in0=pb[4][:, :], in1=yn[:], op=ALU.mult), reads=[r_yn, r_pb[4]], writes=[r_yo])
                P.dma("sync", yout[0, :, g0:g0 + BT], yo[:], reads=[r_yo], is_out=True)
                P.op(G, lambda e: e.tensor_tensor(out=cu[:], in0=raw[7][:], in1=raw[8][:], op=ALU.mult), reads=[r_raw[7], r_raw[8]], writes=[r_cu])
                P.op(V, lambda e: e.tensor_scalar(out=hc[:], in0=cu[:, 0:BT], scalar1=col(CW0), scalar2=None, op0=ALU.mult),
                     reads=[r_cu, r_pv], writes=[r_hc])
                P.op(V, lambda e: e.scalar_tensor_tensor(out=hc[:], in0=cu[:, 1:BT + 1], scalar=col(CW1), in1=hc[:], op0=ALU.mult, op1=ALU.add),
                     reads=[r_cu, r_hc, r_pv], writes=[r_hc])
                P.op(V, lambda e: e.scalar_tensor_tensor(out=hc[:], in0=cu[:, 2:BT + 2], scalar=col(CW2), in1=hc[:], op0=ALU.mult, op1=ALU.add),
                     reads=[r_cu, r_hc, r_pv], writes=[r_hc])
                P.op(G, lambda e: e.tensor_tensor(out=yc[:], in0=hc[:], in1=raw[6][:, 1:BT + 1], op=ALU.mult), reads=[r_hc, r_raw[6]], writes=[r_yc])
                P.dma("sync", yout[1, :, g0:g0 + BT], yc[:], reads=[r_yc], is_out=True)


D = 2048
KC = 16
F = 5504
FC = 43
NT = 1024
TT = 512
NTT = NT // TT
D_IN = 13696
RC = 6528
QC0 = 6528
GC0 = 7552
ALPHA = (2 * 2) ** 0.25
LN_EPS = 1e-5
V, G, A, PE = "vector", "gpsimd", "scalar", "tensor"
FGROUPS = [(0, 11), (11, 22), (22, 33), (33, 43)]


class DenseCtx:
    def __init__(self, P):
        self.P = P
        sb, ps = P.sb, P.ps
        self.pb = [ps(f"pb{i}", [128, 512]) for i in range(8)]
        self.r_pb = [Res() for _ in range(8)]
        self.bank = 0
        self.onesf = sb("onesf", [128, 128]); self.r_c = Res()
        self.onesb = sb("onesb", [128, 128], BF16)
        self.epsv = sb("epsv", [128, 1])
        P.op(V, lambda e: e.memset(self.onesf[:], 1.0 / D), writes=[self.r_c])
        P.op(V, lambda e: e.memset(self.onesb[:], 1.0), writes=[self.r_c])
        P.op(V, lambda e: e.memset(self.epsv[:], LN_EPS), writes=[self.r_c])
        self.wA = [sb(f"wA{i}", [128, 16, 256], BF16) for i in range(3)]
        self.r_wA = [Res() for _ in range(3)]
        self.wA_i = 0
        self.wD = [sb(f"wD{i}", [128, 11, 256], BF16) for i in range(2)]
        self.r_wD = [Res() for _ in range(2)]
        self.wD_i = 0
        self.tmp = [sb(f"tmp{i}", [128, 512]) for i in range(4)]
        self.r_tmp = [Res() for _ in range(4)]
        self.tmp_i = 0
        self.stat = [sb(f"stat{i}", [128, 512]) for i in range(3)]
        self.r_stat = [Res() for _ in range(3)]

    def nbank(self):
        b = self.bank
        self.bank = (b + 1) % 8
        return b

    def ntmp(self):
        i = self.tmp_i
        self.tmp_i = (i + 1) % 4
        return i

    def load_wA(self, w_ap, kc, mcols):
        i = self.wA_i
        self.wA_i = (i + 1) % 3
        t = self.wA[i]
        self.P.dma("gpsimd", t[:, 0:kc, 0:mcols], w_ap.rearrange("(k p) m -> p k m", p=128), writes=[self.r_wA[i]])
        return t, self.r_wA[i]

    def load_wD(self, w_ap, kc, mcols):
        i = self.wD_i
        self.wD_i = (i + 1) % 2
        t = self.wD[i]
        self.P.dma("gpsimd", t[:, 0:kc, 0:mcols], w_ap.rearrange("(k p) m -> p k m", p=128), writes=[self.r_wD[i]])
        return t, self.r_wD[i]


def mm_group(P, ctx, bank, pairs, reads, n=512, mrows=128):
    def f(e):
        ins = None
        L = len(pairs)
        for i, (l, r) in enumerate(pairs):
            ins = e.matmul(ctx.pb[bank][0:mrows, 0:n], l, r, start=(i == 0), stop=(i == L - 1))
        return ins
    P.op(PE, f, reads=reads, writes=[ctx.r_pb[bank]])


def layer_norm_fm(P, ctx, X, r_X, XB, r_XB, gb, r_gb, gcol, bcol, kc_n=KC, ntok=NT, write_x=True):
    tts = [(t0, min(TT, ntok - t0)) for t0 in range(0, ntok, TT)]
    for (t0, n) in tts:
        b_sum = ctx.nbank()
        mm_group(P, ctx, b_sum, [(ctx.onesf[:], X[:, kc, t0:t0 + n]) for kc in range(kc_n)], [ctx.r_c] + [r_X[kc] for kc in range(kc_n)], n=n)
        b_sq = ctx.nbank()
        sqs = []
        for kc in range(kc_n):
            ti = ctx.ntmp()
            P.op(A if kc % 2 else G, (lambda e, ti=ti, kc=kc: e.activation(out=ctx.tmp[ti][:, 0:n], in_=X[:, kc, t0:t0 + n], func=AF.Square)) if kc % 2 else
                 (lambda e, ti=ti, kc=kc: e.tensor_tensor(out=ctx.tmp[ti][:, 0:n], in0=X[:, kc, t0:t0 + n], in1=X[:, kc, t0:t0 + n], op=ALU.mult)),
                 reads=[r_X[kc]], writes=[ctx.r_tmp[ti]])
            P.op(PE, lambda e, ti=ti, kc=kc: e.matmul(ctx.pb[b_sq][:, 0:n], ctx.onesf[:], ctx.tmp[ti][:, 0:n], start=(kc == 0), stop=(kc == kc_n - 1)),
                 reads=[ctx.r_c, ctx.r_tmp[ti]], writes=[ctx.r_pb[b_sq]])
        mean_ps = ctx.pb[b_sum][:, 0:n]
        e2_ps = ctx.pb[b_sq][:, 0:n]
        m2, rstd, nmr = ctx.stat[0][:, 0:n], ctx.stat[1][:, 0:n], ctx.stat[2][:, 0:n]
        P.op(A, lambda e: e.activation(out=m2, in_=mean_ps, func=AF.Square), reads=[ctx.r_pb[b_sum]], writes=[ctx.r_stat[0]])
        P.op(V, lambda e: e.tensor_tensor(out=rstd, in0=e2_ps, in1=m2, op=ALU.subtract), reads=[ctx.r_pb[b_sq], ctx.r_stat[0]], writes=[ctx.r_stat[1]])
        P.op(A, lambda e: e.activation(out=rstd, in_=rstd, func=AF.Sqrt, bias=ctx.epsv[:, 0:1], scale=1.0), reads=[ctx.r_stat[1], ctx.r_c], writes=[ctx.r_stat[1]])
        P.op(V, lambda e: e.reciprocal(out=rstd, in_=rstd), reads=[ctx.r_stat[1]], writes=[ctx.r_stat[1]])
        P.op(V, lambda e: e.scalar_tensor_tensor(out=nmr, in0=mean_ps, scalar=-1.0, in1=rstd, op0=ALU.mult, op1=ALU.mult),
             reads=[ctx.r_pb[b_sum], ctx.r_stat[1]], writes=[ctx.r_stat[2]])
        for kc in range(kc_n):
            ti = ctx.ntmp()
            t = ctx.tmp[ti][:, 0:n]
            P.op(V, lambda e, t=t, kc=kc: e.tensor_tensor(out=t, in0=X[:, kc, t0:t0 + n], in1=rstd, op=ALU.mult),
                 reads=[r_X[kc], ctx.r_stat[1]], writes=[ctx.r_tmp[ti]])
            P.op(G, lambda e, t=t: e.tensor_tensor(out=t, in0=t, in1=nmr, op=ALU.add), reads=[ctx.r_tmp[ti], ctx.r_stat[2]], writes=[ctx.r_tmp[ti]])
            if write_x:
                P.op(V, lambda e, t=t, kc=kc: e.tensor_scalar(out=X[:, kc, t0:t0 + n], in0=t, scalar1=gb[:, gcol, kc:kc + 1], scalar2=gb[:, bcol, kc:kc + 1],
                                                              op0=ALU.mult, op1=ALU.add), reads=[ctx.r_tmp[ti], r_gb], writes=[r_X[kc]])
                P.op(A, lambda e, kc=kc: e.activation(out=XB[:, kc, t0:t0 + n], in_=X[:, kc, t0:t0 + n], func=AF.Copy), reads=[r_X[kc]], writes=[r_XB[kc]])
            else:
                P.op(V, lambda e, t=t, kc=kc: e.tensor_scalar(out=XB[:, kc, t0:t0 + n], in0=t, scalar1=gb[:, gcol, kc:kc + 1], scalar2=gb[:, bcol, kc:kc + 1],
                                                              op0=ALU.mult, op1=ALU.add), reads=[ctx.r_tmp[ti], r_gb], writes=[r_XB[kc]])


def ffn(P, ctx, X, r_X, XB, r_XB, H, r_H, wg, wu, wd):
    for kc in range(KC):
        P.op(G, lambda e, kc=kc: e.tensor_scalar(out=X[:, kc, :], in0=X[:, kc, :], scalar1=ALPHA, scalar2=None, op0=ALU.mult),
             reads=[r_X[kc]], writes=[r_X[kc]])
    for (f0, f1) in FGROUPS:
        nf = f1 - f0
        for fb in range(f0, f1, 2):
            nb = min(2, f1 - fb)
            wgt, r_wg = ctx.load_wA(wg[:, fb * 128:(fb + nb) * 128], KC, nb * 128)
            wut, r_wu = ctx.load_wA(wu[:, fb * 128:(fb + nb) * 128], KC, nb * 128)
            for j in range(nb):
                fi = fb + j - f0
                for tt in range(NTT):
                    ts_ = slice(tt * TT, (tt + 1) * TT)
                    bg = ctx.nbank()
                    mm_group(P, ctx, bg, [(wgt[:, kc, j * 128:(j + 1) * 128], XB[:, kc, ts_]) for kc in range(KC)], [r_wg] + list(r_XB))
                    bu = ctx.nbank()
                    mm_group(P, ctx, bu, [(wut[:, kc, j * 128:(j + 1) * 128], XB[:, kc, ts_]) for kc in range(KC)], [r_wu] + list(r_XB))
                    ti = ctx.ntmp()
                    P.op(A, lambda e, ti=ti, bg=bg: e.activation(out=ctx.tmp[ti][:], in_=ctx.pb[bg][:, :], func=AF.Silu),
                         reads=[ctx.r_pb[bg]], writes=[ctx.r_tmp[ti]])
                    P.op(V, lambda e, ti=ti, bu=bu, fi=fi, ts_=ts_: e.tensor_tensor(out=H[:, fi, ts_], in0=ctx.pb[bu][:, :], in1=ctx.tmp[ti][:], op=ALU.mult),
                         reads=[ctx.r_pb[bu], ctx.r_tmp[ti]], writes=[r_H[fi]])
        for db in range(0, KC, 2):
            wdt, r_wd = ctx.load_wD(wd[f0 * 128:f1 * 128, db * 128:(db + 2) * 128], nf, 256)
            for j in range(2):
                dc = db + j
                for tt in range(NTT):
                    ts_ = slice(tt * TT, (tt + 1) * TT)
                    b = ctx.nbank()
                    mm_group(P, ctx, b, [(wdt[:, fi, j * 128:(j + 1) * 128], H[:, fi, ts_]) for fi in range(nf)], [r_wd] + [r_H[fi] for fi in range(nf)])
                    P.op(V, lambda e, b=b, dc=dc, ts_=ts_: e.scalar_tensor_tensor(out=X[:, dc, ts_], in0=ctx.pb[b][:, :], scalar=0.5, in1=X[:, dc, ts_],
                                                                                 op0=ALU.mult, op1=ALU.add),
                         reads=[ctx.r_pb[b], r_X[dc]], writes=[r_X[dc]])


def load_x(P, X, r_X, XB, r_XB, xT):
    for kc in range(KC):
        P.dma("sync", X[:, kc, :], xT[kc * 128:(kc + 1) * 128, :], writes=[r_X[kc]])
        P.op(A if kc % 2 else V, (lambda e, kc=kc: e.activation(out=XB[:, kc, :], in_=X[:, kc, :], func=AF.Copy)) if kc % 2 else
             (lambda e, kc=kc: e.tensor_copy(out=XB[:, kc, :], in_=X[:, kc, :])), reads=[r_X[kc]], writes=[r_XB[kc]])


def build_A():
    nc = bass.Bass("TRN2", target_bir_lowering=False)
    xT = nc.dram_tensor("xT", [D, NT], F32, kind="ExternalInput").ap()
    wg = nc.dram_tensor("wg", [D, F], F32, kind="ExternalInput").ap()
    wu = nc.dram_tensor("wu", [D, F], F32, kind="ExternalInput").ap()
    wd = nc.dram_tensor("wd", [F, D], F32, kind="ExternalInput").ap()
    lngb = nc.dram_tensor("lngb", [128, 2, KC], F32, kind="ExternalInput").ap()
    w_in = nc.dram_tensor("w_in", [D, RC], F32, kind="ExternalInput").ap()
    x1T = nc.dram_tensor("x1T", [D, NT], F32, kind="ExternalOutput").ap()
    pT = nc.dram_tensor("pT", [RC, NT], F32, kind="ExternalOutput").ap()
    with ExitStack() as es:
        P = Prog(nc, es)
        ctx = DenseCtx(P)
        X = P.sb("X", [128, KC, NT]); r_X = [Res() for _ in range(KC)]
        XB = P.sb("XB", [128, KC, NT], BF16); r_XB = [Res() for _ in range(KC)]
        H = P.sb("H", [128, 11, NT], BF16); r_H = [Res() for _ in range(11)]
        gb = P.sb("gb", [128, 2, KC]); r_gb = Res()
        P.dma("sync", gb[:], lngb[:, :, :], writes=[r_gb])
        load_x(P, X, r_X, XB, r_XB, xT)
        ffn(P, ctx, X, r_X, XB, r_XB, H, r_H, wg, wu, wd)
        layer_norm_fm(P, ctx, X, r_X, XB, r_XB, gb, r_gb, 0, 1)
        for kc in range(KC):
            P.dma("sync", x1T[kc * 128:(kc + 1) * 128, :], X[:, kc, :], reads=[r_X[kc]], is_out=True)
        for mb in range(0, RC // 128, 2):
            nb = min(2, RC // 128 - mb)
            wt, r_w = ctx.load_wA(w_in[:, mb * 128:(mb + nb) * 128], KC, nb * 128)
            for j in range(nb):
                m = mb + j
                for tt in range(NTT):
                    ts_ = slice(tt * TT, (tt + 1) * TT)
                    b = ctx.nbank()
                    mm_group(P, ctx, b, [(wt[:, kc, j * 128:(j + 1) * 128], XB[:, kc, ts_]) for kc in range(KC)], [r_w] + list(r_XB))
                    ti = ctx.ntmp()
                    if (m + tt) % 2:
                        P.op(A, lambda e, ti=ti, b=b: e.activation(out=ctx.tmp[ti][:], in_=ctx.pb[b][:, :], func=AF.Copy), reads=[ctx.r_pb[b]], writes=[ctx.r_tmp[ti]])
                    else:
                        P.op(V, lambda e, ti=ti, b=b: e.tensor_copy(out=ctx.tmp[ti][:], in_=ctx.pb[b][:, :]), reads=[ctx.r_pb[b]], writes=[ctx.r_tmp[ti]])
                    P.dma("sync", pT[m * 128:(m + 1) * 128, ts_], ctx.tmp[ti][:], reads=[ctx.r_tmp[ti]], is_out=True)
        P.finish()
        P.emit()
    return nc


def build_C():
    nc = bass.Bass("TRN2", target_bir_lowering=False)
    x1T = nc.dram_tensor("x1T", [D, NT], F32, kind="ExternalInput").ap()
    yT = nc.dram_tensor("yT", [2, 1024, NT], F32, kind="ExternalInput").ap()
    memT = nc.dram_tensor("memT", [D, 256], F32, kind="ExternalInput").ap()
    w_q = nc.dram_tensor("w_q", [D, 1024], F32, kind="ExternalInput").ap()
    w_g = nc.dram_tensor("w_g", [D, 3 * D], F32, kind="ExternalInput").ap()
    w_kv = nc.dram_tensor("w_kv", [D, 2048], F32, kind="ExternalInput").ap()
    w_br = nc.dram_tensor("w_br", [3, 1024, D], F32, kind="ExternalInput").ap()
    w_o = nc.dram_tensor("w_o", [D, D], F32, kind="ExternalInput").ap()
    vecs = nc.dram_tensor("vecs", [128, 9, KC], F32, kind="ExternalInput").ap()
    wg = nc.dram_tensor("wg", [D, F], F32, kind="ExternalInput").ap()
    wu = nc.dram_tensor("wu", [D, F], F32, kind="ExternalInput").ap()
    wd = nc.dram_tensor("wd", [F, D], F32, kind="ExternalInput").ap()
    x2T = nc.dram_tensor("x2T", [D, NT], F32, kind="ExternalOutput").ap()
    with ExitStack() as es:
        P = Prog(nc, es)
        ctx = DenseCtx(P)
        ARENA = P.sb("ARENA", [128, KC * NT])
        ARENA2 = P.sb("ARENA2", [128, 8192])
        X = ARENA[:, :].rearrange("p (c t) -> p c t", t=NT); r_X = [Res() for _ in range(KC)]
        XB = P.sb("XB", [128, KC, NT], BF16); r_XB = [Res() for _ in range(KC)]
        Yall = ARENA[:, 0:12288].bitcast(BF16).rearrange("p (c t) -> p c t", t=NT); r_Y = [Res() for _ in range(24)]
        QB = ARENA[:, 12288:16384].bitcast(BF16).rearrange("p (c t) -> p c t", t=NT); r_QB = [Res() for _ in range(8)]
        A2B = ARENA2[:, :].bitcast(BF16)
        MIXB = A2B.rearrange("p (c t) -> p c t", t=NT); r_MIX = [Res() for _ in range(KC)]
        H = A2B[:, 0:11 * NT].rearrange("p (c t) -> p c t", t=NT); r_H = [Res() for _ in range(11)]
        MX = ARENA2[:, 0:4096].rearrange("p (c t) -> p c t", t=256); r_MX = [Res() for _ in range(KC)]
        MB = ARENA2[:, 4096:6144].bitcast(BF16).rearrange("p (c t) -> p c t", t=256); r_MB = [Res() for _ in range(KC)]
        KTB = ARENA2[:, 6144:7168].bitcast(BF16).rearrange("p (c t) -> p c t", t=256); r_KTB = Res()
        VB = ARENA2[:, 7168:8192].bitcast(BF16).rearrange("p (c t) -> p c t", t=1024); r_VB = Res()
        vc = P.sb("vc", [128, 9, KC]); r_vc = Res()
        P.dma("sync", vc[:], vecs[:, :, :], writes=[r_vc])
        for kc in range(KC):
            P.dma("gpsimd", XB[:, kc, :], x1T[kc * 128:(kc + 1) * 128, :], writes=[r_XB[kc]])
        for n in range(2):
            for c in range(8):
                P.dma("gpsimd", Yall[:, n * 8 + c, :], yT[n, c * 128:(c + 1) * 128, :], writes=[r_Y[n * 8 + c]])
        for kc in range(KC):
            P.dma("sync", MX[:, kc, :], memT[kc * 128:(kc + 1) * 128, :], writes=[r_MX[kc]])
        layer_norm_fm(P, ctx, MX, r_MX, MB, r_MB, vc, r_vc, 0, 1, ntok=256, write_x=False)
        for mb in range(0, 8, 2):
            wt, r_w = ctx.load_wA(w_kv[:, mb * 128:(mb + 2) * 128], KC, 256)
            for j in range(2):
                b = ctx.nbank()
                mm_group(P, ctx, b, [(wt[:, kc, j * 128:(j + 1) * 128], MB[:, kc, :]) for kc in range(KC)], [r_w] + list(r_MB), n=256)
                P.op(V, lambda e, b=b, m=mb + j: e.tensor_copy(out=KTB[:, m, :], in_=ctx.pb[b][:, 0:256]), reads=[ctx.r_pb[b]], writes=[r_KTB])
        for vb in range(4):
            wt, r_w = ctx.load_wA(w_kv[:, 1024 + vb * 256:1024 + (vb + 1) * 256], KC, 256)
            for mc in range(2):
                b = ctx.nbank()
                mm_group(P, ctx, b, [(MB[:, kc, mc * 128:(mc + 1) * 128], wt[:, kc, :]) for kc in range(KC)], [r_w] + list(r_MB), n=256)
                P.op(V, lambda e, b=b, mc=mc, vb=vb: e.tensor_copy(out=VB[:, mc, vb * 256:(vb + 1) * 256], in_=ctx.pb[b][:, 0:256]), reads=[ctx.r_pb[b]], writes=[r_VB])
        for mb in range(0, 8, 2):
            wt, r_w = ctx.load_wA(w_q[:, mb * 128:(mb + 2) * 128], KC, 256)
            for j in range(2):
                for tt in range(NTT):
                    ts_ = slice(tt * TT, (tt + 1) * TT)
                    b = ctx.nbank()
                    mm_group(P, ctx, b, [(wt[:, kc, j * 128:(j + 1) * 128], XB[:, kc, ts_]) for kc in range(KC)], [r_w] + list(r_XB))
                    P.op(A, lambda e, b=b, m=mb + j, ts_=ts_: e.activation(out=QB[:, m, ts_], in_=ctx.pb[b][:, :], func=AF.Copy), reads=[ctx.r_pb[b]], writes=[r_QB[mb + j]])
        EB = P.sb("EB", [128, 2, TT], BF16); r_EB = Res()
        rden = P.sb("rden", [128, TT]); r_rden = Res()
        for h in range(4):
            for tt in range(NTT):
                ts_ = slice(tt * TT, (tt + 1) * TT)
                for mc in range(2):
                    b = ctx.nbank()
                    mm_group(P, ctx, b, [(KTB[:, h * 2 + dc, mc * 128:(mc + 1) * 128], QB[:, h * 2 + dc, ts_]) for dc in range(2)], [r_KTB, r_QB[h * 2], r_QB[h * 2 + 1]])
                    P.op(A, lambda e, b=b, mc=mc: e.activation(out=EB[:, mc, :], in_=ctx.pb[b][:, :], func=AF.Exp, scale=1.0 / 16.0), reads=[ctx.r_pb[b]], writes=[r_EB])
                b = ctx.nbank()
                mm_group(P, ctx, b, [(ctx.onesb[:], EB[:, mc, :]) for mc in range(2)], [ctx.r_c, r_EB])
                P.op(V, lambda e, b=b: e.reciprocal(out=rden[:], in_=ctx.pb[b][:, :]), reads=[ctx.r_pb[b]], writes=[r_rden])
                for dc in range(2):
                    b = ctx.nbank()
                    mm_group(P, ctx, b, [(VB[:, mc, h * 256 + dc * 128:h * 256 + (dc + 1) * 128], EB[:, mc, :]) for mc in range(2)], [r_VB, r_EB])
                    P.op(V, lambda e, b=b, c=16 + h * 2 + dc, ts_=ts_: e.tensor_tensor(out=Yall[:, c, ts_], in0=ctx.pb[b][:, :], in1=rden[:], op=ALU.mult),
                         reads=[ctx.r_pb[b], r_rden], writes=[r_Y[16 + h * 2 + dc]])
        gate = P.sb("gate", [128, TT]); r_gate = Res()
        term = P.sb("term", [128, TT]); r_term = Res()
        ACC = P.sb("ACC", [128, 2, NT]); r_ACC = Res()
        for db in range(0, KC, 2):
            for n in range(3):
                wt, r_w = ctx.load_wA(w_g[:, n * D + db * 128:n * D + (db + 2) * 128], KC, 256)
                wbt, r_wb = ctx.load_wD(w_br[n, :, db * 128:(db + 2) * 128], 8, 256)
                for j in range(2):
                    dc = db + j
                    for tt in range(NTT):
                        ts_ = slice(tt * TT, (tt + 1) * TT)
                        bg = ctx.nbank()
                        mm_group(P, ctx, bg, [(wt[:, kc, j * 128:(j + 1) * 128], XB[:, kc, ts_]) for kc in range(KC)], [r_w] + list(r_XB))
                        P.op(A, lambda e, bg=bg, n=n, dc=dc: e.activation(out=gate[:], in_=ctx.pb[bg][:, :], func=AF.Sigmoid, bias=vc[:, 2 + n, dc:dc + 1], scale=1.0),
                             reads=[ctx.r_pb[bg], r_vc], writes=[r_gate])
                        bp_ = ctx.nbank()
                        mm_group(P, ctx, bp_, [(wbt[:, c, j * 128:(j + 1) * 128], Yall[:, n * 8 + c, ts_]) for c in range(8)], [r_wb] + [r_Y[n * 8 + c] for c in range(8)])
                        if n == 0:
                            P.op(V, lambda e, bp_=bp_, j=j, ts_=ts_: e.tensor_tensor(out=ACC[:, j, ts_], in0=ctx.pb[bp_][:, :], in1=gate[:], op=ALU.mult),
                                 reads=[ctx.r_pb[bp_], r_gate], writes=[r_ACC])
                        else:
                            P.op(V, lambda e, bp_=bp_: e.tensor_tensor(out=term[:], in0=ctx.pb[bp_][:, :], in1=gate[:], op=ALU.mult),
                                 reads=[ctx.r_pb[bp_], r_gate], writes=[r_term])
                            if n == 1:
                                P.op(G, lambda e, j=j, ts_=ts_: e.tensor_tensor(out=ACC[:, j, ts_], in0=ACC[:, j, ts_], in1=term[:], op=ALU.add), reads=[r_ACC, r_term], writes=[r_ACC])
                            else:
                                P.op(G, lambda e, dc=dc, j=j, ts_=ts_: e.tensor_tensor(out=MIXB[:, dc, ts_], in0=ACC[:, j, ts_], in1=term[:], op=ALU.add),
                                     reads=[r_ACC, r_term], writes=[r_MIX[dc]])
        xs = [P.sb(f"xs{i}", [128, TT]) for i in range(2)]; r_xs = [Res(), Res()]
        cnt = 0
        for db in range(0, KC, 2):
            wt, r_w = ctx.load_wA(w_o[:, db * 128:(db + 2) * 128], KC, 256)
            for j in range(2):
                dc = db + j
                for tt in range(NTT):
                    ts_ = slice(tt * TT, (tt + 1) * TT)
                    i = cnt % 2; cnt += 1
                    P.dma("sync", xs[i][:], x1T[dc * 128:(dc + 1) * 128, ts_], writes=[r_xs[i]])
                    P.op(G, lambda e, i=i: e.tensor_scalar(out=xs[i][:], in0=xs[i][:], scalar1=ALPHA, scalar2=None, op0=ALU.mult), reads=[r_xs[i]], writes=[r_xs[i]])
                    b = ctx.nbank()
                    mm_group(P, ctx, b, [(wt[:, kc, j * 128:(j + 1) * 128], MIXB[:, kc, ts_]) for kc in range(KC)], [r_w] + list(r_MIX))
                    P.op(V, lambda e, b=b, i=i, dc=dc, ts_=ts_: e.tensor_tensor(out=X[:, dc, ts_], in0=ctx.pb[b][:, :], in1=xs[i][:], op=ALU.add),
                         reads=[ctx.r_pb[b], r_xs[i]], writes=[r_X[dc]])
        layer_norm_fm(P, ctx, X, r_X, XB, r_XB, vc, r_vc, 5, 6)
        ffn(P, ctx, X, r_X, XB, r_XB, H, r_H, wg, wu, wd)
        layer_norm_fm(P, ctx, X, r_X, XB, r_XB, vc, r_vc, 7, 8)
        for kc in range(KC):
            P.dma("sync", x2T[kc * 128:(kc + 1) * 128, :], X[:, kc, :], reads=[r_X[kc]], is_out=True)
        P.finish()
        P.emit()
    return nc


_PROGS = {}


def _prog(name):
    if name not in _PROGS:
        _PROGS[name] = {"A": build_A, "C": build_C, "S": lambda: build_scan(4096, 2)}[name]()
    return _PROGS[name]


def _vec16(v):
    return np.ascontiguousarray(np.asarray(v, np.float32).reshape(16, 128).T)


def kernel(x, mem, ffn1_w_gate, ffn1_w_up, ffn1_w_down, ln1_g, ln1_b, w_in,
           rwkv_mu, rwkv_w0, rwkv_w_up, rwkv_a0, rwkv_a_up, rwkv_g_up, rwkv_k_k,
           rwkv_k_a, rwkv_r_k, rwkv_gn_g, rwkv_gn_b, conv_w, mem_ln_g, mem_ln_b,
           w_mem_kv, w_branch, gate_b, w_out, ln2_g, ln2_b, ffn2_w_gate, ffn2_w_up,
           ffn2_w_down, ln3_g, ln3_b):
    f32 = lambda a: np.ascontiguousarray(np.asarray(a, dtype=np.float32))
    x = f32(x); mem = f32(mem)
    NCORE = 8
    cores = list(range(NCORE))
    xT = [np.ascontiguousarray(x[c // 4, (c % 4) * 1024:(c % 4 + 1) * 1024].T) for c in cores]
    memT = [np.ascontiguousarray(mem[c // 4].T) for c in cores]
    cm, rmask = scan_consts()
    for l in range(2):
        w_in_l = f32(w_in[l])
        mA = {"wg": f32(ffn1_w_gate[l]), "wu": f32(ffn1_w_up[l]), "wd": f32(ffn1_w_down[l]),
              "lngb": np.ascontiguousarray(np.stack([_vec16(ln1_g[l]), _vec16(ln1_b[l])], 1)),
              "w_in": np.ascontiguousarray(w_in_l[:, :RC])}
        resA = run_bass_kernel_spmd(_prog("A"), [dict(mA, xT=xT[c]) for c in cores], core_ids=cores).results
        x1T = [np.asarray(resA[c]["x1T"]) for c in cores]
        pall = np.concatenate([np.asarray(resA[c]["pT"]) for c in cores], axis=1)
        del resA
        mu = f32(rwkv_mu[l]); w0 = f32(rwkv_w0[l]); a0 = f32(rwkv_a0[l])
        wup = f32(rwkv_w_up[l]); aup = f32(rwkv_a_up[l]); gup = f32(rwkv_g_up[l])
        kkv = f32(rwkv_k_k[l]); kav = f32(rwkv_k_a[l]); rkv = f32(rwkv_r_k[l]).reshape(-1)
        gng = f32(rwkv_gn_g[l]); gnb = f32(rwkv_gn_b[l]); cw = f32(conv_w[l])
        mS = []
        for c in cores:
            cs = slice(c * 128, (c + 1) * 128)
            zin = np.stack([pall[0:1024][cs], pall[1024:2048][cs], pall[2048:3072][cs], pall[3072:3200], pall[3200:3328], pall[3328:3456],
                            pall[3456:4480][cs], pall[4480:5504][cs], pall[5504:6528][cs]], 0)
            pv = np.stack([mu[0:1024][cs], mu[1024:2048][cs], mu[2048:3072][cs], mu[3072:3200], mu[3200:3328], mu[3328:3456],
                           w0[0][cs], w0[1][cs], a0[0][cs], a0[1][cs], kkv[cs], kav[cs], rkv[cs], gng[cs], gnb[cs],
                           cw[0][cs], cw[1][cs], cw[2][cs]], 1)
            pm = np.stack([wup[:, :, cs].reshape(128, 128), aup[:, :, cs].reshape(128, 128), gup[:, cs]], 1)
            mS.append({"zin": np.ascontiguousarray(zin), "pvec": np.ascontiguousarray(pv), "pmat": np.ascontiguousarray(pm), "cmat": cm, "rmask": rmask})
        del pall
        resS = run_bass_kernel_spmd(_prog("S"), mS, core_ids=cores).results
        yall = np.concatenate([np.asarray(resS[c]["yout"]) for c in cores], axis=1)
        del resS, mS
        vecs = np.ascontiguousarray(np.stack([_vec16(mem_ln_g[l]), _vec16(mem_ln_b[l]), _vec16(gate_b[l][0]), _vec16(gate_b[l][1]), _vec16(gate_b[l][2]),
                                              _vec16(ln2_g[l]), _vec16(ln2_b[l]), _vec16(ln3_g[l]), _vec16(ln3_b[l])], 1))
        mC = {"w_q": np.ascontiguousarray(w_in_l[:, 6528:7552]), "w_g": np.ascontiguousarray(w_in_l[:, 7552:]), "w_kv": f32(w_mem_kv[l]),
              "w_br": f32(w_branch[l]), "w_o": f32(w_out[l]), "vecs": vecs,
              "wg": f32(ffn2_w_gate[l]), "wu": f32(ffn2_w_up[l]), "wd": f32(ffn2_w_down[l])}
        resC = run_bass_kernel_spmd(_prog("C"), [dict(mC, x1T=x1T[c], yT=np.ascontiguousarray(yall[:, :, c * 1024:(c + 1) * 1024]), memT=memT[c]) for c in cores],
                                    core_ids=cores).results
        xT = [np.asarray(resC[c]["x2T"]) for c in cores]
        del resC, yall
    out = np.empty((2, 4096, 2048), np.float32)
    for c in cores:
        out[c // 4, (c % 4) * 1024:(c % 4 + 1) * 1024] = xT[c].T
    return out
```

```python
import numpy as np
from contextlib import ExitStack
import concourse.bass as bass
import concourse.mybir as mybir
from concourse.bass_utils import run_bass_kernel_spmd


F32 = mybir.dt.float32
BF16 = mybir.dt.bfloat16
AF = mybir.ActivationFunctionType
ALU = mybir.AluOpType
ND = 6


class Res:
    __slots__ = ("w", "r")

    def __init__(self):
        self.w = None
        self.r = []


def resgrid(*shape):
    a = np.empty(shape, dtype=object)
    for idx in np.ndindex(*shape):
        a[idx] = Res()
    return a


class _Rec:
    def __init__(self):
        self.calls = []

    def __getattr__(self, name):
        def f(*a, **k):
            self.calls.append((name, a, k))
            return None
        return f


def _replay(calls):
    def fn(e):
        ins = None
        for name, a, k in calls:
            ins = getattr(e, name)(*a, **k)
        return ins
    return fn


class Prog:
    ENGS = ("tensor", "vector", "scalar", "gpsimd", "sync")

    def __init__(self, nc, es):
        self.nc = nc
        self.es = es
        self.q = {e: [] for e in self.ENGS}
        self.sem = {}
        for e in ("tensor", "vector", "scalar", "gpsimd"):
            self.sem[("e", e)] = es.enter_context(nc.semaphore("s_" + e))
        self.ecnt = {e: 0 for e in self.ENGS}
        self.dcnt = {}
        self.dnext = {}
        for qn in ("sync", "gpsimd", "scalar"):
            self.dnext[qn] = 0
            for i in range(ND):
                self.sem[("d", qn, i)] = es.enter_context(nc.semaphore(f"d_{qn}{i}"))
                self.dcnt[(qn, i)] = 0
        self.seen = {e: {} for e in self.ENGS}
        self.out_stamps = []

    stop_stage = None

    def stage(self, n):
        if self.stop_stage is not None and n == self.stop_stage:
            raise StopIteration

    def sb(self, name, shape, dt=F32):
        return self.es.enter_context(self.nc.sbuf_tensor(name, list(shape), dt))

    def ps(self, name, shape, dt=F32):
        return self.es.enter_context(self.nc.psum_tensor(name, list(shape), dt))

    def _deps(self, eng, reads, writes, extra=()):
        deps = {}

        def add(st):
            if st is None:
                return
            k, v = st
            if deps.get(k, 0) < v:
                deps[k] = v

        for r in reads:
            add(r.w)
        for w in writes:
            add(w.w)
            for s in w.r:
                add(s)
        for s in extra:
            add(s)
        waits = []
        for k, v in deps.items():
            if eng == "tensor" and k == ("e", "tensor"):
                continue
            if self.seen[eng].get(k, 0) >= v:
                continue
            self.seen[eng][k] = v
            waits.append((k, v))
        return waits

    def _commit(self, st, reads, writes):
        for r in reads:
            r.r.append(st)
        for w in writes:
            w.w = st
            w.r = []

    def op(self, eng, fn, reads=(), writes=()):
        waits = self._deps(eng, reads, writes)
        self.ecnt[eng] += 1
        st = (("e", eng), self.ecnt[eng])
        rec = _Rec()
        fn(rec)
        assert rec.calls
        self.q[eng].append((waits, _replay(rec.calls), st))
        self._commit(st, reads, writes)
        return st

    def dma(self, qn, out, in_, reads=(), writes=(), is_out=False):
        i = self.dnext[qn]
        self.dnext[qn] = (i + 1) % ND
        key = ("d", qn, i)
        prev = self.dcnt[(qn, i)]
        extra = [(key, prev)] if prev > 0 else []
        waits = self._deps(qn, reads, writes, extra)
        self.dcnt[(qn, i)] = prev + 16
        st = (key, prev + 16)
        self.q[qn].append((waits, (lambda e, o=out, i_=in_: e.dma_start(out=o, in_=i_)), st))
        self._commit(st, reads, writes)
        if is_out:
            self.out_stamps.append(st)
        return st

    def finish(self):
        final = {}
        for qn in ("sync", "gpsimd", "scalar"):
            for i in range(ND):
                v = self.dcnt[(qn, i)]
                if v > 0:
                    final[("d", qn, i)] = v
        self.q["sync"].append((list(final.items()), None, None))

    def emit(self):
        nc = self.nc
        with nc.Block() as block:
            def mk(name):
                def body(e):
                    for waits, fn, st in self.q[name]:
                        for k, v in waits:
                            e.wait_ge(self.sem[k], v)
                        if fn is None:
                            continue
                        ins = fn(e)
                        if st is not None:
                            ins.then_inc(self.sem[st[0]], 16 if st[0][0] == "d" else 1)
                return body
            block.tensor(mk("tensor"))
            block.vector(mk("vector"))
            block.scalar(mk("scalar"))
            block.gpsimd(mk("gpsimd"))
            block.sync(mk("sync"))


C = 64
NB = 8
BT = NB * C
NIT = NB * 2
DEC = 0.606531
GN_EPS = 64e-5
(MU_R, MU_K, MU_V, MU_HW, MU_HA, MU_HG, W0F, W0B, A0F, A0B, KK_, KA_, RK_, GNG, GNB, CW0, CW1, CW2) = range(18)
NPV = 18


def scan_consts():
    idx = np.arange(C)
    cm = np.zeros((128, 7, 128), np.float32)
    cm[:, 0, :] = np.eye(128)
    bd = np.zeros((128, 128), np.float32)
    bd[:64, :64] = 1
    bd[64:, 64:] = 1
    cm[:, 1, :] = bd
    cm[:, 2, :] = bd / 64.0
    for d in range(2):
        if d == 0:
            strict = (idx[:, None] < idx[None, :]).astype(np.float32)
            incl = (idx[:, None] <= idx[None, :]).astype(np.float32)
        else:
            strict = (idx[:, None] > idx[None, :]).astype(np.float32)
            incl = (idx[:, None] >= idx[None, :]).astype(np.float32)
        m1 = np.zeros((128, 128), np.float32)
        m1[:64, :64] = -strict
        m1[:64, 64:] = incl
        m1[64:, :64] = -strict
        m1[64:, 64:] = incl
        cm[:, 3 + 2 * d, :] = m1
        m2 = np.zeros((128, 128), np.float32)
        m2[:64, :64] = -strict.T
        m2[:64, 64:] = strict.T
        m2[64:, :64] = -strict.T
        m2[64:, 64:] = strict.T
        cm[:, 4 + 2 * d, :] = m2
    rmask = np.ones((128, BT), np.float32)
    rmask[:, ::C] = 0
    return cm, rmask


STOP = None


def build_scan(T, NBATCH):
    nc = bass.Bass("TRN2", target_bir_lowering=False)
    NTOK = T * NBATCH
    zin = nc.dram_tensor("zin", [9, 128, NTOK], F32, kind="ExternalInput").ap()
    pvec = nc.dram_tensor("pvec", [128, NPV], F32, kind="ExternalInput").ap()
    pmat = nc.dram_tensor("pmat", [128, 3, 128], F32, kind="ExternalInput").ap()
    cmat = nc.dram_tensor("cmat", [128, 7, 128], F32, kind="ExternalInput").ap()
    rmk = nc.dram_tensor("rmask", [128, BT], F32, kind="ExternalInput").ap()
    yout = nc.dram_tensor("yout", [2, 128, NTOK], F32, kind="ExternalOutput").ap()
    with ExitStack() as es:
        P = Prog(nc, es)
        try:
            emit_scan(P, zin, pvec, pmat, cmat, rmk, yout, T, NBATCH)
        except StopIteration:
            pass
        P.finish()
        P.emit()
    return nc


def emit_scan(P, zin, pvec, pmat, cmat, rmk, yout, T, NBATCH):
    nblk = T // BT
    sb, ps = P.sb, P.ps
    V, G, A, PE = "vector", "gpsimd", "scalar", "tensor"
    pv = sb("pv", [128, NPV + 8]); r_pv = Res()
    pm = sb("pm", [128, 3, 128]); r_pm = Res()
    cm = sb("cm", [128, 7, 128]); r_cm = Res()
    rm_t = sb("rmaskt", [128, BT]); r_rm = Res()
    P.dma("sync", pv[:, 0:NPV], pvec[:, :], writes=[r_pv])
    P.dma("sync", pm[:], pmat[:, :, :], writes=[r_pm])
    P.dma("sync", cm[:], cmat[:, :, :], writes=[r_cm])
    P.dma("sync", rm_t[:], rmk[:, :], writes=[r_rm])
    ident = cm[:, 0, :]
    bdones = cm[:, 1, :]
    bdavg = cm[:, 2, :]
    hm = sb("hm", [128, 8]); r_hm = Res()
    P.op(V, lambda e: e.tensor_scalar(out=pv[:, NPV:NPV + 6], in0=pv[:, 0:6], scalar1=-1.0, scalar2=1.0,
                                      op0=ALU.mult, op1=ALU.add), reads=[r_pv], writes=[r_hm])
    P.op(V, lambda e: e.tensor_scalar(out=hm[:, 0:6], in0=pv[:, 0:6], scalar1=0.5, scalar2=None, op0=ALU.mult),
         reads=[r_pv], writes=[r_hm])
    P.op(V, lambda e: e.tensor_scalar(out=hm[:, 7:8], in0=pv[:, KA_:KA_ + 1], scalar1=0.5, scalar2=None, op0=ALU.mult),
         reads=[r_pv], writes=[r_hm])
    P.op(V, lambda e: e.tensor_scalar(out=hm[:, 6:7], in0=pv[:, KA_:KA_ + 1], scalar1=-1.0, scalar2=1.0,
                                      op0=ALU.mult, op1=ALU.add), reads=[r_pv], writes=[r_hm])
    epsv = sb("epsv", [128, 2])
    P.op(V, lambda e: e.memset(epsv[:, 0:1], 1e-12), writes=[r_hm])
    P.op(V, lambda e: e.memset(epsv[:, 1:2], GN_EPS), writes=[r_hm])

    def col(j):
        return pv[:, j:j + 1]

    NRAW = 9
    raw = [sb(f"raw{i}", [128, BT + 2]) for i in range(NRAW)]
    r_raw = [Res() for _ in range(NRAW)]
    sh = [sb(f"sh{i}", [128, BT]) for i in range(6)]
    r_sh = [Res() for _ in range(6)]
    tmpA = sb("tmpA", [128, BT]); r_tmpA = Res()
    tmpB = sb("tmpB", [128, BT]); r_tmpB = Res()
    tmpA2 = sb("tmpA2", [128, BT]); r_tmpA2 = Res()
    tmpB2 = sb("tmpB2", [128, BT]); r_tmpB2 = Res()
    th = sb("th", [128, BT]); r_th = Res()
    sig = sb("sig", [128, BT]); r_sig = Res()
    aa = sb("aa", [128, BT]); r_aa = Res()
    af = sb("af", [128, BT]); r_af = Res()
    cumS = sb("cumS", [128, BT]); r_cum = Res()
    cp = sb("cp", [128, BT]); r_cp = Res()
    rmm = sb("rmm", [128, BT]); r_rmm = Res()
    cb = sb("cb", [128, BT]); r_cb = Res()
    eW = sb("eW", [128, BT]); r_eW = Res()
    eWp = sb("eWp", [128, BT]); r_eWp = Res()
    eWi = sb("eWi", [128, BT]); r_eWi = Res()
    eD = sb("eD", [128, BT]); r_eD = Res()
    WCt = sb("WCt", [128, NB]); r_WCt = Res()
    WCs = sb("WCs", [64, NB, 2]); r_WCs = Res()
    kkr = sb("kkr", [128, BT]); r_kkr = Res()
    sq = sb("sq", [128, BT]); r_sq = Res()
    rn = sb("rn", [128, BT]); r_rn = Res()
    kk = sb("kk", [128, BT]); r_kk = Res()
    t1 = sb("t1", [128, BT]); r_t1 = Res()
    kd = sb("kd", [128, BT]); r_kd = Res()
    bb = sb("bb", [128, BT]); r_bb = Res()
    LT = sb("LT", [128, NB, 2, C]); r_LT = Res()
    RT = sb("RT", [128, NB, 2, C]); r_RT = Res()
    bp = sb("bp", [128, BT]); r_bp = Res()
    ktp = sb("ktp", [128, BT]); r_ktp = Res()
    KLin = sb("KLin", [128, NB, 128], BF16); r_KLa = Res(); r_KLb = [Res() for _ in range(2)]
    RB = sb("RB", [128, NB, 128]); r_RBa = Res(); r_RBb = [Res() for _ in range(2)]
    Bs = sb("Bs", [128, NIT, 128]); r_Bs_tl = Res(); r_Bs_tr = Res(); r_Bs_bl = Res(); r_Bs_br = [Res() for _ in range(2)]
    PPa = sb("PPa", [128, NB, 192], BF16); r_PPa = [Res() for _ in range(4)]
    PPb = sb("PPb", [128, NB, 192], BF16); r_PPb = [Res() for _ in range(4)]
    TTl = [sb("TTa", [128, NB, 64], BF16), sb("TTb", [128, NB, 64], BF16)]; TTh = TTl; r_TTl = [[Res() for _ in range(4)], [Res() for _ in range(4)]]
    idst = sb("idst", [128, 64])
    KL = sb("KL", [128, NB, 128]); r_KL = [Res() for _ in range(2)]
    KLs = sb("KLs", [64, NB, 128]); r_KLs = [Res() for _ in range(2)]
    RBs = sb("RBs", [64, NB, 128]); r_RBs = [Res() for _ in range(2)]
    ABQH = sb("ABQH", [128, NIT, 128]); r_AB = [Res() for _ in range(4)]
    Z = sb("Z", [128, NB, 2, 64]); r_Zv = Res(); r_Zs = [Res() for _ in range(NB)]
    STc = sb("STc", [64, 2, 64]); r_ST = Res()
    yf = sb("yf", [128, T]); r_yf = [Res() for _ in range(nblk)]
    cu = sb("cu", [128, BT + 2]); r_cu = Res()
    ysum = eW; r_ysum = r_eW
    ysq = eWp; r_ysq = r_eWp
    m2 = eWi; r_m2 = r_eWi
    yn = eD; r_yn = r_eD
    sg = kkr; r_sg = r_kkr
    rkb = rn; r_rkb = r_rn
    yo = bb; r_yo = r_bb
    hc = bp; r_hc = r_bp
    yc = ktp; r_yc = r_ktp
    pb = [ps(f"pb{i}", [128, 512]) for i in range(8)]
    r_pb = [Res() for _ in range(8)]

    def mm(e, out, lhsT, rhs, start=True, stop=True):
        rp = lhsT.base_partition(); cp_ = out.base_partition()
        if rp or cp_:
            return e.matmul(out, lhsT, rhs, start=start, stop=stop, tile_position=(rp, cp_))
        return e.matmul(out, lhsT, rhs, start=start, stop=stop)

    P.op(V, lambda e: e.tensor_tensor(out=idst[:], in0=cm[:, 0, 0:64], in1=cm[:, 0, 64:128], op=ALU.add), reads=[r_cm], writes=[r_hm])
    for b in range(NBATCH):
        for d in range(2):
            P.op(V, lambda e: e.memset(STc[:], 0.0), writes=[r_ST])
            blks = range(nblk) if d == 0 else range(nblk - 1, -1, -1)
            for blk in blks:
                t0 = blk * BT
                g0 = b * T + t0
                narr = 5 if d == 0 else 9
                arrs = [0, 1, 2, 3, 4] if d == 0 else list(range(9))
                for i in arrs:
                    lo = 1 if blk == 0 else 0
                    hi = BT + 1 if blk == nblk - 1 else BT + 2
                    if blk == 0:
                        P.op(G, lambda e, i=i: e.memset(raw[i][:, 0:1], 0.0), writes=[r_raw[i]])
                    if blk == nblk - 1:
                        P.op(G, lambda e, i=i: e.memset(raw[i][:, BT + 1:BT + 2], 0.0), writes=[r_raw[i]])
                    P.dma("sync", raw[i][:, lo:hi], zin[i, :, g0 - 1 + lo:g0 - 1 + hi], writes=[r_raw[i]])
                shl = [0, 1, 2, 3, 4] if d == 0 else [0, 1, 2, 3, 4, 5]
                for n, i in enumerate(shl):
                    e1 = G if n % 2 == 0 else V
                    tA, rA, tB, rB = (tmpA, r_tmpA, tmpB, r_tmpB) if n % 2 == 0 else (tmpA2, r_tmpA2, tmpB2, r_tmpB2)
                    P.op(e1, lambda e, i=i, tA=tA: e.tensor_tensor(out=tA[:], in0=raw[i][:, 0:BT], in1=raw[i][:, 2:BT + 2],
                                                                   op=ALU.add), reads=[r_raw[i]], writes=[rA])
                    P.op(A, lambda e, i=i, tA=tA, tB=tB: e.activation(out=tB[:], in_=tA[:], func=AF.Copy, scale=hm[:, i:i + 1]),
                         reads=[rA, r_hm], writes=[rB])
                    P.op(V, lambda e, i=i, tB=tB: e.scalar_tensor_tensor(out=sh[i][:], in0=raw[i][:, 1:BT + 1],
                                                                         scalar=pv[:, NPV + i:NPV + i + 1], in1=tB[:],
                                                                         op0=ALU.mult, op1=ALU.add),
                         reads=[r_raw[i], rB, r_hm], writes=[r_sh[i]])
                P.stage(1)
                rs, ks, vs, hws, has, hgs = sh
                ds = slice(64 * d, 64 * d + 64)
                P.op(A, lambda e: e.activation(out=th[:], in_=hws[:], func=AF.Tanh), reads=[r_sh[3]], writes=[r_th])
                P.op(PE, lambda e: mm(e, pb[0][:, :], pm[ds, 0, :], th[ds, :]), reads=[r_pm, r_th], writes=[r_pb[0]])
                P.op(PE, lambda e: mm(e, pb[1][:, :], pm[ds, 1, :], has[ds, :]), reads=[r_pm, r_sh[4]], writes=[r_pb[1]])
                P.op(A, lambda e: e.activation(out=sig[:], in_=pb[0][:, :], func=AF.Sigmoid, bias=col(W0F + d), scale=1.0),
                     reads=[r_pb[0], r_pv], writes=[r_sig])
                P.op(A, lambda e: e.activation(out=aa[:], in_=pb[1][:, :], func=AF.Sigmoid, bias=col(A0F + d), scale=1.0),
                     reads=[r_pb[1], r_pv], writes=[r_aa])
                P.stage(2)
                P.op(V, lambda e: e.tensor_tensor_scan(out=cumS[:], data0=rm_t[:], data1=sig[:], initial=0.0,
                                                       op0=ALU.mult, op1=ALU.add), reads=[r_rm, r_sig], writes=[r_cum])
                P.op(V, lambda e: e.tensor_tensor(out=cp[:], in0=cumS[:], in1=sig[:], op=ALU.subtract),
                     reads=[r_cum, r_sig], writes=[r_cp])
                cum3 = cumS[:].rearrange("p (c t) -> p c t", t=C)
                tot_b = cum3[:, :, C - 1:C].to_broadcast([128, NB, C])
                P.op(V, lambda e: e.tensor_tensor(out=rmm[:].rearrange("p (c t) -> p c t", t=C), in0=tot_b, in1=cum3,
                                                  op=ALU.subtract), reads=[r_cum], writes=[r_rmm])
                if d == 0:
                    c_cum, c_prev, c_rem = cumS, cp, rmm
                    rr_cum, rr_prev, rr_rem = r_cum, r_cp, r_rmm
                else:
                    P.op(V, lambda e: e.tensor_tensor(out=cb[:], in0=rmm[:], in1=sig[:], op=ALU.add),
                         reads=[r_rmm, r_sig], writes=[r_cb])
                    c_cum, c_prev, c_rem = cb, rmm, cp
                    rr_cum, rr_prev, rr_rem = r_cb, r_rmm, r_cp
                P.op(A, lambda e: e.activation(out=eW[:], in_=c_cum[:], func=AF.Exp, scale=-DEC), reads=[rr_cum], writes=[r_eW])
                P.op(A, lambda e: e.activation(out=eWp[:], in_=c_prev[:], func=AF.Exp, scale=-DEC), reads=[rr_prev], writes=[r_eWp])
                P.op(A, lambda e: e.activation(out=eWi[:], in_=c_cum[:], func=AF.Exp, scale=DEC), reads=[rr_cum], writes=[r_eWi])
                P.op(A, lambda e: e.activation(out=eD[:], in_=c_rem[:], func=AF.Exp, scale=-DEC), reads=[rr_rem], writes=[r_eD])
                P.op(A, lambda e: e.activation(out=WCt[:, :], in_=cum3[:, :, C - 1], func=AF.Exp, scale=-DEC),
                     reads=[r_cum], writes=[r_WCt])
                P.op(V, lambda e: e.tensor_copy(out=WCs[:, :, 0], in_=WCt[0:64, :]), reads=[r_WCt], writes=[r_WCs])
                P.op(V, lambda e: e.tensor_copy(out=WCs[:, :, 1], in_=WCt[64:128, :]), reads=[r_WCt], writes=[r_WCs])
                P.stage(3)
                P.op(V, lambda e: e.tensor_scalar(out=kkr[:], in0=ks[:], scalar1=col(KK_), scalar2=None, op0=ALU.mult),
                     reads=[r_sh[1], r_pv], writes=[r_kkr])
                P.op(G, lambda e: e.tensor_tensor(out=sq[:], in0=kkr[:], in1=kkr[:], op=ALU.mult), reads=[r_kkr], writes=[r_sq])
                P.op(PE, lambda e: mm(e, pb[2][:, :], bdones, sq[:]), reads=[r_cm, r_sq], writes=[r_pb[2]])
                P.op(A, lambda e: e.activation(out=rn[:], in_=pb[2][:, :], func=AF.Sqrt, bias=epsv[:, 0:1], scale=1.0),
                     reads=[r_pb[2], r_hm], writes=[r_rn])
                P.op(V, lambda e: e.reciprocal(out=rn[:], in_=rn[:]), reads=[r_rn], writes=[r_rn])
                P.op(G, lambda e: e.tensor_tensor(out=kk[:], in0=kkr[:], in1=rn[:], op=ALU.mult), reads=[r_kkr, r_rn], writes=[r_kk])
                P.op(V, lambda e: e.tensor_scalar(out=t1[:], in0=aa[:], scalar1=col(KA_), scalar2=hm[:, 6:7],
                                                  op0=ALU.mult, op1=ALU.add), reads=[r_aa, r_pv, r_hm], writes=[r_t1])
                P.op(G, lambda e: e.tensor_tensor(out=kd[:], in0=ks[:], in1=t1[:], op=ALU.mult), reads=[r_sh[1], r_t1], writes=[r_kd])
                P.op(G, lambda e: e.tensor_tensor(out=bb[:], in0=kk[:], in1=aa[:], op=ALU.mult), reads=[r_kk, r_aa], writes=[r_bb])
                P.stage(4)
                P.op(V, lambda e: e.tensor_tensor(out=LT[:, :, 0, :], in0=bb[:].rearrange("p (c t) -> p c t", t=C), in1=eWi[:].rearrange("p (c t) -> p c t", t=C), op=ALU.mult), reads=[r_bb, r_eWi], writes=[r_LT])
                P.op(G, lambda e: e.tensor_tensor(out=LT[:, :, 1, :], in0=kd[:].rearrange("p (c t) -> p c t", t=C), in1=eWi[:].rearrange("p (c t) -> p c t", t=C), op=ALU.mult), reads=[r_kd, r_eWi], writes=[r_LT])
                P.op(V, lambda e: e.tensor_tensor(out=RT[:, :, 0, :], in0=kk[:].rearrange("p (c t) -> p c t", t=C), in1=eWp[:].rearrange("p (c t) -> p c t", t=C), op=ALU.mult), reads=[r_kk, r_eWp], writes=[r_RT])
                P.op(G, lambda e: e.tensor_tensor(out=RT[:, :, 1, :], in0=rs[:].rearrange("p (c t) -> p c t", t=C), in1=eW[:].rearrange("p (c t) -> p c t", t=C), op=ALU.mult), reads=[r_sh[0], r_eW], writes=[r_RT])
                P.op(V, lambda e: e.tensor_tensor(out=bp[:], in0=bb[:], in1=eD[:], op=ALU.mult), reads=[r_bb, r_eD], writes=[r_bp])
                P.op(G, lambda e: e.tensor_tensor(out=ktp[:], in0=kd[:], in1=eD[:], op=ALU.mult), reads=[r_kd, r_eD], writes=[r_ktp])
                P.stage(5)
                side = []

                def transp(src_ap_fn, rsrc, bank0, evac):
                    def f(e):
                        ins = None
                        for c in range(NB):
                            o = pb[bank0 + c // 4][0:64, (c % 4) * 128:(c % 4 + 1) * 128]
                            ins = e.transpose(o, src_ap_fn(c), ident)
                        return ins
                    def thunk():
                        P.op(PE, f, reads=[rsrc, r_cm], writes=[r_pb[bank0], r_pb[bank0 + 1]])
                        evac()
                    side.append(thunk)

                def ps2(bank0):
                    return [pb[bank0 + j][0:64, :].rearrange("p (c n) -> p c n", n=128) for j in range(2)]

                def transp_h(src_ap_fn, rsrc, bank, dst_tile, r_dst):
                    def f(e):
                        ins = None
                        for c in range(NB):
                            for h in range(2):
                                hs = slice(64 * h, 64 * h + 64)
                                ins = mm(e, pb[bank][hs, c * 64:(c + 1) * 64], src_ap_fn(c, hs), ident[hs, hs])
                        return ins

                    def thunk():
                        P.op(PE, f, reads=[rsrc, r_cm], writes=[r_pb[bank]])
                        src = pb[bank][:, :].rearrange("p (c k) -> p c k", k=64)
                        P.op(V, lambda e, src=src: e.tensor_copy(out=dst_tile[:, :, 0:64], in_=src), reads=[r_pb[bank]], writes=[r_dst])
                    side.append(thunk)
                transp_h(lambda c, hs: RT[hs, c, 0, :], r_RT, 4, KLin, r_KLa)
                transp_h(lambda c, hs: bp[hs, c * C:(c + 1) * C], r_bp, 5, RB, r_RBa)

                def ev_ktp():
                    for j in range(2):
                        src = ps2(6)[j].rearrange("p c (h k) -> p c h k", h=2)
                        dst = Bs[64:128, j * 8:(j + 1) * 8, 0:64].rearrange("p (c h) k -> p c h k", h=2)
                        P.op(V if j == 0 else A,
                             (lambda e, s=src, d_=dst: e.tensor_copy(out=d_, in_=s)) if j == 0 else
                             (lambda e, s=src, d_=dst: e.activation(out=d_, in_=s, func=AF.Copy)),
                             reads=[r_pb[6 + j]], writes=[r_Bs_bl])
                transp(lambda c: ktp[:, c * C:(c + 1) * C], r_ktp, 6, ev_ktp)

                def ev_v():
                    for j in range(2):
                        src = ps2(6)[j].rearrange("p c (h k) -> p c h k", h=2)
                        dst = Z[64:128, j * 4:(j + 1) * 4, :, :]
                        P.op(V if j == 0 else A,
                             (lambda e, s=src, d_=dst: e.tensor_copy(out=d_, in_=s)) if j == 0 else
                             (lambda e, s=src, d_=dst: e.activation(out=d_, in_=s, func=AF.Copy)),
                             reads=[r_pb[6 + j]], writes=[r_Zv])
                transp(lambda c: vs[:, c * C:(c + 1) * C], r_sh[2], 6, ev_v)
                P.stage(6)
                idb = ident[0:64, 0:64].unsqueeze(1).to_broadcast([64, NIT, 64])
                wcb = WCs[:].rearrange("p c h -> p (c h)").unsqueeze(2).to_broadcast([64, NIT, 64])
                def bs_thunk():
                    P.op(G, lambda e: e.tensor_tensor(out=Bs[0:64, :, 0:64], in0=idb, in1=wcb, op=ALU.mult),
                         reads=[r_cm, r_WCs], writes=[r_Bs_tl])
                    for h in range(2):
                        dst = Bs[0:64, :, 64:128].rearrange("p (c h) t -> p c h t", h=2)[:, :, h, :]
                        src = RT[64 * h:64 * h + 64, :, 1, :]
                        P.op(G, lambda e, s=src, d_=dst: e.tensor_copy(out=d_, in_=s), reads=[r_RT], writes=[r_Bs_tr])
                side.append(bs_thunk)
                P.stage(7)
                m1 = cm[:, 3 + 2 * d, :]
                m2_ = cm[:, 4 + 2 * d, :]
                for grp in range(2):
                    bk1, bk2, bk3 = 0, 1, 2 + grp

                    def fg(e, grp=grp, bk1=bk1, bk2=bk2, bk3=bk3):
                        ins = None
                        for q in range(4):
                            c = grp * 4 + q
                            for h in range(2):
                                hs = slice(64 * h, 64 * h + 64)
                                Lc = LT[hs, c].rearrange("p a t -> p (a t)")
                                Rc = RT[hs, c].rearrange("p a t -> p (a t)")
                                mm(e, pb[bk1][hs, q * 128:(q + 1) * 128], LT[hs, c, 0, :], Rc)
                                mm(e, pb[bk2][hs, q * 128:(q + 1) * 128], RT[hs, c, 0, :], Lc)
                                ins = mm(e, pb[bk3][hs, q * 64:(q + 1) * 64], LT[hs, c, 1, :], RT[hs, c, 1, :])
                        return ins
                    P.op(PE, fg, reads=[r_LT, r_RT], writes=[r_pb[bk1], r_pb[bk2], r_pb[bk3]])
                    cs4 = slice(grp * 4, grp * 4 + 4)
                    its8 = slice(grp * 8, grp * 8 + 8)
                    g1 = pb[bk1][:, :].rearrange("p (q n) -> p q n", n=128)
                    g2 = pb[bk2][:, :].rearrange("p (q n) -> p q n", n=128)
                    g3 = pb[bk3][64:128, :].rearrange("p (q n) -> p q n", n=64)
                    mTL = m1[:, 0:64].unsqueeze(1).to_broadcast([128, 4, 64])
                    mTR = m1[:, 64:128].unsqueeze(1).to_broadcast([128, 4, 64])
                    mBR = m1[64:128, 64:128].unsqueeze(1).to_broadcast([64, 8, 64])
                    m2L = m2_[:, 0:64].unsqueeze(1).to_broadcast([128, 4, 64])
                    m2R = m2_[:, 64:128].unsqueeze(1).to_broadcast([128, 4, 64])
                    rpp = [r_PPa[2 * grp], r_PPa[2 * grp + 1]]
                    P.op(V, lambda e, cs4=cs4, g1=g1, mTL=mTL: e.tensor_tensor(out=PPa[:, cs4, 0:64], in0=g1[:, :, 0:64], in1=mTL, op=ALU.mult),
                         reads=[r_pb[bk1], r_cm], writes=rpp)
                    P.op(V, lambda e, cs4=cs4, g1=g1, mTR=mTR: e.tensor_tensor(out=RB[:, cs4, 64:128], in0=g1[:, :, 64:128], in1=mTR, op=ALU.mult),
                         reads=[r_pb[bk1], r_cm], writes=[r_RBb[grp]])
                    for h in range(2):
                        hs = slice(64 * h, 64 * h + 64)
                        g3h = pb[bk3][hs, 0:256].rearrange("p (q n) -> p q n", n=64)
                        mh = m1[hs, 64:128].unsqueeze(1).to_broadcast([64, 4, 64])
                        dst = Bs[64:128, its8, 64:128].rearrange("p (c h) t -> p c h t", h=2)[:, :, h, :]
                        P.op(V, lambda e, g3h=g3h, mh=mh, dst=dst: e.tensor_tensor(out=dst, in0=g3h, in1=mh, op=ALU.mult),
                             reads=[r_pb[bk3], r_cm], writes=[r_Bs_br[grp]])
                    P.op(V, lambda e, cs4=cs4, g2=g2, m2L=m2L: e.tensor_tensor(out=PPa[:, cs4, 64:128], in0=g2[:, :, 0:64], in1=m2L, op=ALU.mult),
                         reads=[r_pb[bk2], r_cm], writes=rpp)
                    P.op(V, lambda e, cs4=cs4, g2=g2, m2R=m2R: e.tensor_tensor(out=KLin[:, cs4, 64:128], in0=g2[:, :, 64:128], in1=m2R, op=ALU.mult),
                         reads=[r_pb[bk2], r_cm], writes=[r_KLb[grp]])
                P.stage(8)
                idb16 = idst[:].unsqueeze(1).to_broadcast([128, NB, 64])
                P.op(V, lambda e: e.tensor_tensor(out=TTl[1][:], in0=PPa[:, :, 0:64], in1=idb16, op=ALU.add),
                     reads=list(r_PPa) + [r_hm], writes=list(r_TTl[1]))
                cur, r_cur, nxt, r_nxt = PPa, r_PPa, PPb, r_PPb
                for s in range(1, 7):
                    for grp in range(4):
                        bk = grp % 4
                        cs2 = slice(grp * 2, grp * 2 + 2)

                        def fi(e, s=s, grp=grp, bk=bk, cur=cur):
                            ins = None
                            for q in range(2):
                                c = grp * 2 + q
                                for h in range(2):
                                    hs = slice(64 * h, 64 * h + 64)
                                    o = pb[bk][hs, q * 192:(q + 1) * 192]
                                    Pm = cur[hs, c, 0:64]
                                    PmT = cur[hs, c, 64:128]
                                    if s <= 5:
                                        ins = mm(e, o[:, 0:64], PmT, Pm)
                                        ins = mm(e, o[:, 64:128], Pm, PmT)
                                    if s >= 2:
                                        ins = mm(e, o[:, 128:192], PmT, TTl[(s - 1) % 2][hs, c, :])
                            return ins
                        rds = [r_cur[grp]] + ([r_TTl[(s - 1) % 2][grp]] if s >= 2 else [])
                        P.op(PE, fi, reads=rds, writes=[r_pb[bk]])
                        o3 = pb[bk][:, 0:384].rearrange("p (q n) -> p q n", n=192)
                        lo_c = 0 if s <= 5 else 128
                        hi_c = 192 if s >= 2 else 128
                        if grp % 2:
                            P.op(A, lambda e, cs2=cs2, o3=o3, nxt=nxt, lo_c=lo_c, hi_c=hi_c: e.activation(out=nxt[:, cs2, lo_c:hi_c], in_=o3[:, :, lo_c:hi_c], func=AF.Copy),
                                 reads=[r_pb[bk]], writes=[r_nxt[grp]])
                        else:
                            P.op(V, lambda e, cs2=cs2, o3=o3, nxt=nxt, lo_c=lo_c, hi_c=hi_c: e.tensor_copy(out=nxt[:, cs2, lo_c:hi_c], in_=o3[:, :, lo_c:hi_c]),
                                 reads=[r_pb[bk]], writes=[r_nxt[grp]])
                        if s >= 2:
                            P.op(G if grp % 2 == 0 else V, lambda e, cs2=cs2, nxt=nxt, s=s: e.tensor_tensor(out=TTl[s % 2][:, cs2, :], in0=nxt[:, cs2, 128:192], in1=TTl[(s - 1) % 2][:, cs2, :], op=ALU.add),
                                 reads=[r_nxt[grp], r_TTl[(s - 1) % 2][grp]], writes=[r_TTl[s % 2][grp]])
                        if side:
                            side.pop(0)()
                    cur, r_cur, nxt, r_nxt = nxt, r_nxt, cur, r_cur
                while side:
                    side.pop(0)()
                P.stage(9)
                for grp in range(2):
                    bk = 6 + grp % 2

                    def f7(e, grp=grp, bk=bk):
                        ins = None
                        for q in range(4):
                            c = grp * 4 + q
                            for h in range(2):
                                hs = slice(64 * h, 64 * h + 64)
                                ins = mm(e, pb[bk][hs, q * 128:(q + 1) * 128], TTl[0][hs, c, :], KLin[hs, c, :])
                        return ins
                    P.op(PE, f7, reads=[r_TTl[0][2 * grp], r_TTl[0][2 * grp + 1], r_KLa, r_KLb[grp]], writes=[r_pb[bk]])
                    cs4 = slice(grp * 4, grp * 4 + 4)
                    src = pb[bk][:, :].rearrange("p (q n) -> p q n", n=128)
                    P.op(A, lambda e, cs4=cs4, src=src: e.activation(out=KL[:, cs4, :], in_=src, func=AF.Copy),
                         reads=[r_pb[bk]], writes=[r_KL[grp]])
                for g2_ in range(2):
                    c4 = slice(g2_ * 4, g2_ * 4 + 4)
                    P.op(V, lambda e, c4=c4: e.tensor_copy(out=KLs[:, c4, :], in_=KL[64:128, c4, :]), reads=[r_KL[g2_]], writes=[r_KLs[g2_]])
                    P.op(G, lambda e, c4=c4: e.tensor_copy(out=RBs[:, c4, :], in_=RB[64:128, c4, :]), reads=[r_RBa, r_RBb[g2_]], writes=[r_RBs[g2_]])
                for grp in range(NIT // 4):
                    bk = grp % 2

                    def f8(e, grp=grp, bk=bk):
                        ins = None
                        for q in range(4):
                            it = grp * 4 + q
                            ins = mm(e, pb[bk][:, q * 128:(q + 1) * 128], (KLs if it % 2 else KL)[0:64, it // 2, :], (RBs if it % 2 else RB)[0:64, it // 2, :])
                        return ins
                    P.op(PE, f8, reads=[r_KL[grp // 2], r_KLs[grp // 2], r_RBs[grp // 2], r_RBa, r_RBb[grp // 2]], writes=[r_pb[bk]])
                    its = slice(grp * 4, grp * 4 + 4)
                    src = pb[bk][:, :].rearrange("p (q n) -> p q n", n=128)
                    P.op(V, lambda e, its=its, src=src: e.scalar_tensor_tensor(out=ABQH[:, its, :], in0=src, scalar=-1.0, in1=Bs[:, its, :], op0=ALU.mult, op1=ALU.add),
                         reads=[r_pb[bk], r_Bs_tl, r_Bs_tr, r_Bs_bl, r_Bs_br[grp // 2]], writes=[r_AB[grp]])
                P.stage(10)
                order = list(range(NB)) if d == 0 else list(range(NB - 1, -1, -1))
                P.op(V, lambda e, c0=order[0]: e.tensor_copy(out=Z[0:64, c0, :, :], in_=STc[:]), reads=[r_ST], writes=[r_Zs[order[0]]])
                ybank = 7
                for n, c in enumerate(order):
                    sbk = 2 + n % 2

                    def fs(e, c=c, sbk=sbk):
                        ins = None
                        for h in range(2):
                            it = c * 2 + h
                            ins = mm(e, pb[sbk][0:64, h * 64:(h + 1) * 64], ABQH[:, it, 0:64], Z[:, c, h, :])
                        return ins
                    P.op(PE, fs, reads=[r_AB[c // 2], r_Zv, r_Zs[c]], writes=[r_pb[sbk]])
                    src = pb[sbk][0:64, 0:128].rearrange("p (h v) -> p h v", h=2)
                    if n < NB - 1:
                        cn = order[n + 1]
                        P.op(A, lambda e, cn=cn, src=src: e.activation(out=Z[0:64, cn, :, :], in_=src, func=AF.Copy),
                             reads=[r_pb[sbk]], writes=[r_Zs[cn]])
                    else:
                        P.op(A, lambda e, src=src: e.activation(out=STc[:], in_=src, func=AF.Copy),
                             reads=[r_pb[sbk]], writes=[r_ST])

                    def fy(e, c=c):
                        ins = None
                        for h in range(2):
                            it = c * 2 + h
                            ins = mm(e, pb[ybank][64 * h:64 * h + 64, c * C:(c + 1) * C], Z[:, c, h, :], ABQH[:, it, 64:128])
                        return ins
                    P.op(PE, fy, reads=[r_AB[c // 2], r_Zv, r_Zs[c]], writes=[r_pb[ybank]])
                P.stage(11)
                if d == 0:
                    P.op(A, lambda e, t0=t0: e.activation(out=yf[:, t0:t0 + BT], in_=pb[ybank][:, :], func=AF.Copy),
                         reads=[r_pb[ybank]], writes=[r_yf[blk]])
                    continue
                P.op(V, lambda e, t0=t0: e.tensor_tensor(out=ysum[:], in0=pb[ybank][:, :], in1=yf[:, t0:t0 + BT], op=ALU.add),
                     reads=[r_pb[ybank], r_yf[blk]], writes=[r_ysum])
                P.op(A, lambda e: e.activation(out=ysq[:], in_=ysum[:], func=AF.Square), reads=[r_ysum], writes=[r_ysq])
                P.op(PE, lambda e: mm(e, pb[0][:, :], bdavg, ysum[:]), reads=[r_cm, r_ysum], writes=[r_pb[0]])
                P.op(PE, lambda e: mm(e, pb[1][:, :], bdavg, ysq[:]), reads=[r_cm, r_ysq], writes=[r_pb[1]])
                P.op(A, lambda e: e.activation(out=m2[:], in_=pb[0][:, :], func=AF.Square), reads=[r_pb[0]], writes=[r_m2])
                P.op(V, lambda e: e.tensor_tensor(out=m2[:], in0=pb[1][:, :], in1=m2[:], op=ALU.subtract), reads=[r_pb[1], r_m2], writes=[r_m2])
                P.op(A, lambda e: e.activation(out=m2[:], in_=m2[:], func=AF.Sqrt, bias=epsv[:, 1:2], scale=1.0),
                     reads=[r_m2, r_hm], writes=[r_m2])
                P.op(V, lambda e: e.reciprocal(out=m2[:], in_=m2[:]), reads=[r_m2], writes=[r_m2])
                P.op(V, lambda e: e.scalar_tensor_tensor(out=yn[:], in0=pb[0][:, :], scalar=-1.0, in1=ysum[:], op0=ALU.mult, op1=ALU.add), reads=[r_ysum, r_pb[0]], writes=[r_yn])
                P.op(V, lambda e: e.tensor_tensor(out=yn[:], in0=yn[:], in1=m2[:], op=ALU.mult), reads=[r_yn, r_m2], writes=[r_yn])
                P.op(V, lambda e: e.tensor_scalar(out=yn[:], in0=yn[:], scalar1=col(GNG), scalar2=col(GNB), op0=ALU.mult, op1=ALU.add),
                     reads=[r_yn, r_pv], writes=[r_yn])
                P.op(PE, lambda e: mm(e, pb[2][:, :], pm[0:64, 1, :], has[0:64, :]), reads=[r_pm, r_sh[4]], writes=[r_pb[2]])
                P.op(A, lambda e: e.activation(out=af[:], in_=pb[2][:, :], func=AF.Sigmoid, bias=col(A0F), scale=1.0),
                     reads=[r_pb[2], r_pv], writes=[r_af])
                P.op(G, lambda e: e.tensor_tensor(out=af[:], in0=af[:], in1=aa[:], op=ALU.add), reads=[r_af, r_aa], writes=[r_af])
                P.op(V, lambda e: e.tensor_scalar(out=t1[:], in0=af[:], scalar1=hm[:, 7:8], scalar2=hm[:, 6:7], op0=ALU.mult, op1=ALU.add),
                     reads=[r_af, r_pv, r_hm], writes=[r_t1])
                P.op(G, lambda e: e.tensor_tensor(out=rkb[:], in0=ks[:], in1=t1[:], op=ALU.mult), reads=[r_sh[1], r_t1], writes=[r_rkb])
                P.op(V, lambda e: e.scalar_tensor_tensor(out=rkb[:], in0=rkb[:], scalar=col(RK_), in1=rs[:], op0=ALU.mult, op1=ALU.mult),
                     reads=[r_rkb, r_sh[0], r_pv], writes=[r_rkb])
                P.op(PE, lambda e: mm(e, pb[3][:, :], bdones, rkb[:]), reads=[r_cm, r_rkb], writes=[r_pb[3]])
                P.op(V, lambda e: e.tensor_tensor(out=rkb[:], in0=pb[3][:, :], in1=vs[:], op=ALU.mult), reads=[r_pb[3], r_sh[2]], writes=[r_rkb])
                P.op(V, lambda e: e.tensor_tensor(out=yn[:], in0=yn[:], in1=rkb[:], op=ALU.add), reads=[r_yn, r_rkb], writes=[r_yn])
                P.op(A, lambda e: e.activation(out=sg[:], in_=hgs[:], func=AF.Sigmoid), reads=[r_sh[5]], writes=[r_sg])
                P.op(PE, lambda e: mm(e, pb[4][:, :], pm[:, 2, :], sg[:]), reads=[r_pm, r_sg], writes=[r_pb[4]])
                P.op(V, lambda e: e.tensor_tensor(out=yo[:], in0=pb[4][:, :], in1=yn[:], op=ALU.mult), reads=[r_yn, r_pb[4]], writes=[r_yo])
                P.dma("sync", yout[0, :, g0:g0 + BT], yo[:], reads=[r_yo], is_out=True)
                P.op(G, lambda e: e.tensor_tensor(out=cu[:], in0=raw[7][:], in1=raw[8][:], op=ALU.mult), reads=[r_raw[7], r_raw[8]], writes=[r_cu])
                P.op(V, lambda e: e.tensor_scalar(out=hc[:], in0=cu[:, 0:BT], scalar1=col(CW0), scalar2=None, op0=ALU.mult),
                     reads=[r_cu, r_pv], writes=[r_hc])
                P.op(V, lambda e: e.scalar_tensor_tensor(out=hc[:], in0=cu[:, 1:BT + 1], scalar=col(CW1), in1=hc[:], op0=ALU.mult, op1=ALU.add),
                     reads=[r_cu, r_hc, r_pv], writes=[r_hc])
                P.op(V, lambda e: e.scalar_tensor_tensor(out=hc[:], in0=cu[:, 2:BT + 2], scalar=col(CW2), in1=hc[:], op0=ALU.mult, op1=ALU.add),
                     reads=[r_cu, r_hc, r_pv], writes=[r_hc])
                P.op(G, lambda e: e.tensor_tensor(out=yc[:], in0=hc[:], in1=raw[6][:, 1:BT + 1], op=ALU.mult), reads=[r_hc, r_raw[6]], writes=[r_yc])
                P.dma("sync", yout[1, :, g0:g0 + BT], yc[:], reads=[r_yc], is_out=True)


D = 2048
KC = 16
F = 5504
FC = 43
NT = 1024
TT = 512
NTT = NT // TT
D_IN = 13696
RC = 6528
QC0 = 6528
GC0 = 7552
ALPHA = (2 * 2) ** 0.25
LN_EPS = 1e-5
V, G, A, PE = "vector", "gpsimd", "scalar", "tensor"
FGROUPS = [(0, 11), (11, 22), (22, 33), (33, 43)]


class DenseCtx:
    def __init__(self, P):
        self.P = P
        sb, ps = P.sb, P.ps
        self.pb = [ps(f"pb{i}", [128, 512]) for i in range(8)]
        self.r_pb = [Res() for _ in range(8)]
        self.bank = 0
        self.onesf = sb("onesf", [128, 128]); self.r_c = Res()
        self.onesb = sb("onesb", [128, 128], BF16)
        self.epsv = sb("epsv", [128, 1])
        P.op(V, lambda e: e.memset(self.onesf[:], 1.0 / D), writes=[self.r_c])
        P.op(V, lambda e: e.memset(self.onesb[:], 1.0), writes=[self.r_c])
        P.op(V, lambda e: e.memset(self.epsv[:], LN_EPS), writes=[self.r_c])
        self.wA = [sb(f"wA{i}", [128, 16, 256], BF16) for i in range(3)]
        self.r_wA = [Res() for _ in range(3)]
        self.wA_i = 0
        self.wD = [sb(f"wD{i}", [128, 11, 256], BF16) for i in range(2)]
        self.r_wD = [Res() for _ in range(2)]
        self.wD_i = 0
        self.tmp = [sb(f"tmp{i}", [128, 512]) for i in range(4)]
        self.r_tmp = [Res() for _ in range(4)]
        self.tmp_i = 0
        self.stat = [sb(f"stat{i}", [128, 512]) for i in range(3)]
        self.r_stat = [Res() for _ in range(3)]

    def nbank(self):
        b = self.bank
        self.bank = (b + 1) % 8
        return b

    def ntmp(self):
        i = self.tmp_i
        self.tmp_i = (i + 1) % 4
        return i

    def load_wA(self, w_ap, kc, mcols):
        i = self.wA_i
        self.wA_i = (i + 1) % 3
        t = self.wA[i]
        self.P.dma("gpsimd", t[:, 0:kc, 0:mcols], w_ap.rearrange("(k p) m -> p k m", p=128), writes=[self.r_wA[i]])
        return t, self.r_wA[i]

    def load_wD(self, w_ap, kc, mcols):
        i = self.wD_i
        self.wD_i = (i + 1) % 2
        t = self.wD[i]
        self.P.dma("gpsimd", t[:, 0:kc, 0:mcols], w_ap.rearrange("(k p) m -> p k m", p=128), writes=[self.r_wD[i]])
        return t, self.r_wD[i]


def mm_group(P, ctx, bank, pairs, reads, n=512, mrows=128):
    def f(e):
        ins = None
        L = len(pairs)
        for i, (l, r) in enumerate(pairs):
            ins = e.matmul(ctx.pb[bank][0:mrows, 0:n], l, r, start=(i == 0), stop=(i == L - 1))
        return ins
    P.op(PE, f, reads=reads, writes=[ctx.r_pb[bank]])


def layer_norm_fm(P, ctx, X, r_X, XB, r_XB, gb, r_gb, gcol, bcol, kc_n=KC, ntok=NT, write_x=True):
    tts = [(t0, min(TT, ntok - t0)) for t0 in range(0, ntok, TT)]
    for (t0, n) in tts:
        b_sum = ctx.nbank()
        mm_group(P, ctx, b_sum, [(ctx.onesf[:], X[:, kc, t0:t0 + n]) for kc in range(kc_n)], [ctx.r_c] + [r_X[kc] for kc in range(kc_n)], n=n)
        b_sq = ctx.nbank()
        sqs = []
        for kc in range(kc_n):
            ti = ctx.ntmp()
            P.op(A if kc % 2 else V, (lambda e, ti=ti, kc=kc: e.activation(out=ctx.tmp[ti][:, 0:n], in_=X[:, kc, t0:t0 + n], func=AF.Square)) if kc % 2 else
                 (lambda e, ti=ti, kc=kc: e.tensor_tensor(out=ctx.tmp[ti][:, 0:n], in0=X[:, kc, t0:t0 + n], in1=X[:, kc, t0:t0 + n], op=ALU.mult)),
                 reads=[r_X[kc]], writes=[ctx.r_tmp[ti]])
            P.op(PE, lambda e, ti=ti, kc=kc: e.matmul(ctx.pb[b_sq][:, 0:n], ctx.onesf[:], ctx.tmp[ti][:, 0:n], start=(kc == 0), stop=(kc == kc_n - 1)),
                 reads=[ctx.r_c, ctx.r_tmp[ti]], writes=[ctx.r_pb[b_sq]])
        mean_ps = ctx.pb[b_sum][:, 0:n]
        e2_ps = ctx.pb[b_sq][:, 0:n]
        m2, rstd, nmr = ctx.stat[0][:, 0:n], ctx.stat[1][:, 0:n], ctx.stat[2][:, 0:n]
        P.op(A, lambda e: e.activation(out=m2, in_=mean_ps, func=AF.Square), reads=[ctx.r_pb[b_sum]], writes=[ctx.r_stat[0]])
        P.op(V, lambda e: e.tensor_tensor(out=rstd, in0=e2_ps, in1=m2, op=ALU.subtract), reads=[ctx.r_pb[b_sq], ctx.r_stat[0]], writes=[ctx.r_stat[1]])
        P.op(A, lambda e: e.activation(out=rstd, in_=rstd, func=AF.Sqrt, bias=ctx.epsv[:, 0:1], scale=1.0), reads=[ctx.r_stat[1], ctx.r_c], writes=[ctx.r_stat[1]])
        P.op(V, lambda e: e.reciprocal(out=rstd, in_=rstd), reads=[ctx.r_stat[1]], writes=[ctx.r_stat[1]])
        P.op(V, lambda e: e.scalar_tensor_tensor(out=nmr, in0=mean_ps, scalar=-1.0, in1=rstd, op0=ALU.mult, op1=ALU.mult),
             reads=[ctx.r_pb[b_sum], ctx.r_stat[1]], writes=[ctx.r_stat[2]])
        for kc in range(kc_n):
            ti = ctx.ntmp()
            t = ctx.tmp[ti][:, 0:n]
            P.op(V, lambda e, t=t, kc=kc: e.tensor_tensor(out=t, in0=X[:, kc, t0:t0 + n], in1=rstd, op=ALU.mult),
                 reads=[r_X[kc], ctx.r_stat[1]], writes=[ctx.r_tmp[ti]])
            P.op(V, lambda e, t=t: e.tensor_tensor(out=t, in0=t, in1=nmr, op=ALU.add), reads=[ctx.r_tmp[ti], ctx.r_stat[2]], writes=[ctx.r_tmp[ti]])
            if write_x:
                P.op(V, lambda e, t=t, kc=kc: e.tensor_scalar(out=X[:, kc, t0:t0 + n], in0=t, scalar1=gb[:, gcol, kc:kc + 1], scalar2=gb[:, bcol, kc:kc + 1],
                                                              op0=ALU.mult, op1=ALU.add), reads=[ctx.r_tmp[ti], r_gb], writes=[r_X[kc]])
                P.op(A, lambda e, kc=kc: e.activation(out=XB[:, kc, t0:t0 + n], in_=X[:, kc, t0:t0 + n], func=AF.Copy), reads=[r_X[kc]], writes=[r_XB[kc]])
            else:
                P.op(V, lambda e, t=t, kc=kc: e.tensor_scalar(out=XB[:, kc, t0:t0 + n], in0=t, scalar1=gb[:, gcol, kc:kc + 1], scalar2=gb[:, bcol, kc:kc + 1],
                                                              op0=ALU.mult, op1=ALU.add), reads=[ctx.r_tmp[ti], r_gb], writes=[r_XB[kc]])


def ffn(P, ctx, X, r_X, XB, r_XB, H, r_H, wg, wu, wd):
    for kc in range(KC):
        P.op(A, lambda e, kc=kc: e.activation(out=X[:, kc, :], in_=X[:, kc, :], func=AF.Copy, scale=ALPHA),
             reads=[r_X[kc]], writes=[r_X[kc]])
    for (f0, f1) in FGROUPS:
        nf = f1 - f0
        for fb in range(f0, f1, 2):
            nb = min(2, f1 - fb)
            wgt, r_wg = ctx.load_wA(wg[:, fb * 128:(fb + nb) * 128], KC, nb * 128)
            wut, r_wu = ctx.load_wA(wu[:, fb * 128:(fb + nb) * 128], KC, nb * 128)
            for j in range(nb):
                fi = fb + j - f0
                for tt in range(NTT):
                    ts_ = slice(tt * TT, (tt + 1) * TT)
                    bg = ctx.nbank()
                    mm_group(P, ctx, bg, [(wgt[:, kc, j * 128:(j + 1) * 128], XB[:, kc, ts_]) for kc in range(KC)], [r_wg] + list(r_XB))
                    bu = ctx.nbank()
                    mm_group(P, ctx, bu, [(wut[:, kc, j * 128:(j + 1) * 128], XB[:, kc, ts_]) for kc in range(KC)], [r_wu] + list(r_XB))
                    ti = ctx.ntmp()
                    P.op(A, lambda e, ti=ti, bg=bg: e.activation(out=ctx.tmp[ti][:], in_=ctx.pb[bg][:, :], func=AF.Silu),
                         reads=[ctx.r_pb[bg]], writes=[ctx.r_tmp[ti]])
                    P.op(V, lambda e, ti=ti, bu=bu, fi=fi, ts_=ts_: e.tensor_tensor(out=H[:, fi, ts_], in0=ctx.pb[bu][:, :], in1=ctx.tmp[ti][:], op=ALU.mult),
                         reads=[ctx.r_pb[bu], ctx.r_tmp[ti]], writes=[r_H[fi]])
        for db in range(0, KC, 2):
            wdt, r_wd = ctx.load_wD(wd[f0 * 128:f1 * 128, db * 128:(db + 2) * 128], nf, 256)
            for j in range(2):
                dc = db + j
                for tt in range(NTT):
                    ts_ = slice(tt * TT, (tt + 1) * TT)
                    b = ctx.nbank()
                    mm_group(P, ctx, b, [(wdt[:, fi, j * 128:(j + 1) * 128], H[:, fi, ts_]) for fi in range(nf)], [r_wd] + [r_H[fi] for fi in range(nf)])
                    P.op(V, lambda e, b=b, dc=dc, ts_=ts_: e.scalar_tensor_tensor(out=X[:, dc, ts_], in0=ctx.pb[b][:, :], scalar=0.5, in1=X[:, dc, ts_],
                                                                                 op0=ALU.mult, op1=ALU.add),
                         reads=[ctx.r_pb[b], r_X[dc]], writes=[r_X[dc]])


def load_x(P, X, r_X, XB, r_XB, xT):
    for kc in range(KC):
        P.dma("gpsimd", XB[:, kc, :], xT[kc * 128:(kc + 1) * 128, :], writes=[r_XB[kc]])
        P.dma("sync", X[:, kc, :], xT[kc * 128:(kc + 1) * 128, :], writes=[r_X[kc]])


def build_A():
    nc = bass.Bass("TRN2", target_bir_lowering=False)
    xT = nc.dram_tensor("xT", [D, NT], F32, kind="ExternalInput").ap()
    wg = nc.dram_tensor("wg", [D, F], F32, kind="ExternalInput").ap()
    wu = nc.dram_tensor("wu", [D, F], F32, kind="ExternalInput").ap()
    wd = nc.dram_tensor("wd", [F, D], F32, kind="ExternalInput").ap()
    lngb = nc.dram_tensor("lngb", [128, 2, KC], F32, kind="ExternalInput").ap()
    w_in = nc.dram_tensor("w_in", [D, RC], F32, kind="ExternalInput").ap()
    x1T = nc.dram_tensor("x1T", [D, NT], F32, kind="ExternalOutput").ap()
    pT = nc.dram_tensor("pT", [RC, NT], F32, kind="ExternalOutput").ap()
    with ExitStack() as es:
        P = Prog(nc, es)
        ctx = DenseCtx(P)
        X = P.sb("X", [128, KC, NT]); r_X = [Res() for _ in range(KC)]
        XB = P.sb("XB", [128, KC, NT], BF16); r_XB = [Res() for _ in range(KC)]
        H = P.sb("H", [128, 11, NT], BF16); r_H = [Res() for _ in range(11)]
        gb = P.sb("gb", [128, 2, KC]); r_gb = Res()
        P.dma("sync", gb[:], lngb[:, :, :], writes=[r_gb])
        load_x(P, X, r_X, XB, r_XB, xT)
        ffn(P, ctx, X, r_X, XB, r_XB, H, r_H, wg, wu, wd)
        layer_norm_fm(P, ctx, X, r_X, XB, r_XB, gb, r_gb, 0, 1)
        for kc in range(KC):
            P.dma("sync", x1T[kc * 128:(kc + 1) * 128, :], X[:, kc, :], reads=[r_X[kc]], is_out=True)
        for mb in range(0, RC // 128, 2):
            nb = min(2, RC // 128 - mb)
            wt, r_w = ctx.load_wA(w_in[:, mb * 128:(mb + nb) * 128], KC, nb * 128)
            for j in range(nb):
                m = mb + j
                for tt in range(NTT):
                    ts_ = slice(tt * TT, (tt + 1) * TT)
                    b = ctx.nbank()
                    mm_group(P, ctx, b, [(wt[:, kc, j * 128:(j + 1) * 128], XB[:, kc, ts_]) for kc in range(KC)], [r_w] + list(r_XB))
                    ti = ctx.ntmp()
                    if (m + tt) % 2:
                        P.op(A, lambda e, ti=ti, b=b: e.activation(out=ctx.tmp[ti][:], in_=ctx.pb[b][:, :], func=AF.Copy), reads=[ctx.r_pb[b]], writes=[ctx.r_tmp[ti]])
                    else:
                        P.op(V, lambda e, ti=ti, b=b: e.tensor_copy(out=ctx.tmp[ti][:], in_=ctx.pb[b][:, :]), reads=[ctx.r_pb[b]], writes=[ctx.r_tmp[ti]])
                    P.dma("sync", pT[m * 128:(m + 1) * 128, ts_], ctx.tmp[ti][:], reads=[ctx.r_tmp[ti]], is_out=True)
        P.finish()
        P.emit()
    return nc


def build_C():
    nc = bass.Bass("TRN2", target_bir_lowering=False)
    x1T = nc.dram_tensor("x1T", [D, NT], F32, kind="ExternalInput").ap()
    yT = nc.dram_tensor("yT", [2, 1024, NT], F32, kind="ExternalInput").ap()
    memT = nc.dram_tensor("memT", [D, 256], F32, kind="ExternalInput").ap()
    w_q = nc.dram_tensor("w_q", [D, 1024], F32, kind="ExternalInput").ap()
    w_g = nc.dram_tensor("w_g", [D, 3 * D], F32, kind="ExternalInput").ap()
    w_kv = nc.dram_tensor("w_kv", [D, 2048], F32, kind="ExternalInput").ap()
    w_br = nc.dram_tensor("w_br", [3, 1024, D], F32, kind="ExternalInput").ap()
    w_o = nc.dram_tensor("w_o", [D, D], F32, kind="ExternalInput").ap()
    vecs = nc.dram_tensor("vecs", [128, 9, KC], F32, kind="ExternalInput").ap()
    wg = nc.dram_tensor("wg", [D, F], F32, kind="ExternalInput").ap()
    wu = nc.dram_tensor("wu", [D, F], F32, kind="ExternalInput").ap()
    wd = nc.dram_tensor("wd", [F, D], F32, kind="ExternalInput").ap()
    x2T = nc.dram_tensor("x2T", [D, NT], F32, kind="ExternalOutput").ap()
    with ExitStack() as es:
        P = Prog(nc, es)
        ctx = DenseCtx(P)
        ARENA = P.sb("ARENA", [128, KC * NT])
        ARENA2 = P.sb("ARENA2", [128, 8192])
        X = ARENA[:, :].rearrange("p (c t) -> p c t", t=NT); r_X = [Res() for _ in range(KC)]
        XB = P.sb("XB", [128, KC, NT], BF16); r_XB = [Res() for _ in range(KC)]
        Yall = ARENA[:, 0:12288].bitcast(BF16).rearrange("p (c t) -> p c t", t=NT); r_Y = [Res() for _ in range(24)]
        QB = ARENA[:, 12288:16384].bitcast(BF16).rearrange("p (c t) -> p c t", t=NT); r_QB = [Res() for _ in range(8)]
        A2B = ARENA2[:, :].bitcast(BF16)
        MIXB = A2B.rearrange("p (c t) -> p c t", t=NT); r_MIX = [Res() for _ in range(KC)]
        H = A2B[:, 0:11 * NT].rearrange("p (c t) -> p c t", t=NT); r_H = [Res() for _ in range(11)]
        MX = ARENA2[:, 0:4096].rearrange("p (c t) -> p c t", t=256); r_MX = [Res() for _ in range(KC)]
        MB = ARENA2[:, 4096:6144].bitcast(BF16).rearrange("p (c t) -> p c t", t=256); r_MB = [Res() for _ in range(KC)]
        KTB = ARENA2[:, 6144:7168].bitcast(BF16).rearrange("p (c t) -> p c t", t=256); r_KTB = Res()
        VB = ARENA2[:, 7168:8192].bitcast(BF16).rearrange("p (c t) -> p c t", t=1024); r_VB = Res()
        vc = P.sb("vc", [128, 9, KC]); r_vc = Res()
        P.dma("sync", vc[:], vecs[:, :, :], writes=[r_vc])
        for kc in range(KC):
            P.dma("gpsimd", XB[:, kc, :], x1T[kc * 128:(kc + 1) * 128, :], writes=[r_XB[kc]])
        for n in range(2):
            for c in range(8):
                P.dma("gpsimd", Yall[:, n * 8 + c, :], yT[n, c * 128:(c + 1) * 128, :], writes=[r_Y[n * 8 + c]])
        for kc in range(KC):
            P.dma("sync", MX[:, kc, :], memT[kc * 128:(kc + 1) * 128, :], writes=[r_MX[kc]])
        layer_norm_fm(P, ctx, MX, r_MX, MB, r_MB, vc, r_vc, 0, 1, ntok=256, write_x=False)
        for mb in range(0, 8, 2):
            wt, r_w = ctx.load_wA(w_kv[:, mb * 128:(mb + 2) * 128], KC, 256)
            for j in range(2):
                b = ctx.nbank()
                mm_group(P, ctx, b, [(wt[:, kc, j * 128:(j + 1) * 128], MB[:, kc, :]) for kc in range(KC)], [r_w] + list(r_MB), n=256)
                P.op(V, lambda e, b=b, m=mb + j: e.tensor_copy(out=KTB[:, m, :], in_=ctx.pb[b][:, 0:256]), reads=[ctx.r_pb[b]], writes=[r_KTB])
        for vb in range(4):
            wt, r_w = ctx.load_wA(w_kv[:, 1024 + vb * 256:1024 + (vb + 1) * 256], KC, 256)
            for mc in range(2):
                b = ctx.nbank()
                mm_group(P, ctx, b, [(MB[:, kc, mc * 128:(mc + 1) * 128], wt[:, kc, :]) for kc in range(KC)], [r_w] + list(r_MB), n=256)
                P.op(V, lambda e, b=b, mc=mc, vb=vb: e.tensor_copy(out=VB[:, mc, vb * 256:(vb + 1) * 256], in_=ctx.pb[b][:, 0:256]), reads=[ctx.r_pb[b]], writes=[r_VB])
        for mb in range(0, 8, 2):
            wt, r_w = ctx.load_wA(w_q[:, mb * 128:(mb + 2) * 128], KC, 256)
            for j in range(2):
                for tt in range(NTT):
                    ts_ = slice(tt * TT, (tt + 1) * TT)
                    b = ctx.nbank()
                    mm_group(P, ctx, b, [(wt[:, kc, j * 128:(j + 1) * 128], XB[:, kc, ts_]) for kc in range(KC)], [r_w] + list(r_XB))
                    P.op(A, lambda e, b=b, m=mb + j, ts_=ts_: e.activation(out=QB[:, m, ts_], in_=ctx.pb[b][:, :], func=AF.Copy), reads=[ctx.r_pb[b]], writes=[r_QB[mb + j]])
        EB = P.sb("EB", [128, 2, TT], BF16); r_EB = Res()
        rden = P.sb("rden", [128, TT]); r_rden = Res()
        for h in range(4):
            for tt in range(NTT):
                ts_ = slice(tt * TT, (tt + 1) * TT)
                for mc in range(2):
                    b = ctx.nbank()
                    mm_group(P, ctx, b, [(KTB[:, h * 2 + dc, mc * 128:(mc + 1) * 128], QB[:, h * 2 + dc, ts_]) for dc in range(2)], [r_KTB, r_QB[h * 2], r_QB[h * 2 + 1]])
                    P.op(A, lambda e, b=b, mc=mc: e.activation(out=EB[:, mc, :], in_=ctx.pb[b][:, :], func=AF.Exp, scale=1.0 / 16.0), reads=[ctx.r_pb[b]], writes=[r_EB])
                b = ctx.nbank()
                mm_group(P, ctx, b, [(ctx.onesb[:], EB[:, mc, :]) for mc in range(2)], [ctx.r_c, r_EB])
                P.op(V, lambda e, b=b: e.reciprocal(out=rden[:], in_=ctx.pb[b][:, :]), reads=[ctx.r_pb[b]], writes=[r_rden])
                for dc in range(2):
                    b = ctx.nbank()
                    mm_group(P, ctx, b, [(VB[:, mc, h * 256 + dc * 128:h * 256 + (dc + 1) * 128], EB[:, mc, :]) for mc in range(2)], [r_VB, r_EB])
                    P.op(V, lambda e, b=b, c=16 + h * 2 + dc, ts_=ts_: e.tensor_tensor(out=Yall[:, c, ts_], in0=ctx.pb[b][:, :], in1=rden[:], op=ALU.mult),
                         reads=[ctx.r_pb[b], r_rden], writes=[r_Y[16 + h * 2 + dc]])
        gate = P.sb("gate", [128, TT]); r_gate = Res()
        term = P.sb("term", [128, TT]); r_term = Res()
        ACC = P.sb("ACC", [128, 2, NT]); r_ACC = Res()
        for db in range(0, KC, 2):
            for n in range(3):
                wt, r_w = ctx.load_wA(w_g[:, n * D + db * 128:n * D + (db + 2) * 128], KC, 256)
                wbt, r_wb = ctx.load_wD(w_br[n, :, db * 128:(db + 2) * 128], 8, 256)
                for j in range(2):
                    dc = db + j
                    for tt in range(NTT):
                        ts_ = slice(tt * TT, (tt + 1) * TT)
                        bg = ctx.nbank()
                        mm_group(P, ctx, bg, [(wt[:, kc, j * 128:(j + 1) * 128], XB[:, kc, ts_]) for kc in range(KC)], [r_w] + list(r_XB))
                        P.op(A, lambda e, bg=bg, n=n, dc=dc: e.activation(out=gate[:], in_=ctx.pb[bg][:, :], func=AF.Sigmoid, bias=vc[:, 2 + n, dc:dc + 1], scale=1.0),
                             reads=[ctx.r_pb[bg], r_vc], writes=[r_gate])
                        bp_ = ctx.nbank()
                        mm_group(P, ctx, bp_, [(wbt[:, c, j * 128:(j + 1) * 128], Yall[:, n * 8 + c, ts_]) for c in range(8)], [r_wb] + [r_Y[n * 8 + c] for c in range(8)])
                        if n == 0:
                            P.op(V, lambda e, bp_=bp_, j=j, ts_=ts_: e.tensor_tensor(out=ACC[:, j, ts_], in0=ctx.pb[bp_][:, :], in1=gate[:], op=ALU.mult),
                                 reads=[ctx.r_pb[bp_], r_gate], writes=[r_ACC])
                        else:
                            P.op(V, lambda e, bp_=bp_: e.tensor_tensor(out=term[:], in0=ctx.pb[bp_][:, :], in1=gate[:], op=ALU.mult),
                                 reads=[ctx.r_pb[bp_], r_gate], writes=[r_term])
                            if n == 1:
                                P.op(V, lambda e, j=j, ts_=ts_: e.tensor_tensor(out=ACC[:, j, ts_], in0=ACC[:, j, ts_], in1=term[:], op=ALU.add), reads=[r_ACC, r_term], writes=[r_ACC])
                            else:
                                P.op(V, lambda e, dc=dc, j=j, ts_=ts_: e.tensor_tensor(out=MIXB[:, dc, ts_], in0=ACC[:, j, ts_], in1=term[:], op=ALU.add),
                                     reads=[r_ACC, r_term], writes=[r_MIX[dc]])
        xs = [P.sb(f"xs{i}", [128, TT]) for i in range(2)]; r_xs = [Res(), Res()]
        cnt = 0
        for db in range(0, KC, 2):
            wt, r_w = ctx.load_wA(w_o[:, db * 128:(db + 2) * 128], KC, 256)
            for j in range(2):
                dc = db + j
                for tt in range(NTT):
                    ts_ = slice(tt * TT, (tt + 1) * TT)
                    i = cnt % 2; cnt += 1
                    P.dma("sync", xs[i][:], x1T[dc * 128:(dc + 1) * 128, ts_], writes=[r_xs[i]])
                    P.op(A, lambda e, i=i: e.activation(out=xs[i][:], in_=xs[i][:], func=AF.Copy, scale=ALPHA), reads=[r_xs[i]], writes=[r_xs[i]])
                    b = ctx.nbank()
                    mm_group(P, ctx, b, [(wt[:, kc, j * 128:(j + 1) * 128], MIXB[:, kc, ts_]) for kc in range(KC)], [r_w] + list(r_MIX))
                    P.op(V, lambda e, b=b, i=i, dc=dc, ts_=ts_: e.tensor_tensor(out=X[:, dc, ts_], in0=ctx.pb[b][:, :], in1=xs[i][:], op=ALU.add),
                         reads=[ctx.r_pb[b], r_xs[i]], writes=[r_X[dc]])
        layer_norm_fm(P, ctx, X, r_X, XB, r_XB, vc, r_vc, 5, 6)
        ffn(P, ctx, X, r_X, XB, r_XB, H, r_H, wg, wu, wd)
        layer_norm_fm(P, ctx, X, r_X, XB, r_XB, vc, r_vc, 7, 8)
        for kc in range(KC):
            P.dma("sync", x2T[kc * 128:(kc + 1) * 128, :], X[:, kc, :], reads=[r_X[kc]], is_out=True)
        P.finish()
        P.emit()
    return nc


_PROGS = {}


def _prog(name):
    if name not in _PROGS:
        _PROGS[name] = {"A": build_A, "C": build_C, "S": lambda: build_scan(4096, 2)}[name]()
    return _PROGS[name]


def _vec16(v):
    return np.ascontiguousarray(np.asarray(v, np.float32).reshape(16, 128).T)


def kernel(x, mem, ffn1_w_gate, ffn1_w_up, ffn1_w_down, ln1_g, ln1_b, w_in,
           rwkv_mu, rwkv_w0, rwkv_w_up, rwkv_a0, rwkv_a_up, rwkv_g_up, rwkv_k_k,
           rwkv_k_a, rwkv_r_k, rwkv_gn_g, rwkv_gn_b, conv_w, mem_ln_g, mem_ln_b,
           w_mem_kv, w_branch, gate_b, w_out, ln2_g, ln2_b, ffn2_w_gate, ffn2_w_up,
           ffn2_w_down, ln3_g, ln3_b):
    f32 = lambda a: np.ascontiguousarray(np.asarray(a, dtype=np.float32))
    x = f32(x); mem = f32(mem)
    NCORE = 8
    cores = list(range(NCORE))
    xT = [np.ascontiguousarray(x[c // 4, (c % 4) * 1024:(c % 4 + 1) * 1024].T) for c in cores]
    memT = [np.ascontiguousarray(mem[c // 4].T) for c in cores]
    cm, rmask = scan_consts()
    for l in range(2):
        w_in_l = f32(w_in[l])
        mA = {"wg": f32(ffn1_w_gate[l]), "wu": f32(ffn1_w_up[l]), "wd": f32(ffn1_w_down[l]),
              "lngb": np.ascontiguousarray(np.stack([_vec16(ln1_g[l]), _vec16(ln1_b[l])], 1)),
              "w_in": np.ascontiguousarray(w_in_l[:, :RC])}
        resA = run_bass_kernel_spmd(_prog("A"), [dict(mA, xT=xT[c]) for c in cores], core_ids=cores).results
        x1T = [np.asarray(resA[c]["x1T"]) for c in cores]
        pall = np.concatenate([np.asarray(resA[c]["pT"]) for c in cores], axis=1)
        del resA
        mu = f32(rwkv_mu[l]); w0 = f32(rwkv_w0[l]); a0 = f32(rwkv_a0[l])
        wup = f32(rwkv_w_up[l]); aup = f32(rwkv_a_up[l]); gup = f32(rwkv_g_up[l])
        kkv = f32(rwkv_k_k[l]); kav = f32(rwkv_k_a[l]); rkv = f32(rwkv_r_k[l]).reshape(-1)
        gng = f32(rwkv_gn_g[l]); gnb = f32(rwkv_gn_b[l]); cw = f32(conv_w[l])
        mS = []
        for c in cores:
            cs = slice(c * 128, (c + 1) * 128)
            zin = np.stack([pall[0:1024][cs], pall[1024:2048][cs], pall[2048:3072][cs], pall[3072:3200], pall[3200:3328], pall[3328:3456],
                            pall[3456:4480][cs], pall[4480:5504][cs], pall[5504:6528][cs]], 0)
            pv = np.stack([mu[0:1024][cs], mu[1024:2048][cs], mu[2048:3072][cs], mu[3072:3200], mu[3200:3328], mu[3328:3456],
                           w0[0][cs], w0[1][cs], a0[0][cs], a0[1][cs], kkv[cs], kav[cs], rkv[cs], gng[cs], gnb[cs],
                           cw[0][cs], cw[1][cs], cw[2][cs]], 1)
            pm = np.stack([wup[:, :, cs].reshape(128, 128), aup[:, :, cs].reshape(128, 128), gup[:, cs]], 1)
            mS.append({"zin": np.ascontiguousarray(zin), "pvec": np.ascontiguousarray(pv), "pmat": np.ascontiguousarray(pm), "cmat": cm, "rmask": rmask})
        del pall
        resS = run_bass_kernel_spmd(_prog("S"), mS, core_ids=cores).results
        yall = np.concatenate([np.asarray(resS[c]["yout"]) for c in cores], axis=1)
        del resS, mS
        vecs = np.ascontiguousarray(np.stack([_vec16(mem_ln_g[l]), _vec16(mem_ln_b[l]), _vec16(gate_b[l][0]), _vec16(gate_b[l][1]), _vec16(gate_b[l][2]),
                                              _vec16(ln2_g[l]), _vec16(ln2_b[l]), _vec16(ln3_g[l]), _vec16(ln3_b[l])], 1))
        mC = {"w_q": np.ascontiguousarray(w_in_l[:, 6528:7552]), "w_g": np.ascontiguousarray(w_in_l[:, 7552:]), "w_kv": f32(w_mem_kv[l]),
              "w_br": f32(w_branch[l]), "w_o": f32(w_out[l]), "vecs": vecs,
              "wg": f32(ffn2_w_gate[l]), "wu": f32(ffn2_w_up[l]), "wd": f32(ffn2_w_down[l])}
        resC = run_bass_kernel_spmd(_prog("C"), [dict(mC, x1T=x1T[c], yT=np.ascontiguousarray(yall[:, :, c * 1024:(c + 1) * 1024]), memT=memT[c]) for c in cores],
                                    core_ids=cores).results
        xT = [np.asarray(resC[c]["x2T"]) for c in cores]
        del resC, yall
    out = np.empty((2, 4096, 2048), np.float32)
    for c in cores:
        out[c // 4, (c % 4) * 1024:(c % 4 + 1) * 1024] = xT[c].T
    return out
```
